# Optimizing a Trainium2 kernel written in Bass

```python
import math
import jax, jax.numpy as jnp
from jax import lax
import numpy as np

D_MODEL = 1024
BATCH = 2
SEQ = 8192
DEPTH = 1

GRID_W = 64
CTX_LEN = 256
D_SSM = D_MODEL // 2
SSM_GROUP_CH = 16
SSM_GROUPS = D_SSM // SSM_GROUP_CH
SSM_STATE = 64
D_POOL = D_MODEL // 2
POOL_WINDOWS = (2, 4, 8, 16)
POOL_GROUP_CH = D_POOL // len(POOL_WINDOWS)
FFN_HIDDEN = ((8 * D_MODEL // 3 + 255) // 256) * 256
RMS_EPS = 1e-6
DT_MIN = 1e-3
DT_MAX = 1e-1

kernel_name = "hybrid_s5_pool_prefix_dit_block"


def rms_norm(x, g):
    x32 = x.astype(jnp.float32)
    y = x32 * lax.rsqrt(jnp.mean(x32 * x32, axis=-1, keepdims=True) + RMS_EPS)
    return (y * g.astype(jnp.float32)).astype(x.dtype)


def modulate(h, shift, scale):
    return h * (1.0 + scale) + shift


def s5_discretize(a_re, a_im, log_dt, b_re, b_im):
    A = lax.complex(a_re.astype(jnp.float32), a_im.astype(jnp.float32))
    dt = jnp.exp(log_dt.astype(jnp.float32))[:, None]
    a_bar = jnp.exp(A * dt)
    B = lax.complex(b_re.astype(jnp.float32), b_im.astype(jnp.float32))
    b_bar = ((a_bar - 1.0) / A)[..., None] * B
    return a_bar, b_bar


def _lin_rec(e1, e2):
    a1, b1 = e1
    a2, b2 = e2
    return a1 * a2, a2 * b1 + b2


def s5_states(u, a_bar, b_bar, s0, reverse):
    bsz, length, _ = u.shape
    ug = u.astype(jnp.float32).reshape(bsz, length, SSM_GROUPS, SSM_GROUP_CH).astype(jnp.complex64)
    bu = jnp.einsum('gpc,blgc->blgp', b_bar, ug)
    if s0 is not None:
        idx = length - 1 if reverse else 0
        bu = bu.at[:, idx].add(a_bar[None] * s0)
    a = jnp.broadcast_to(a_bar, (1, length) + a_bar.shape)
    _, states = lax.associative_scan(_lin_rec, (a, bu), reverse=reverse, axis=1)
    return states


def s5_readout(u_a, st_f, st_b, c_f, c_b, d_skip, w_glu, b_glu):
    bsz, length, _ = u_a.shape
    y = (jnp.einsum('gcp,blgp->blgc', c_f, st_f) + jnp.einsum('gcp,blgp->blgc', c_b, st_b)).real
    y = y.reshape(bsz, length, D_SSM) + d_skip.astype(jnp.float32) * u_a.astype(jnp.float32)
    y = jax.nn.gelu(y)
    return y * jax.nn.sigmoid(y @ w_glu.astype(jnp.float32) + b_glu.astype(jnp.float32))


def pool_mixer(u, pool_w, pool_scale):
    width = u.shape[2]
    pos = jnp.arange(width)
    u32 = u.astype(jnp.float32)
    cs = jnp.pad(jnp.cumsum(u32, axis=2), ((0, 0), (0, 0), (1, 0), (0, 0)))
    outs = []
    for j, w in enumerate(POOL_WINDOWS):
        sl = slice(j * POOL_GROUP_CH, (j + 1) * POOL_GROUP_CH)
        lo = jnp.clip(pos - w // 2, 0, width - 1)
        hi = jnp.clip(pos + w - 1 - w // 2, 0, width - 1) + 1
        csg = cs[..., sl]
        mean = (csg[:, :, hi] - csg[:, :, lo]) / (hi - lo).astype(jnp.float32)[:, None]
        outs.append((mean - u32[..., sl]) @ pool_w[j].astype(jnp.float32))
    return jnp.concatenate(outs, axis=-1) * pool_scale.astype(jnp.float32)


def hybrid_mixer(proj, st_f, st_b, rows, width, c_f, c_b, s5_d, w_glu, b_glu, pool_w, pool_scale,
                 w_branch_a, w_branch_b, w_out):
    bsz, length, _ = proj.shape
    u_a, u_b, gate_a, gate_b = jnp.split(proj, [D_SSM, D_SSM + D_POOL, D_SSM + D_POOL + D_MODEL], axis=-1)
    y_a = s5_readout(u_a, st_f, st_b, c_f, c_b, s5_d, w_glu, b_glu) @ w_branch_a.astype(jnp.float32)
    y_b = pool_mixer(u_b.reshape(bsz, rows, width, D_POOL), pool_w, pool_scale).reshape(bsz, length, D_POOL)
    y_b = y_b @ w_branch_b.astype(jnp.float32)
    merged = jax.nn.sigmoid(gate_a.astype(jnp.float32)) * y_a + jax.nn.sigmoid(gate_b.astype(jnp.float32)) * y_b
    return (merged @ w_out.astype(jnp.float32)).astype(proj.dtype)


def swiglu(h, w_in, w_out):
    gate, up = jnp.split(h @ w_in, 2, axis=-1)
    return (jax.nn.silu(gate) * up) @ w_out


def setup_inputs(seed: int = 0) -> dict:
    key = jax.random.key(seed)
    ks = jax.random.split(key, 28)
    f32 = jnp.float32

    def nrm(k, shape, scale):
        return jax.random.normal(k, shape, f32) * scale

    G, P, CH = SSM_GROUPS, SSM_STATE, SSM_GROUP_CH
    n_idx = jnp.arange(P, dtype=f32)
    return {
        "x": nrm(ks[0], (BATCH, SEQ, D_MODEL), 1.0),
        "c": nrm(ks[1], (BATCH, D_MODEL), 1.0),
        "ctx": nrm(ks[2], (BATCH, CTX_LEN, D_MODEL), 1.0),
        "c_ctx": nrm(ks[3], (D_MODEL,), 1.0),
        "w_mod": nrm(ks[4], (DEPTH, D_MODEL, 6 * D_MODEL), 0.5 * D_MODEL ** -0.5),
        "b_mod": nrm(ks[5], (DEPTH, 6 * D_MODEL), 0.01),
        "norm1_g": 1.0 + nrm(ks[6], (DEPTH, D_MODEL), 0.05),
        "norm2_g": 1.0 + nrm(ks[7], (DEPTH, D_MODEL), 0.05),
        "w_in": nrm(ks[8], (DEPTH, D_MODEL, D_SSM + D_POOL + 2 * D_MODEL), D_MODEL ** -0.5),
        "s5_a_re": -0.5 + nrm(ks[9], (DEPTH, 2, G, P), 0.01),
        "s5_a_im": math.pi * n_idx + nrm(ks[10], (DEPTH, 2, G, P), 0.01),
        "s5_log_dt": jax.random.uniform(ks[11], (DEPTH, 2, G), f32, math.log(DT_MIN), math.log(DT_MAX)),
        "s5_b_re": nrm(ks[12], (DEPTH, 2, G, P, CH), (2 * CH) ** -0.5),
        "s5_b_im": nrm(ks[13], (DEPTH, 2, G, P, CH), (2 * CH) ** -0.5),
        "s5_c_re": nrm(ks[14], (DEPTH, 2, G, CH, P), (2 * P) ** -0.5),
        "s5_c_im": nrm(ks[15], (DEPTH, 2, G, CH, P), (2 * P) ** -0.5),
        "s5_d": nrm(ks[16], (DEPTH, D_SSM), 0.5),
        "w_glu": nrm(ks[17], (DEPTH, D_SSM, D_SSM), D_SSM ** -0.5),
        "b_glu": nrm(ks[18], (DEPTH, D_SSM), 0.01),
        "pool_w": nrm(ks[19], (DEPTH, len(POOL_WINDOWS), POOL_GROUP_CH, POOL_GROUP_CH), POOL_GROUP_CH ** -0.5),
        "pool_scale": 1.0 + nrm(ks[20], (DEPTH, D_POOL), 0.1),
        "w_branch_a": nrm(ks[21], (DEPTH, D_SSM, D_MODEL), D_SSM ** -0.5),
        "w_branch_b": nrm(ks[22], (DEPTH, D_POOL, D_MODEL), D_POOL ** -0.5),
        "w_out": nrm(ks[23], (DEPTH, D_MODEL, D_MODEL), D_MODEL ** -0.5),
        "w_ffn_in": nrm(ks[24], (DEPTH, D_MODEL, 2 * FFN_HIDDEN), D_MODEL ** -0.5),
        "w_ffn_out": nrm(ks[25], (DEPTH, FFN_HIDDEN, D_MODEL), FFN_HIDDEN ** -0.5),
        "final_norm_g": 1.0 + nrm(ks[26], (D_MODEL,), 0.05),
    }


def reference(x, c, ctx, c_ctx, w_mod, b_mod, norm1_g, norm2_g, w_in, s5_a_re, s5_a_im, s5_log_dt,
              s5_b_re, s5_b_im, s5_c_re, s5_c_im, s5_d, w_glu, b_glu, pool_w, pool_scale,
              w_branch_a, w_branch_b, w_out, w_ffn_in, w_ffn_out, final_norm_g):
    bsz, n_tok, _ = x.shape
    rows = n_tok // GRID_W
    ctx_len = ctx.shape[1]
    for i in range(DEPTH):
        last = i == DEPTH - 1
        mod_x = (jax.nn.silu(c) @ w_mod[i] + b_mod[i])[:, None, :]
        mod_c = (jax.nn.silu(c_ctx) @ w_mod[i] + b_mod[i])[None, None, :]
        sh1, sc1, g1, sh2, sc2, g2 = jnp.split(mod_x, 6, axis=-1)
        csh1, csc1, cg1, csh2, csc2, cg2 = jnp.split(mod_c, 6, axis=-1)

        h = modulate(rms_norm(x, norm1_g[i]), sh1, sc1)
        hc = modulate(rms_norm(ctx, norm1_g[i]), csh1, csc1)
        proj = h @ w_in[i]
        proj_c = hc @ (w_in[i][:, :D_SSM] if last else w_in[i])

        a_f, b_f = s5_discretize(s5_a_re[i, 0], s5_a_im[i, 0], s5_log_dt[i, 0], s5_b_re[i, 0], s5_b_im[i, 0])
        a_b, b_b = s5_discretize(s5_a_re[i, 1], s5_a_im[i, 1], s5_log_dt[i, 1], s5_b_re[i, 1], s5_b_im[i, 1])
        c_f = lax.complex(s5_c_re[i, 0].astype(jnp.float32), s5_c_im[i, 0].astype(jnp.float32))
        c_b = lax.complex(s5_c_re[i, 1].astype(jnp.float32), s5_c_im[i, 1].astype(jnp.float32))

        ua_c = proj_c[..., :D_SSM]
        st_cf = s5_states(ua_c, a_f, b_f, None, False)
        st_cb = s5_states(ua_c, a_b, b_b, None, True)
        st_f = s5_states(proj[..., :D_SSM], a_f, b_f, st_cf[:, -1], False)
        st_b = s5_states(proj[..., :D_SSM], a_b, b_b, st_cb[:, 0], True)

        mixed = hybrid_mixer(proj, st_f, st_b, rows, GRID_W, c_f, c_b, s5_d[i], w_glu[i], b_glu[i],
                             pool_w[i], pool_scale[i], w_branch_a[i], w_branch_b[i], w_out[i])
        x = x + g1 * mixed
        h2 = modulate(rms_norm(x, norm2_g[i]), sh2, sc2)
        x = x + g2 * swiglu(h2, w_ffn_in[i], w_ffn_out[i])

        if not last:
            mixed_c = hybrid_mixer(proj_c, st_cf, st_cb, 1, ctx_len, c_f, c_b, s5_d[i], w_glu[i], b_glu[i],
                                   pool_w[i], pool_scale[i], w_branch_a[i], w_branch_b[i], w_out[i])
            ctx = ctx + cg1 * mixed_c
            hc2 = modulate(rms_norm(ctx, norm2_g[i]), csh2, csc2)
            ctx = ctx + cg2 * swiglu(hc2, w_ffn_in[i], w_ffn_out[i])

    return rms_norm(x, final_norm_g)
```

```python
import numpy as np
from contextlib import ExitStack
import concourse.bass as bass
import concourse.mybir as mybir
from concourse.bass_utils import run_bass_kernel_spmd

F32 = mybir.dt.float32
BF16 = mybir.dt.bfloat16
ALU = mybir.AluOpType
AF = mybir.ActivationFunctionType

D = 1024
KC = D // 128
NTOK = 2048
NT = NTOK // 128
FH = 2816
HT = FH // 128
EPS = 1e-6
MIXER = "full"
DEBUG = False
DBG_WORDS = 131072
STAGE = "full"
DBG_LAYOUT = {}


class Sched:
    def __init__(self):
        self.ops = []

    def op(self, eng, fn, reads=(), writes=(), dma=False, key=None):
        if dma:
            key = (eng, writes[0] if key is None else key)
        self.ops.append(dict(eng=eng, fn=fn, reads=list(reads), writes=list(writes),
                             dma=dma, key=key, deps=set(), sig=False))
        return len(self.ops) - 1

    @staticmethod
    def _ov(a, b):
        return a[0] == b[0] and a[1] < b[2] and b[1] < a[2]

    @staticmethod
    def _cov(a, b):
        return a[0] == b[0] and a[1] <= b[1] and b[2] <= a[2]

    def analyse(self):
        wr, rd = {}, {}
        for i, o in enumerate(self.ops):
            for r in o["reads"]:
                for (reg, j) in wr.get(r[0], []):
                    if self._ov(reg, r):
                        o["deps"].add(j)
            for w in o["writes"]:
                for (reg, j) in wr.get(w[0], []):
                    if self._ov(reg, w):
                        o["deps"].add(j)
                for (reg, j) in rd.get(w[0], []):
                    if self._ov(reg, w):
                        o["deps"].add(j)
            o["deps"].discard(i)
            for w in o["writes"]:
                wr[w[0]] = [(reg, j) for (reg, j) in wr.get(w[0], []) if not self._cov(w, reg)]
                rd[w[0]] = [(reg, j) for (reg, j) in rd.get(w[0], []) if not self._cov(w, reg)]
                wr[w[0]].append((w, i))
            for r in o["reads"]:
                rd.setdefault(r[0], []).append((r, i))
        for o in self.ops:
            for j in o["deps"]:
                self.ops[j]["sig"] = True
        cnt, dcnt = {}, {}
        self.dma_keys = []
        for o in self.ops:
            if o["dma"]:
                k = o["key"]
                if k not in dcnt:
                    dcnt[k] = 0
                    self.dma_keys.append(k)
                dcnt[k] += 1
                o["seq"] = dcnt[k]
            elif o["fn"] is not None and o["sig"]:
                cnt[o["eng"]] = cnt.get(o["eng"], 0) + 1
                o["seq"] = cnt[o["eng"]]

    def emit(self, eng, h, sems, dsems):
        waited = {}
        for o in self.ops:
            if o["eng"] != eng:
                continue
            need = {}
            for j in sorted(o["deps"]):
                p = self.ops[j]
                if p["dma"]:
                    kk = ("d", p["key"])
                    need[kk] = max(need.get(kk, 0), 16 * p["seq"])
                elif p["fn"] is not None:
                    kk = ("e", p["eng"])
                    need[kk] = max(need.get(kk, 0), p["seq"])
            for kk, v in need.items():
                if waited.get(kk, 0) >= v:
                    continue
                waited[kk] = v
                h.wait_ge(dsems[kk[1]] if kk[0] == "d" else sems[kk[1]], v)
            if o["fn"] is None:
                continue
            ins = o["fn"](h)
            if o["dma"]:
                ins.then_inc(dsems[o["key"]], 16)
            elif o["sig"]:
                ins.then_inc(sems[o["eng"]], 1)


def R(space, lo, hi):
    return (space, int(lo), int(hi))


class Buf:
    def __init__(self, space, t, n, dt=None, boff=0, esz=4):
        self.space, self.n, self.boff, self.esz = space, n, boff, esz
        self.t = t if dt is None else t[:, boff // 4:(boff + n * esz) // 4].bitcast(dt)

    def ap(self, lo, hi, p0=0, p1=128):
        return self.t[p0:p1, lo:hi]

    def reg(self, lo, hi):
        return R(self.space, self.boff + lo * self.esz, self.boff + hi * self.esz)


import math

G2 = 16
NDG = 32
PI = math.pi


class Ctx:
    pass


def make_ctx(nc, es, S, debug, dbg_words):
    c = Ctx()
    c.nc, c.S = nc, S
    ARENA_BYTES = 209920
    c.ARENA_BYTES = ARENA_BYTES
    arena = es.enter_context(nc.sbuf_tensor("arena", [128, ARENA_BYTES // 4], F32))
    c.arena = arena

    def sb(name, n, dt, off):
        esz = 4 if dt == F32 else 2
        assert off % 4 == 0 and (n * esz) % 4 == 0 and off + n * esz <= ARENA_BYTES, (name, off, n)
        return Buf("A", arena, n, dt, off, esz)
    c.sb = sb
    c.psb = [Buf("ps%d" % i, es.enter_context(nc.psum_tensor("ps%d" % i, [128, 512], F32))[:, :], 512) for i in range(8)]
    pst = {"i": 0}

    def ps_next():
        b = c.psb[pst["i"] % 8]
        pst["i"] += 1
        return b
    c.ps_next = ps_next

    c.dbg = nc.dram_tensor("dbg", [128, dbg_words], F32, kind="ExternalOutput").ap() if debug else None
    dst = {"off": 0}

    def dump(name, buf, lo, hi):
        if not debug:
            return
        b0, b1 = buf.boff + lo * buf.esz, buf.boff + hi * buf.esz
        nw = (b1 - b0) // 4
        o = dst["off"]
        dst["off"] += nw
        assert dst["off"] <= dbg_words, name
        DBG_LAYOUT[name] = (o, nw, buf.esz)
        S.op("sp", lambda e: e.dma_start(out=c.dbg[:, o:o + nw], in_=arena[:, b0 // 4:b1 // 4]),
             reads=[R("A", b0, b1)], writes=[R("dram_dbg", o, o + nw)], dma=True, key=("dbg", name))
    c.dump = dump

    def _rd(*xs):
        return [x[1] for x in xs if isinstance(x, tuple)]

    def _a(x):
        return x[0] if isinstance(x, tuple) else x

    def tt(eng, out, in0, in1, op):
        S.op(eng, lambda e: e.tensor_tensor(out=out[0], in0=in0[0], in1=in1[0], op=op), reads=_rd(in0, in1), writes=[out[1]])

    def ts(eng, out, in0, s1, op0, s2=None, op1=None):
        if op1 is None:
            S.op(eng, lambda e: e.tensor_scalar(out=out[0], in0=in0[0], scalar1=_a(s1), scalar2=None, op0=op0), reads=_rd(in0, s1), writes=[out[1]])
        else:
            S.op(eng, lambda e: e.tensor_scalar(out=out[0], in0=in0[0], scalar1=_a(s1), scalar2=_a(s2), op0=op0, op1=op1), reads=_rd(in0, s1, s2), writes=[out[1]])

    def stt(eng, out, in0, sc, in1, op0, op1):
        S.op(eng, lambda e: e.scalar_tensor_tensor(out=out[0], in0=in0[0], scalar=_a(sc), in1=in1[0], op0=op0, op1=op1), reads=_rd(in0, sc, in1), writes=[out[1]])

    def act(out, in_, func, scale=None, bias=None, accum=None):
        kw = {}
        if scale is not None:
            kw["scale"] = _a(scale)
        if bias is not None:
            kw["bias"] = _a(bias)
        if accum is not None:
            kw["accum_out"] = accum[0]
        wr = [out[1]] + ([accum[1]] if accum is not None else [])
        S.op("act", lambda e: e.activation(out=out[0], in_=in_[0], func=func, **kw), reads=_rd(in_, scale, bias), writes=wr)

    def cp(eng, out, in_):
        if eng == "act":
            act(out, in_, AF.Copy)
        else:
            S.op(eng, lambda e: e.tensor_copy(out=out[0], in_=in_[0]), reads=_rd(in_), writes=[out[1]])

    def ms(eng, out, val):
        S.op(eng, lambda e: e.memset(out[0], val), writes=[out[1]])

    def dma(eng, out, in_ap, in_reg=None):
        S.op(eng, lambda e: e.dma_start(out=out[0], in_=in_ap), reads=([in_reg] if in_reg else []), writes=[out[1]], dma=True)
    c.tt, c.ts, c.stt, c.act, c.cp, c.ms, c.dma = tt, ts, stt, act, cp, ms, dma
    return c


def V(buf, lo, hi, pat=None, **kw):
    ap = buf.ap(lo, hi)
    if pat:
        ap = ap.rearrange(pat, **kw)
    return (ap, buf.reg(lo, hi))


def s5_precompute(c, din):
    sb, tt, ts, stt, act, cp, ms, dma = c.sb, c.tt, c.ts, c.stt, c.act, c.cp, c.ms, c.dma
    S = c.S
    o = {}
    BASE = 20480
    ET = sb("ET", 8192, F32, 20480)
    FT = sb("FT", 8192, F32, 53248)
    YP = sb("YP", 8192, F32, 86016)
    EP = sb("EP", 128 * 128, BF16, 118784)
    X1 = 151552
    MT = sb("MT", 32 * 128, BF16, 184320)
    SBI = sb("SBI", 1024, F32, 192512)
    SCI = sb("SCI", 1024, F32, 196608)
    o.update(EP=EP, MT=MT)
    sm_off = {"o": 200704}

    def small(n=NDG):
        b = sb("sm", n, F32, sm_off["o"])
        sm_off["o"] += n * 4
        return b
    tmp4 = [sb("tmp%d" % i, 512, F32, 16384 + i * 2048) for i in range(2)]

    s5sm = small(96)
    dma("sp", V(s5sm, 0, 96), din["s5sm"][:, :])
    dma("sp", V(SBI, 0, 1024), din["s5B"][:, :])
    dma("sp", V(SCI, 0, 1024), din["s5C"][:, :])
    are, aim, ldt = V(s5sm, 0, 32), V(s5sm, 32, 64), V(s5sm, 64, 96)
    hm = small(2)
    dma("sp", V(hm, 0, 2), din["hm"][:, :])
    dcol = small(32)
    dma("sp", V(dcol, 0, 32), din["dcol"][:, :])

    def sv():
        b = small()
        return V(b, 0, NDG)
    dt_, al, th, ea = sv(), sv(), sv(), sv()
    xq = sv()
    ts("dve", xq, ldt, 1.0 / 16.0, ALU.mult)
    fct = [1.0]
    for k_ in range(1, 11):
        fct.append(fct[-1] * k_)
    ms("dve", dt_, 1.0 / fct[10])
    for k_ in range(9, -1, -1):
        tt("dve", dt_, dt_, xq, ALU.mult)
        ts("dve", dt_, dt_, 1.0 / fct[k_], ALU.add)
    for _ in range(4):
        tt("dve", dt_, dt_, dt_, ALU.mult)
    tt("dve", al, are, dt_, ALU.mult)
    tt("dve", th, aim, dt_, ALU.mult)
    ms("dve", ea, 1.0 / 720.0)
    for cf in (1.0 / 120.0, 1.0 / 24.0, 1.0 / 6.0, 0.5, 1.0, 1.0):
        tt("dve", ea, ea, al, ALU.mult)
        ts("dve", ea, ea, cf, ALU.add)
    sn, cs, wk, mk = sv(), sv(), sv(), sv()
    w2 = sv()
    ts("dve", wk, th, 1.0 / 16.0, ALU.mult)
    tt("dve", w2, wk, wk, ALU.mult)
    fact = [1.0]
    for k_ in range(1, 17):
        fact.append(fact[-1] * k_)
    ms("dve", sn, -1.0 / fact[15])
    for k_ in range(6, -1, -1):
        tt("dve", sn, sn, w2, ALU.mult)
        ts("dve", sn, sn, ((-1.0) ** k_) / fact[2 * k_ + 1], ALU.add)
    tt("dve", sn, sn, wk, ALU.mult)
    ms("dve", cs, 1.0 / fact[16])
    for k_ in range(7, -1, -1):
        tt("dve", cs, cs, w2, ALU.mult)
        ts("dve", cs, cs, ((-1.0) ** k_) / fact[2 * k_], ALU.add)
    for _ in range(4):
        tt("dve", mk, sn, cs, ALU.mult)
        tt("dve", wk, cs, cs, ALU.mult)
        tt("dve", w2, sn, sn, ALU.mult)
        tt("dve", cs, wk, w2, ALU.subtract)
        ts("dve", sn, mk, 2.0, ALU.mult)
    PWr, PWi = sb("PWr", 512, F32, X1), sb("PWi", 512, F32, X1 + 2048)
    PBr, PBi = sb("PBr", 512, F32, X1 + 4096), sb("PBi", 512, F32, X1 + 6144)

    def pw(buf, k):
        return (buf.ap(0, NDG * 16).rearrange("p (n k) -> p n k", k=16)[:, :, k + 7], buf.reg(0, NDG * 16))
    t1, t2 = sv(), sv()

    def cmul(outr, outi, ar, ai, br, bi, eng="dve", ta=None, tb=None):
        ta = ta or t1
        tb = tb or t2
        tt(eng, ta, ar, br, ALU.mult)
        tt(eng, tb, ai, bi, ALU.mult)
        tt(eng, outr, ta, tb, ALU.subtract)
        tt(eng, ta, ar, bi, ALU.mult)
        tt(eng, tb, ai, br, ALU.mult)
        tt(eng, outi, ta, tb, ALU.add)
    ms("dve", pw(PWr, 0), 1.0)
    ms("dve", pw(PWi, 0), 0.0)
    tt("dve", pw(PWr, 1), ea, cs, ALU.mult)
    tt("dve", pw(PWi, 1), ea, sn, ALU.mult)
    for k in range(2, 9):
        cmul(pw(PWr, k), pw(PWi, k), pw(PWr, k - 1), pw(PWi, k - 1), pw(PWr, 1), pw(PWi, 1))
    e2, e21 = sv(), sv()
    tt("dve", e21, ea, ea, ALU.mult)
    S.op("dve", lambda e: e.reciprocal(out=e21[0], in_=e21[0]), reads=[e21[1]], writes=[e21[1]])
    cp("dve", e2, e21)
    for k in range(1, 8):
        if k > 1:
            tt("dve", e2, e2, e21, ALU.mult)
        tt("dve", pw(PWr, -k), pw(PWr, k), e2, ALU.mult)
        stt("dve", pw(PWi, -k), pw(PWi, k), -1.0, e2, ALU.mult, ALU.mult)
    nr, den, cr, ci = sv(), sv(), sv(), sv()
    ts("dve", nr, pw(PWr, 1), -1.0, ALU.add)
    tt("dve", den, are, are, ALU.mult)
    tt("dve", t1, aim, aim, ALU.mult)
    tt("dve", den, den, t1, ALU.add)
    S.op("dve", lambda e: e.reciprocal(out=den[0], in_=den[0]), reads=[den[1]], writes=[den[1]])
    tt("dve", t1, nr, are, ALU.mult)
    tt("dve", t2, pw(PWi, 1), aim, ALU.mult)
    tt("dve", cr, t1, t2, ALU.add)
    tt("dve", cr, cr, den, ALU.mult)
    tt("dve", t1, pw(PWi, 1), are, ALU.mult)
    tt("dve", t2, nr, aim, ALU.mult)
    tt("dve", ci, t1, t2, ALU.subtract)
    tt("dve", ci, ci, den, ALU.mult)
    pwr3 = (PWr.ap(0, 512).rearrange("p (n k) -> p n k", k=16), PWr.reg(0, 512))
    pwi3 = (PWi.ap(0, 512).rearrange("p (n k) -> p n k", k=16), PWi.reg(0, 512))
    pbr3 = (PBr.ap(0, 512).rearrange("p (n k) -> p n k", k=16), PBr.reg(0, 512))
    pbi3 = (PBi.ap(0, 512).rearrange("p (n k) -> p n k", k=16), PBi.reg(0, 512))
    crb = (cr[0].unsqueeze(2).to_broadcast([128, NDG, 16]), cr[1])
    cib = (ci[0].unsqueeze(2).to_broadcast([128, NDG, 16]), ci[1])
    ta3 = (tmp4[0].ap(0, 512).rearrange("p (n k) -> p n k", k=16), tmp4[0].reg(0, 512))
    tb3 = (tmp4[1].ap(0, 512).rearrange("p (n k) -> p n k", k=16), tmp4[1].reg(0, 512))
    cmul(pbr3, pbi3, pwr3, pwi3, crb, cib, ta=ta3, tb=tb3)

    PWrn = sb("PWrn", 512, F32, X1 + 8192)
    ts("dve", V(PWrn, 0, 512), V(PWr, 0, 512), -1.0, ALU.mult)
    def tab(buf, d, ri, i):
        base = d * 4096 + ri * 128 + i * 16
        full = buf.ap(d * 4096, (d + 1) * 4096).rearrange("p (g r x) -> p g r x", g=G2, r=2)
        return (full[:, :, ri, i * 16:(i + 1) * 16], buf.reg(d * 4096, (d + 1) * 4096))

    def pslot(buf, d, slot):
        v3 = buf.ap(0, 512).rearrange("p (n k) -> p n k", k=16)
        return (v3[:, d * G2:(d + 1) * G2, slot:slot + 1].to_broadcast([128, G2, 16]), buf.reg(0, 512))

    def bc(buf, part, d):
        lo = part * 512 + d * 256
        return (buf.ap(lo, lo + 256).rearrange("p (g x) -> p g x", g=G2), buf.reg(lo, lo + 256))
    tq = [(tmp4[j // 2].ap((j % 2) * 256, (j % 2) * 256 + 256).rearrange("p (g x) -> p g x", g=G2),
           tmp4[j // 2].reg((j % 2) * 256, (j % 2) * 256 + 256)) for j in range(4)]
    for d in range(2):
        for i in range(8):
            sE = (14 - i) if d == 0 else (7 + i)
            sF = (8 + i) if d == 0 else (15 - i)
            sY = i if d == 0 else (7 - i)
            Br, Bi = bc(SBI, 0, d), bc(SBI, 1, d)
            Cr, Ci = bc(SCI, 0, d), bc(SCI, 1, d)
            pr, pi_ = pslot(PBr, d, sE), pslot(PBi, d, sE)
            tt("dve", tq[0], pr, Br, ALU.mult)
            tt("dve", tq[1], pi_, Bi, ALU.mult)
            tt("dve", tab(ET, d, 0, i), tq[0], tq[1], ALU.subtract)
            tt("dve", tq[0], pr, Bi, ALU.mult)
            tt("dve", tq[1], pi_, Br, ALU.mult)
            tt("dve", tab(ET, d, 1, i), tq[0], tq[1], ALU.add)
            for (TB, sl, eng, qa, qb) in ((FT, sF, "dve", tq[0], tq[1]), (YP, sY, "dve", tq[2], tq[3])):
                pr, pi_, prn = pslot(PWr, d, sl), pslot(PWi, d, sl), pslot(PWrn, d, sl)
                tt(eng, qa, pr, Cr, ALU.mult)
                tt(eng, qb, pi_, Ci, ALU.mult)
                tt(eng, tab(TB, d, 0, i), qa, qb, ALU.subtract)
                tt(eng, qa, prn, Ci, ALU.mult)
                tt(eng, qb, pi_, Cr, ALU.mult)
                tt(eng, tab(TB, d, 1, i), qa, qb, ALU.subtract)
    o_ea = ea
    p8r, p8i = sv(), sv()
    cp("dve", p8r, pw(PWr, 8))
    cp("dve", p8i, pw(PWi, 8))
    c.dump("ET", ET, 0, 8192)
    c.dump("FT", FT, 0, 8192)
    c.dump("YP", YP, 0, 8192)

    identf = c.identf
    ms("pool", V(EP, 0, 128 * 128), 0.0)
    ep4 = EP.ap(0, 128 * 128).rearrange("p (n h x) -> p n h x", h=2, x=128)
    for n0 in range(0, 64, 4):
        pb = c.ps_next()

        def trE(e, pb=pb, n0=n0):
            ins = None
            for q in range(4):
                ins = e.transpose(out=pb.ap(q * 128, (q + 1) * 128), in_=ET.ap((n0 + q) * 128, (n0 + q + 1) * 128), identity=identf.ap(0, 128))
            return ins
        S.op("pe", trE, reads=[ET.reg(n0 * 128, (n0 + 4) * 128), identf.reg(0, 128)], writes=[pb.reg(0, 512)])
        p3 = pb.ap(0, 512).rearrange("p (q x) -> p q x", q=4)
        for gh in range(2):
            eng = "act"
            c.cp(eng, (ep4[:, n0:n0 + 4, gh, gh * 64:(gh + 1) * 64], EP.reg(n0 * 256, (n0 + 4) * 256)),
                 (p3[:, :, gh * 64:(gh + 1) * 64], pb.reg(0, 512)))
    c.dump("EP", EP, 0, 128 * 128)

    mkf, mkb = sb("mkf", 128, F32, 16384), sb("mkb", 128, F32, 16896)
    dma("sp", V(mkf, 0, 128), c.din["maskf"][:, :])
    dma("sp", V(mkb, 0, 128), c.din["maskb"][:, :])
    YPP = sb("YPP", 8192, F32, X1)
    m1 = sb("m1", 128, F32, 17408)
    m2 = sb("m2", 128, F32, 17920)
    for gh in range(2):
        act(V(YPP, 0, 8192), V(YP, 0, 8192), AF.Copy, scale=V(hm, gh, gh + 1))
        for g2 in range(G2):
            g = 2 * g2 + gh
            pb = c.ps_next()

            def mmM(e, pb=pb, g2=g2):
                ins = None
                for d in range(2):
                    for ri in range(2):
                        lo = ((d * G2 + g2) * 2 + ri) * 128
                        ins = e.matmul(pb.ap(d * 128, (d + 1) * 128), lhsT=ET.ap(lo, lo + 128), rhs=YPP.ap(lo, lo + 128),
                                       start=(ri == 0), stop=(ri == 1))
                return ins
            S.op("pe", mmM, reads=[ET.reg(0, 8192), YPP.reg(0, 8192)], writes=[pb.reg(0, 256)])
            tt("dve", V(m1, 0, 128), (pb.ap(0, 128), pb.reg(0, 128)), V(mkf, 0, 128), ALU.mult)
            tt("dve", V(m2, 0, 128), (pb.ap(128, 256), pb.reg(128, 256)), V(mkb, 0, 128), ALU.mult)
            tt("dve", V(m1, 0, 128), V(m1, 0, 128), V(m2, 0, 128), ALU.add)
            stt("dve", V(MT, g * 128, (g + 1) * 128), V(identf, 0, 128), V(dcol, g, g + 1), V(m1, 0, 128), ALU.mult, ALU.add)
    c.dump("MT", MT, 0, 32 * 128)

    FTb = sb("FTb", 64 * 128, BF16, 20480)
    act(V(FTb, 0, 8192), V(FT, 0, 8192), AF.Copy)
    rho8, rinv, ur, ui = sv(), sv(), sv(), sv()
    tt("dve", rho8, o_ea, o_ea, ALU.mult)
    tt("dve", rho8, rho8, rho8, ALU.mult)
    tt("dve", rho8, rho8, rho8, ALU.mult)
    S.op("dve", lambda e: e.reciprocal(out=rinv[0], in_=rho8[0]), reads=[rho8[1]], writes=[rinv[1]])
    tt("dve", ur, p8r, rinv, ALU.mult)
    stt("dve", ui, p8i, -1.0, rinv, ALU.mult, ALU.mult)
    nq1, nq2 = sv(), sv()

    def unit(xr, xi):
        tt("dve", nq1, xr, xr, ALU.mult)
        tt("dve", nq2, xi, xi, ALU.mult)
        tt("dve", nq1, nq1, nq2, ALU.add)
        ts("dve", nq1, nq1, -0.5, ALU.mult, 1.5, ALU.add)
        tt("dve", xr, xr, nq1, ALU.mult)
        tt("dve", xi, xi, nq1, ALU.mult)
    unit(ur, ui)
    NJ = 256
    TRf = sb("TRf", NDG * NJ, F32, 53248)
    TIf = sb("TIf", NDG * NJ, F32, 53248 + NDG * NJ * 4)
    tr3 = TRf.ap(0, NDG * NJ).rearrange("p (n j) -> p n j", j=NJ)
    ti3 = TIf.ap(0, NDG * NJ).rearrange("p (n j) -> p n j", j=NJ)
    rR, rI = TRf.reg(0, NDG * NJ), TIf.reg(0, NDG * NJ)
    ms("dve", (tr3[:, :, 0:1], rR), 1.0)
    ms("dve", (ti3[:, :, 0:1], rI), 0.0)
    upr, upi = ur, ui
    upows = [(ur, ui)]
    ta_b = sb("tdA", NDG * 128, F32, X1)
    tb_b = sb("tdB", NDG * 128, F32, X1 + NDG * 128 * 4)
    for s_ in range(8):
        b = 1 << s_
        ta = (ta_b.ap(0, NDG * b).rearrange("p (n j) -> p n j", j=b), ta_b.reg(0, NDG * b))
        tb = (tb_b.ap(0, NDG * b).rearrange("p (n j) -> p n j", j=b), tb_b.reg(0, NDG * b))
        ubr = (upr[0].unsqueeze(2).to_broadcast([128, NDG, b]), upr[1])
        ubi = (upi[0].unsqueeze(2).to_broadcast([128, NDG, b]), upi[1])
        cmul((tr3[:, :, b:2 * b], rR), (ti3[:, :, b:2 * b], rI), (tr3[:, :, 0:b], rR), (ti3[:, :, 0:b], rI), ubr, ubi, ta=ta, tb=tb)
        nr_, ni_ = sv(), sv()
        cmul(nr_, ni_, upr, upi, upr, upi)
        unit(nr_, ni_)
        upr, upi = nr_, ni_
        upows.append((nr_, ni_))
    TB = sb("TB", 2 * NDG * NJ, BF16, X1)
    act(V(TB, 0, NDG * NJ), V(TRf, 0, NDG * NJ), AF.Copy)
    act(V(TB, NDG * NJ, 2 * NDG * NJ), V(TIf, 0, NDG * NJ), AF.Copy)
    o.update(FTb=FTb, TB=TB, rho8=rho8, u256=(upr, upi), u1=(ur, ui), al=al, ea=ea, upows=upows, hm=hm, small=small, sv=sv, cmul=cmul, t12=(t1, t2))
    c.dump("TRf", TRf, 0, NDG * NJ)
    c.dump("TIf", TIf, 0, NDG * NJ)
    return o


def mod_phase(c):
    S, sb, tt, ts, stt, act, cp, ms, dma = c.S, c.sb, c.tt, c.ts, c.stt, c.act, c.cp, c.ms, c.dma
    din = c.din
    ccs = sb("ccs", KC * 64, F32, 98304)
    ccb = sb("ccb", KC * 64, BF16, 100352)
    modrow = sb("modrow", 6 * D, F32, 53248)
    NWM = 2
    wmb = sb("wmb", NWM * KC * 512, BF16, 77824)
    dma("sp", V(ccs, 0, KC * 64), din["cc"][:, :])
    act(V(ccb, 0, KC * 64), V(ccs, 0, KC * 64), AF.Silu)
    wmod_v = din["w_mod"].rearrange("(k p) n -> p k n", p=128)
    bmrow = sb("bmrow", 6 * D, BF16, 102400)
    osel = sb("osel", 64, BF16, 101376)
    dma("pool", (bmrow.ap(0, 6 * D, 0, 1), bmrow.reg(0, 6 * D)), din["b_mod"][0:1, :])
    ms("pool", (osel.ap(0, 64, 0, 1), osel.reg(0, 64)), 0.0)
    ms("pool", (osel.ap(0, 1, 0, 1), osel.reg(0, 64)), 1.0)
    ms("pool", (osel.ap(32, 33, 0, 1), osel.reg(0, 64)), 1.0)
    for nb in range(12):
        sl = nb % NWM
        wlo, whi = sl * KC * 512, (sl + 1) * KC * 512
        wv = wmb.ap(wlo, whi).rearrange("p (k n) -> p k n", k=KC)
        dma("pool", (wv, wmb.reg(wlo, whi)), wmod_v[:, :, nb * 512:(nb + 1) * 512])
        pb = c.ps_next()

        def mm_mod(e, wv=wv, pb=pb, nb=nb):
            ins = None
            for kc in range(KC):
                ins = e.matmul(pb.ap(0, 512, 0, 64), lhsT=ccb.ap(kc * 64, (kc + 1) * 64), rhs=wv[:, kc, :], start=(kc == 0), stop=False)
            ins = e.matmul(pb.ap(0, 512, 0, 64), lhsT=osel.ap(0, 64, 0, 1), rhs=bmrow.ap(nb * 512, (nb + 1) * 512, 0, 1), start=False, stop=True)
            return ins
        S.op("pe", mm_mod, reads=[ccb.reg(0, KC * 64), wmb.reg(wlo, whi), osel.reg(0, 64), bmrow.reg(0, 6 * D)], writes=[pb.reg(0, 512)])
        act((modrow.ap(nb * 512, (nb + 1) * 512, 0, 64), modrow.reg(nb * 512, (nb + 1) * 512)), (pb.ap(0, 512, 0, 64), pb.reg(0, 512)), AF.Copy)
    modcol = sb("modcol", 4 * KC * 2, F32, 1536)
    for qi, q in enumerate((0, 1, 3, 4)):
        pb = c.ps_next()

        def tr_mod(e, pb=pb, q=q):
            ins = None
            for kc in range(KC):
                ins = e.transpose(out=pb.ap(kc * 64, (kc + 1) * 64), in_=modrow.ap(q * D + kc * 128, q * D + (kc + 1) * 128, 0, 64),
                                  identity=c.identf.ap(0, 64, 0, 64))
            return ins
        S.op("pe", tr_mod, reads=[modrow.reg(q * D, (q + 1) * D), c.identf.reg(0, 128)], writes=[pb.reg(0, 512)])
        act((modcol.ap(qi * KC * 2, (qi + 1) * KC * 2).rearrange("p (k c) -> p k c", k=KC), modcol.reg(qi * KC * 2, (qi + 1) * KC * 2)),
            (pb.ap(0, 512).rearrange("p (k c) -> p k c", k=KC)[:, :, 0:33:32], pb.reg(0, 512)), AF.Copy)
    gbc = sb("gbc", 2 * D, F32, 8192)
    for gi, q in enumerate((2, 5)):
        for hf in range(2):
            pb = c.ps_next()
            lo = q * D + hf * 512
            S.op("pe", lambda e, pb=pb, lo=lo: e.matmul(pb.ap(0, 512), lhsT=c.onesf.ap(0, 128, 0, 1), rhs=modrow.ap(lo, lo + 512, 0, 1), start=True, stop=True),
                 reads=[c.onesf.reg(0, 128), modrow.reg(lo, lo + 512)], writes=[pb.reg(0, 512)])
            glo = gi * D + hf * 512
            act(V(gbc, glo, glo + 512), (pb.ap(0, 512), pb.reg(0, 512)), AF.Copy)
    cols = sb("cols", 8 * KC, F32, 1280)
    n1c, n2c = V(cols, 0, KC), V(cols, KC, 2 * KC)
    dma("sp", n1c, din["n1col"][:, :])
    dma("sp", n2c, din["n2col"][:, :])
    mc = modcol.ap(0, 4 * KC * 2).rearrange("p (q k c) -> p q k c", q=4, k=KC)
    mr = modcol.reg(0, 4 * KC * 2)
    m = {"gbc": gbc}
    names = ["gs1", "sh1", "gs2", "sh2", "cgs1", "csh1"]
    for i, nm in enumerate(names):
        m[nm] = Buf("A", c.arena, KC, F32, 1280 + (2 + i) * KC * 4, 4)
    def finish():
        stt("dve", V(m["gs1"], 0, KC), (mc[:, 1, :, 0], mr), 1.0, n1c, ALU.add, ALU.mult)
        cp("dve", V(m["sh1"], 0, KC), (mc[:, 0, :, 0], mr))
        stt("dve", V(m["gs2"], 0, KC), (mc[:, 3, :, 0], mr), 1.0, n2c, ALU.add, ALU.mult)
        cp("dve", V(m["sh2"], 0, KC), (mc[:, 2, :, 0], mr))
        stt("dve", V(m["cgs1"], 0, KC), (mc[:, 1, :, 1], mr), 1.0, n1c, ALU.add, ALU.mult)
        cp("dve", V(m["csh1"], 0, KC), (mc[:, 0, :, 1], mr))
    m["finish"] = finish
    gf = sb("gf", D, F32, 4096)
    dma("sp", V(gf, 0, D), din["final_norm_g"].partition_broadcast(128).rearrange("p o n -> p (o n)"))
    m["gf"] = gf
    return m


def make_front(c, xs, xnb, junk, evac="dve"):
    S, ts, act = c.S, c.ts, c.act
    stc = {"i": 0}

    def front(src, row0, ntok, gsc, shc, hdst, hoff, hlen):
        for st_ in front_steps(src, row0, ntok, gsc, shc, hdst, hoff, hlen):
            st_()

    def front_steps(src, row0, ntok, gsc, shc, hdst, hoff, hlen, plain=False):
        hv = hdst.ap(0, KC * hlen).rearrange("p (k t) -> p k t", k=KC)
        ntile = ntok // 128
        steps = []
        for t0 in range(0, ntile, 2):
            grp = list(range(t0, min(t0 + 2, ntile)))
            info = {}
            steps.append(lambda grp=grp, info=info: pairA(src, row0, hdst, hoff, hlen, hv, grp, info))
            steps.append(lambda grp=grp, info=info: pairB(gsc, shc, hdst, hoff, hlen, hv, grp, info, plain))
        return steps

    def pairA(src, row0, hdst, hoff, hlen, hv, grp, info):
        if True:
            for t in grp:
                sl = t % 2
                sbase = (stc["i"] % 32) * 4
                stc["i"] += 1
                xa = V(xs, sl * D, (sl + 1) * D)
                c.dma("sp", xa, src[row0 + t * 128:row0 + (t + 1) * 128, :])
                info[t] = (sl, xa, V(c.st, sbase, sbase + 1), V(c.st, sbase + 1, sbase + 2))
            for t in grp:
                sl, xa, ss, s2 = info[t]
                act((hv[:, :, hoff + t * 128:hoff + (t + 1) * 128], hdst.reg(0, KC * hlen)),
                    (xa[0].rearrange("p (k x) -> p k x", k=KC), xa[1]), AF.Square, accum=ss)
            for t in grp:
                sl, xa, ss, s2 = info[t]
                act(s2, ss, AF.Ln, scale=1.0 / D, bias=c.epsc)
            for t in grp:
                sl, xa, ss, s2 = info[t]
                act(s2, s2, AF.Exp, scale=-0.5)
            for t in grp:
                sl, xa, ss, s2 = info[t]
                act(V(xnb, sl * D, (sl + 1) * D), xa, AF.Copy, scale=s2)

    def pairB(gsc, shc, hdst, hoff, hlen, hv, grp, info, plain=False):
        if True:
            for t in grp:
                sl, xa, ss, s2 = info[t]
                pb = c.ps_next()
                pbb = pb.t[:, :].bitcast(BF16)

                def tr_x(e, pbb=pbb, sl=sl):
                    ins = None
                    for kc in range(KC):
                        ins = e.transpose(out=pbb[:, kc * 128:(kc + 1) * 128], in_=xnb.ap(sl * D + kc * 128, sl * D + (kc + 1) * 128), identity=c.identb.ap(0, 128))
                    return ins
                S.op("pe", tr_x, reads=[xnb.reg(sl * D, (sl + 1) * D), c.identb.reg(0, 128)], writes=[pb.reg(0, 512)])
                if plain:
                    c.cp("act", (hv[:, :, hoff + t * 128:hoff + (t + 1) * 128], hdst.reg(0, KC * hlen)),
                         (pbb[:, 0:1024].rearrange("p (k x) -> p k x", k=KC), pb.reg(0, 512)))
                    continue
                for kc in range(KC):
                    lo = kc * hlen + hoff + t * 128
                    ts("dve", (hv[:, kc, hoff + t * 128:hoff + (t + 1) * 128], hdst.reg(lo, lo + 128)),
                       (pbb[:, kc * 128:(kc + 1) * 128], pb.reg(0, 512)), V(gsc, kc, kc + 1), ALU.mult, V(shc, kc, kc + 1), ALU.add)
    front.steps = front_steps
    return front


def backend(c, mod, yT, own_x, x1d, use_s5=True):
    S, sb, tt, ts, stt, act, cp, ms, dma = c.S, c.sb, c.tt, c.ts, c.stt, c.act, c.cp, c.ms, c.dma
    din = c.din
    wG = sb("wG", KC * 2048, BF16, 20480)
    wB = sb("wB", KC * 512, BF16, 53248)
    wbb = sb("wbb", 4 * D, BF16, 61440)
    wba = sb("wba", 4 * D, BF16, 69632)
    wo = sb("wo", KC * D, BF16, 77824)
    wglu = sb("wglu", 4 * 512, BF16, 94208)
    poolw = sb("poolw", 4 * 128, BF16, 98304)
    Bmat = sb("Bmat", 4 * 128, BF16, 99328)
    invc = sb("invc", 4 * 128, F32, 100352)
    sc2 = sb("sc2", 8, F32, 102400)
    hTc = sb("hTc", KC * 512, BF16, 102912)
    UBc = sb("UBc", 4 * 512, BF16, 111104)
    PM = sb("PM", 4 * 512, BF16, 115200)
    ZT = sb("ZT", 4 * 512, BF16, 119296)
    zT = sb("zT", 4 * 512, BF16, 123392)
    sg = sb("sg", 2 * 512, BF16, 127488)
    tmpf = sb("tmpf", 2 * 512, F32, 129536)
    MGc = sb("MGc", KC * 512, BF16, 151552)
    xs = sb("xsb", 2 * D, F32, 159744)
    xnb = sb("xnbb", 2 * D, BF16, 167936)
    junk = sb("junkb", D, BF16, 188416)
    xr = sb("xr", 2 * D, F32, 172032)
    x1o = sb("x1o", 2 * D, F32, 180224)
    front = make_front(c, xs, xnb, junk, evac="dve")
    w_in_v = din["w_in"].rearrange("(k p) n -> p k n", p=128)
    dma("pool", V(wB, 0, KC * 512, "p (k n) -> p k n", k=KC), w_in_v[:, :, 512:1024])
    dma("pool", V(wG, 0, KC * 2048, "p (k n) -> p k n", k=KC), w_in_v[:, :, 1024:3072])
    dma("pool", V(wbb, 0, 4 * D, "p (k n) -> p k n", k=4), din["w_branch_b"].rearrange("(k p) n -> p k n", p=128))
    dma("pool", V(wba, 0, 4 * D, "p (k n) -> p k n", k=4), din["w_branch_a"].rearrange("(k p) n -> p k n", p=128))
    dma("pool", V(wglu, 0, 4 * 512, "p (k n) -> p k n", k=4), din["w_glu"].rearrange("(k p) n -> p k n", p=128))
    dma("pool", V(poolw, 0, 4 * 128, "p (k n) -> p k n", k=4), din["pool_w"].rearrange("j c n -> c j n"))
    dma("pool", V(Bmat, 0, 4 * 128, "p (k n) -> p k n", k=4), din["poolB"].rearrange("j c n -> c j n"))
    dma("pool", V(wo, 0, KC * D, "p (k n) -> p k n", k=KC), din["w_out"].rearrange("(k p) n -> p k n", p=128))
    dma("sp", V(invc, 0, 512), din["invc"].partition_broadcast(128).rearrange("p o n -> p (o n)"))
    dma("sp", V(sc2, 0, 8), din["sc2"][:, :])
    wG3 = wG.ap(0, KC * 2048).rearrange("p (k n) -> p k n", k=KC)
    wB3 = wB.ap(0, KC * 512).rearrange("p (k n) -> p k n", k=KC)
    wbb3 = wbb.ap(0, 4 * D).rearrange("p (k n) -> p k n", k=4)
    wba3 = wba.ap(0, 4 * D).rearrange("p (k n) -> p k n", k=4)
    wo3 = wo.ap(0, KC * D).rearrange("p (k n) -> p k n", k=KC)
    wglu3 = wglu.ap(0, 4 * 512).rearrange("p (k n) -> p k n", k=4)
    pw3 = poolw.ap(0, 512).rearrange("p (k n) -> p k n", k=4)
    bm3 = Bmat.ap(0, 512).rearrange("p (k n) -> p k n", k=4)
    hv = hTc.ap(0, KC * 512).rearrange("p (k t) -> p k t", k=KC)
    ub3 = UBc.ap(0, 4 * 512).rearrange("p (q n) -> p q n", q=4)
    pm3 = PM.ap(0, 4 * 512).rearrange("p (j t) -> p j t", j=4)
    zt3 = ZT.ap(0, 4 * 512).rearrange("p (j t) -> p j t", j=4)
    zz3 = zT.ap(0, 4 * 512).rearrange("p (j t) -> p j t", j=4)
    mg3 = MGc.ap(0, KC * 512).rearrange("p (m t) -> p m t", m=KC)
    yT3 = yT.ap(0, 4 * NTOK).rearrange("p (q t) -> p q t", q=4) if yT is not None else None
    gbc = mod["gbc"]
    pend = []
    for st_ in front.steps(own_x, 0, 512, mod["gs1"], mod["sh1"], hTc, 0, 512):
        st_()
    for n in range(NT // 4):
        for q in range(4):
            pb = c.ps_next()

            def mm(e, pb=pb, q=q):
                ins = None
                for kc in range(KC):
                    ins = e.matmul(pb.ap(0, 512), lhsT=hv[:, kc, q * 128:(q + 1) * 128], rhs=wB3[:, kc, :], start=(kc == 0), stop=(kc == KC - 1))
                return ins
            S.op("pe", mm, reads=[hTc.reg(0, KC * 512), wB.reg(0, KC * 512)], writes=[pb.reg(0, 512)])
            cp("act", (ub3[:, q, :], UBc.reg(q * 512, (q + 1) * 512)), (pb.ap(0, 512), pb.reg(0, 512)))
        for q in range(4):
            pb = c.ps_next()

            def mmP(e, pb=pb, q=q):
                ins = None
                for jw in range(4):
                    ins = e.matmul(pb.ap(jw * 128, (jw + 1) * 128), lhsT=ub3[:, q, jw * 128:(jw + 1) * 128], rhs=bm3[:, jw, :], start=True, stop=True)
                return ins
            S.op("pe", mmP, reads=[UBc.reg(q * 512, (q + 1) * 512), Bmat.reg(0, 512)], writes=[pb.reg(0, 512)])
            tt("dve", (pm3[:, :, q * 128:(q + 1) * 128], PM.reg(0, 4 * 512)), (pb.ap(0, 512).rearrange("p (j t) -> p j t", j=4), pb.reg(0, 512)),
               V(invc, 0, 512, "p (j t) -> p j t", j=4), ALU.mult)
        for jw in range(4):
            pb = c.ps_next()
            S.op("pe", lambda e, pb=pb, jw=jw: e.matmul(pb.ap(0, 512), lhsT=pw3[:, jw, :], rhs=pm3[:, jw, :], start=True, stop=True),
                 reads=[poolw.reg(0, 512), PM.reg(jw * 512, (jw + 1) * 512)], writes=[pb.reg(0, 512)])
            ts("dve", (zt3[:, jw, :], ZT.reg(jw * 512, (jw + 1) * 512)), (pb.ap(0, 512), pb.reg(0, 512)), V(sc2, jw, jw + 1), ALU.mult)
        if use_s5:
            for m4 in range(4):
                pb = c.ps_next()

                def mmG(e, pb=pb, m4=m4, n=n):
                    ins = None
                    for q4 in range(4):
                        ins = e.matmul(pb.ap(0, 512), lhsT=wglu3[:, q4, m4 * 128:(m4 + 1) * 128], rhs=yT3[:, q4, n * 512:(n + 1) * 512], start=(q4 == 0), stop=(q4 == 3))
                    return ins
                S.op("pe", mmG, reads=[wglu.reg(0, 2048), yT.reg(0, 4 * NTOK)], writes=[pb.reg(0, 512)])
                so = (m4 % 2) * 512
                act(V(sg, so, so + 512), (pb.ap(0, 512), pb.reg(0, 512)), AF.Sigmoid, bias=V(sc2, 4 + m4, 5 + m4))
                tt("dve", (zz3[:, m4, :], zT.reg(m4 * 512, (m4 + 1) * 512)), (yT3[:, m4, n * 512:(n + 1) * 512], yT.reg(0, 4 * NTOK)), V(sg, so, so + 512), ALU.mult)
        for m in range(KC):
            pbs = {}
            if use_s5:
                pa, pga = c.ps_next(), c.ps_next()

                def mmA(e, pa=pa, pga=pga, m=m):
                    ins = None
                    for q4 in range(4):
                        ins = e.matmul(pa.ap(0, 512), lhsT=wba3[:, q4, m * 128:(m + 1) * 128], rhs=zz3[:, q4, :], start=(q4 == 0), stop=(q4 == 3))
                    for kc in range(KC):
                        ins = e.matmul(pga.ap(0, 512), lhsT=wG3[:, kc, m * 128:(m + 1) * 128], rhs=hv[:, kc, :], start=(kc == 0), stop=(kc == KC - 1))
                    return ins
                S.op("pe", mmA, reads=[wba.reg(0, 4 * D), zT.reg(0, 2048), wG.reg(0, KC * 2048), hTc.reg(0, KC * 512)], writes=[pa.reg(0, 512), pga.reg(0, 512)])
                act(V(sg, 0, 512), (pga.ap(0, 512), pga.reg(0, 512)), AF.Sigmoid)
                tt("dve", V(tmpf, 0, 512), (pa.ap(0, 512), pa.reg(0, 512)), V(sg, 0, 512), ALU.mult)
            pbb_, pgb = c.ps_next(), c.ps_next()

            def mmB(e, pbb_=pbb_, pgb=pgb, m=m):
                ins = None
                for jw in range(4):
                    ins = e.matmul(pbb_.ap(0, 512), lhsT=wbb3[:, jw, m * 128:(m + 1) * 128], rhs=zt3[:, jw, :], start=(jw == 0), stop=(jw == 3))
                for kc in range(KC):
                    ins = e.matmul(pgb.ap(0, 512), lhsT=wG3[:, kc, 1024 + m * 128:1024 + (m + 1) * 128], rhs=hv[:, kc, :], start=(kc == 0), stop=(kc == KC - 1))
                return ins
            S.op("pe", mmB, reads=[wbb.reg(0, 4 * D), ZT.reg(0, 2048), wG.reg(0, KC * 2048), hTc.reg(0, KC * 512)], writes=[pbb_.reg(0, 512), pgb.reg(0, 512)])
            act(V(sg, 512, 1024), (pgb.ap(0, 512), pgb.reg(0, 512)), AF.Sigmoid)
            if use_s5:
                tt("dve", V(tmpf, 512, 1024), (pbb_.ap(0, 512), pbb_.reg(0, 512)), V(sg, 512, 1024), ALU.mult)
                tt("pool", (mg3[:, m, :], MGc.reg(m * 512, (m + 1) * 512)), V(tmpf, 0, 512), V(tmpf, 512, 1024), ALU.add)
            else:
                tt("dve", (mg3[:, m, :], MGc.reg(m * 512, (m + 1) * 512)), (pbb_.ap(0, 512), pbb_.reg(0, 512)), V(sg, 512, 1024), ALU.mult)
        pend = front.steps(own_x, (n + 1) * 512, 512, mod["gs1"], mod["sh1"], hTc, 0, 512) if n + 1 < NT // 4 else []
        for q in range(4):
            if pend:
                pend.pop(0)()
            tt_ = n * 4 + q
            sl = tt_ % 2
            xa = V(xr, sl * D, (sl + 1) * D)
            dma("sp", xa, own_x[tt_ * 128:(tt_ + 1) * 128, :])
            for hf in range(2):
                pb = c.ps_next()

                def mmO(e, pb=pb, q=q, hf=hf):
                    ins = None
                    for m in range(KC):
                        ins = e.matmul(pb.ap(0, 512), lhsT=mg3[:, m, q * 128:(q + 1) * 128], rhs=wo3[:, m, hf * 512:(hf + 1) * 512], start=(m == 0), stop=(m == KC - 1))
                    return ins
                S.op("pe", mmO, reads=[MGc.reg(0, KC * 512), wo.reg(0, KC * D)], writes=[pb.reg(0, 512)])
                lo = sl * D + hf * 512
                tt("dve", V(x1o, lo, lo + 512), (pb.ap(0, 512), pb.reg(0, 512)), V(gbc, hf * 512, (hf + 1) * 512), ALU.mult)
                tt("pool", V(x1o, lo, lo + 512), V(x1o, lo, lo + 512), V(xr, lo, lo + 512), ALU.add)
            S.op("pool", lambda e, tt_=tt_, sl=sl: e.dma_start(out=x1d[tt_ * 128:(tt_ + 1) * 128, :], in_=x1o.ap(sl * D, (sl + 1) * D)),
                 reads=[x1o.reg(sl * D, (sl + 1) * D)], writes=[R("dram_x1", tt_ * 128, (tt_ + 1) * 128)], dma=True, key=("x1st", sl))


def ffn_phase(c, mod, x1d, out):
    S, sb, tt, ts, stt, act, cp, ms, dma = c.S, c.sb, c.tt, c.ts, c.stt, c.act, c.cp, c.ms, c.dma
    din = c.din
    wfi = sb("wfi", KC * 2 * FH, BF16, 20480)
    wfo = sb("wfo", HT * D, BF16, 110592)
    actT = sb("actT", HT * 512, BF16, 155648)
    h2T = sb("h2T", KC * 512, BF16, 178176)
    x1s = sb("x1s", 4 * D, F32, 186368)
    xnb = sb("xnbf", D, BF16, 202752)
    sgt = sb("sgt", 2 * 512, BF16, 204800)
    junk = sb("junkf", D, BF16, 206848)
    yo = sb("yo", D, F32, 8192)
    gbc, gf, gs2, sh2c = mod["gbc"], mod["gf"], mod["gs2"], mod["sh2"]
    wfi_v = din["w_ffn_in"].rearrange("(k p) n -> p k n", p=128)
    wfi3w = wfi.ap(0, KC * 2 * FH).rearrange("p (k n) -> p k n", k=KC)
    HB = 2
    wfi_regs = {}
    for h0 in range(0, HT, HB):
        for half in range(2):
            c0 = half * FH + h0 * 128
            c1 = c0 + HB * 128
            S.op("pool", lambda e, c0=c0, c1=c1: e.dma_start(out=wfi3w[:, :, c0:c1], in_=wfi_v[:, :, c0:c1]),
                 writes=[R("A", wfi.boff + (kc * 2 * FH + c0) * 2, wfi.boff + (kc * 2 * FH + c1) * 2) for kc in range(KC)],
                 dma=True, key=("wfi", h0, half))
    wfo_v = din["w_ffn_out"].rearrange("(k p) n -> p k n", p=128)
    for h0 in range(0, HT, 11):
        lo, hi = h0 * D, (h0 + 11) * D
        dma("pool", V(wfo, lo, hi, "p (k n) -> p k n", k=11), wfo_v[:, h0:h0 + 11, :])
    wfi3 = wfi.ap(0, KC * 2 * FH).rearrange("p (k n) -> p k n", k=KC)
    wfo3 = wfo.ap(0, HT * D).rearrange("p (k n) -> p k n", k=HT)
    st = c.st
    stc = {"i": 0}

    def rstd_ops(xa):
        base = (stc["i"] % 32) * 4
        stc["i"] += 1
        ss, s2 = V(st, base, base + 1), V(st, base + 1, base + 2)
        act(V(junk, 0, D), xa, AF.Square, accum=ss)
        act(s2, ss, AF.Ln, scale=1.0 / D, bias=c.epsc)
        act(s2, s2, AF.Exp, scale=-0.5)
        return s2

    def load_x1(tt_, sl):
        S.op("sp", lambda e: e.dma_start(out=x1s.ap(sl * D, (sl + 1) * D), in_=x1d[tt_ * 128:(tt_ + 1) * 128, :]),
             reads=[R("dram_x1", tt_ * 128, (tt_ + 1) * 128)], writes=[x1s.reg(sl * D, (sl + 1) * D)], dma=True)
    h2v = h2T.ap(0, KC * 512).rearrange("p (k t) -> p k t", k=KC)

    def front_a(n, q):
        tt_ = n * 4 + q
        sl = tt_ % 2
        load_x1(tt_, sl)
        xa = V(x1s, sl * D, (sl + 1) * D)
        rs = rstd_ops(xa)
        act(V(xnb, 0, D), xa, AF.Copy, scale=rs)

    def front_b(n, q):
        pb = c.ps_next()
        pbb = pb.t[:, :].bitcast(BF16)

        def tr_x(e, pbb=pbb):
            ins = None
            for kc in range(KC):
                ins = e.transpose(out=pbb[:, kc * 128:(kc + 1) * 128], in_=xnb.ap(kc * 128, (kc + 1) * 128), identity=c.identb.ap(0, 128))
            return ins
        S.op("pe", tr_x, reads=[xnb.reg(0, D), c.identb.reg(0, 128)], writes=[pb.reg(0, 512)])
        for kc in range(KC):
            ts("dve", (h2v[:, kc, q * 128:(q + 1) * 128], h2T.reg(kc * 512 + q * 128, kc * 512 + (q + 1) * 128)),
               (pbb[:, kc * 128:(kc + 1) * 128], pb.reg(0, 512)), V(gs2, kc, kc + 1), ALU.mult, V(sh2c, kc, kc + 1), ALU.add)

    def front_tile(n, q):
        front_a(n, q)
        front_b(n, q)

    def hidden(n):
        for hh in range(HT):
            pg, pu = c.ps_next(), c.ps_next()

            def mm_gu(e, pg=pg, pu=pu, hh=hh):
                ins = None
                for kc in range(KC):
                    ins = e.matmul(pg.ap(0, 512), lhsT=wfi3[:, kc, hh * 128:(hh + 1) * 128], rhs=h2v[:, kc, :], start=(kc == 0), stop=(kc == KC - 1))
                for kc in range(KC):
                    ins = e.matmul(pu.ap(0, 512), lhsT=wfi3[:, kc, FH + hh * 128:FH + (hh + 1) * 128], rhs=h2v[:, kc, :], start=(kc == 0), stop=(kc == KC - 1))
                return ins
            rd = [h2T.reg(0, KC * 512)]
            for kc in range(KC):
                for half in range(2):
                    c0 = half * FH + hh * 128
                    rd.append(R("A", wfi.boff + (kc * 2 * FH + c0) * 2, wfi.boff + (kc * 2 * FH + c0 + 128) * 2))
            S.op("pe", mm_gu, reads=rd, writes=[pg.reg(0, 512), pu.reg(0, 512)])
            so = (hh % 2) * 512
            act(V(sgt, so, so + 512), (pg.ap(0, 512), pg.reg(0, 512)), AF.Silu)
            tt("dve", V(actT, hh * 512, (hh + 1) * 512), (pu.ap(0, 512), pu.reg(0, 512)), V(sgt, so, so + 512), ALU.mult)

    def tail_tile(n, q):
        tt_ = n * 4 + q
        sl = 2 + tt_ % 2
        load_x1(tt_, sl)
        xa = V(x1s, sl * D, (sl + 1) * D)
        for hf in range(2):
            po = c.ps_next()

            def mm_o(e, po=po, q=q, hf=hf):
                ins = None
                for hh in range(HT):
                    ins = e.matmul(po.ap(0, 512), lhsT=actT.ap(hh * 512 + q * 128, hh * 512 + (q + 1) * 128), rhs=wfo3[:, hh, hf * 512:(hf + 1) * 512],
                                   start=(hh == 0), stop=(hh == HT - 1))
                return ins
            S.op("pe", mm_o, reads=[actT.reg(0, HT * 512), wfo.reg(0, HT * D)], writes=[po.reg(0, 512)])
            tt("dve", V(yo, hf * 512, (hf + 1) * 512), (po.ap(0, 512), po.reg(0, 512)), V(gbc, D + hf * 512, D + (hf + 1) * 512), ALU.mult)
        tt("pool", V(yo, 0, D), V(yo, 0, D), xa, ALU.add)
        rs = rstd_ops(V(yo, 0, D))
        stt("dve", V(yo, 0, D), V(yo, 0, D), rs, V(gf, 0, D), ALU.mult, ALU.mult)
        S.op("pool", lambda e, tt_=tt_: e.dma_start(out=out[tt_ * 128:(tt_ + 1) * 128, :], in_=yo.ap(0, D)),
             reads=[yo.reg(0, D)], writes=[R("dram_out", tt_ * 128, (tt_ + 1) * 128)], dma=True, key=("st", 0))

    NCH = NT // 4
    for q in range(4):
        front_tile(0, q)
    for n in range(NCH):
        hidden(n)
        for q in range(4):
            if n + 1 < NCH:
                front_a(n + 1, q)
            tail_tile(n, q)
            if n + 1 < NCH:
                front_b(n + 1, q)


def s5_run(c, s5, mod, xb_rows, ctx_rows, seg_of_slot, own_x):
    S, sb, tt, ts, stt, act, cp, ms, dma = c.S, c.sb, c.tt, c.ts, c.stt, c.act, c.cp, c.ms, c.dma
    din = c.din
    EP, FTb, MT, TB, rho8 = s5["EP"], s5["FTb"], s5["MT"], s5["TB"], s5["rho8"]
    cmul = s5["cmul"]
    p2 = {"o": 16384}

    def small(n=NDG):
        assert p2["o"] + n * 4 <= 20480, "small pool overflow"
        b = sb("sm2", n, F32, p2["o"])
        p2["o"] += n * 4
        assert p2["o"] <= 20480 and not (2560 < p2["o"] <= 16384 and p2["o"] > 4096), p2["o"]
        return b

    def sv():
        return V(small(), 0, NDG)
    NJ = 256
    hTu = sb("hTu", KC * 1024, BF16, 53248)
    Xu = sb("Xu", 32 * 128, BF16, 69632)
    Uu = sb("Uu", 32 * 256, BF16, 77824)
    Uu2 = sb("Uu2", 32 * 256, BF16, 36864)
    Ubufs = [Uu, Uu2]
    xs = sb("xs", 2 * D, F32, 94208)
    xnb = sb("xnb", 2 * D, BF16, 102400)
    junk = None
    wA = sb("wA", KC * 512, BF16, 106496)
    Vq = sb("Vq", 2048, F32, 192512)
    tq = sb("tq", 1024, F32, 114688)
    ucnt = {"i": 0}
    st = c.st
    wA3 = wA.ap(0, KC * 512).rearrange("p (k n) -> p k n", k=KC)
    dma("pool", V(wA, 0, KC * 512, "p (k n) -> p k n", k=KC), din["w_in"].rearrange("(k p) n -> p k n", p=128)[:, :, 0:512])
    front = make_front(c, xs, xnb, junk)

    def ua_steps(hsrc, hoff, hlen, nk, Udst, uoff, ulen, bias=False):
        hv = hsrc.ap(0, KC * hlen).rearrange("p (k t) -> p k t", k=KC)
        x4 = Xu.ap(0, 32 * 128).rearrange("p (g i x) -> p g i x", g=32, i=8)

        def step_a():
            for i in range(8):
                pb = c.ps_next()

                def mm(e, pb=pb, i=i):
                    ins = None
                    for kc in range(KC):
                        ins = e.matmul(pb.ap(0, 512, 0, nk), lhsT=hv[:, kc, hoff + i:hoff + 8 * nk:8], rhs=wA3[:, kc, :],
                                       start=(kc == 0), stop=(kc == KC - 1 and not bias))
                    if bias:
                        ins = e.matmul(pb.ap(0, 512, 0, nk), lhsT=onesb.ap(0, nk, 0, 1), rhs=brow.ap(0, 512, 0, 1), start=False, stop=True)
                    return ins
                S.op("pe", mm, reads=[hsrc.reg(0, KC * hlen), wA.reg(0, KC * 512), onesb.reg(0, 128), brow.reg(0, 512)], writes=[pb.reg(0, 512)])
                cp("act", (x4[0:nk, :, i, :], Xu.reg(0, 32 * 128)), (pb.ap(0, 512, 0, nk).rearrange("p (g x) -> p g x", g=32), pb.reg(0, 512)))

        def step_b():
            u3 = Udst.ap(0, 32 * ulen).rearrange("p (g k) -> p g k", g=32)
            for g0 in range(0, 32, 8):
                pb = c.ps_next()
                pbb = pb.t[:, :].bitcast(BF16)

                def trU(e, pbb=pbb, g0=g0):
                    ins = None
                    for q in range(8):
                        ins = e.transpose(out=pbb[:, q * 128:q * 128 + nk], in_=Xu.ap((g0 + q) * 128, (g0 + q + 1) * 128, 0, nk),
                                          identity=c.identb.ap(0, nk, 0, nk))
                    return ins
                S.op("pe", trU, reads=[Xu.reg(0, 32 * 128), c.identb.reg(0, 128)], writes=[pb.reg(0, 512)])
                cp("act", (u3[:, g0:g0 + 8, uoff:uoff + nk], Udst.reg(0, 32 * ulen)),
                   (pbb[:, 0:1024].rearrange("p (q k) -> p q k", q=8)[:, :, 0:nk], pb.reg(0, 512)))
        return [step_a, step_b]

    ep5 = EP.ap(0, 128 * 128).rearrange("p (d g r h x) -> p d g r h x", d=2, g=G2, r=2, h=2)
    tb4 = TB.ap(0, 2 * NDG * NJ).rearrange("p (c n j) -> p c n j", c=2, n=NDG)

    def e_unit(Usrc, uoff, ulen, nk, d, qq, j0, rev, Vdst, tmp):
        u3 = Usrc.ap(0, 32 * ulen).rearrange("p (g k) -> p g k", g=32)
        pbs = [c.ps_next(), c.ps_next()]

        def mm(e):
            ins = None
            for ri in range(2):
                for gq in range(4):
                    g2 = qq * 4 + gq
                    for gh in range(2):
                        ins = e.matmul(pbs[ri].ap(gq * 128, gq * 128 + nk), lhsT=ep5[:, d, g2, ri, gh, :],
                                       rhs=u3[:, 2 * g2 + gh, uoff:uoff + nk], start=(gh == 0), stop=(gh == 1))
            return ins
        S.op("pe", mm, reads=[EP.reg(0, 128 * 128), Usrc.reg(0, 32 * ulen)], writes=[pbs[0].reg(0, 512), pbs[1].reg(0, 512)])
        n0 = d * G2 + qq * 4
        if not rev:
            tr = tb4[:, 0, n0:n0 + 4, j0:j0 + nk]
            ti = tb4[:, 1, n0:n0 + 4, j0:j0 + nk]
        else:
            lo = j0 - nk + 1
            tr = tb4[:, 0, n0:n0 + 4, lo:lo + nk][:, :, ::-1]
            ti = tb4[:, 1, n0:n0 + 4, lo:lo + nk][:, :, ::-1]
        TR, TI = (tr, TB.reg(0, 2 * NDG * NJ)), (ti, TB.reg(0, 2 * NDG * NJ))
        sre = (pbs[0].ap(0, 512).rearrange("p (g k) -> p g k", g=4)[:, :, 0:nk], pbs[0].reg(0, 512))
        sim = (pbs[1].ap(0, 512).rearrange("p (g k) -> p g k", g=4)[:, :, 0:nk], pbs[1].reg(0, 512))
        vb, lo_ = Vdst
        vre = (vb.ap(lo_, lo_ + 512).rearrange("p (g k) -> p g k", g=4)[:, :, 0:nk], vb.reg(lo_, lo_ + 512))
        vim = (vb.ap(lo_ + 512, lo_ + 1024).rearrange("p (g k) -> p g k", g=4)[:, :, 0:nk], vb.reg(lo_ + 512, lo_ + 1024))
        ta = (tmp.ap(0, 512).rearrange("p (g k) -> p g k", g=4)[:, :, 0:nk], tmp.reg(0, 512))
        tb_ = (tmp.ap(512, 1024).rearrange("p (g k) -> p g k", g=4)[:, :, 0:nk], tmp.reg(512, 1024))
        tt("dve", vre, sre, TR, ALU.mult)
        tt("dve", ta, sim, TI, ALU.mult)
        tt("dve", vim, sim, TR, ALU.mult)
        tt("dve", tb_, sre, TI, ALU.mult)
        tt("dve", vre, vre, ta, ALU.subtract)
        tt("dve", vim, vim, tb_, ALU.add)

    def scan_unit(Vsrc, nk, d, qq, rev, inits, Wdst, lasts=None):
        vb, vlo = Vsrc
        wb, wlo = Wdst
        v3 = vb.ap(vlo, vlo + 1024).rearrange("p (n k) -> p n k", n=8)
        w3 = wb.ap(wlo, wlo + 1024).rearrange("p (n k) -> p n k", n=8)
        for ri in range(2):
            for gq in range(4):
                n_ = d * G2 + qq * 4 + gq
                coef = (rho8[0][:, n_:n_ + 1].to_broadcast([128, nk]), rho8[1])
                n = ri * 4 + gq
                dv = v3[:, n, 0:nk]
                ov = w3[:, n, 0:nk]
                if rev:
                    dv, ov = dv[:, ::-1], ov[:, ::-1]
                ini = inits[n]
                rd = [vb.reg(vlo + ri * 512, vlo + (ri + 1) * 512), coef[1]] + ([ini[1]] if isinstance(ini, tuple) else [])
                S.op("dve", lambda e, ov=ov, dv=dv, coef=coef, ini=ini: e.tensor_tensor_scan(
                    out=ov, data0=coef[0], data1=dv, initial=(ini[0] if isinstance(ini, tuple) else ini), op0=ALU.mult, op1=ALU.add),
                    reads=rd, writes=[wb.reg(wlo + n * 128, wlo + (n + 1) * 128)])

    def newt(n):
        return small(n)
    Eend = {}
    units = [("ctx", ctx_rows, 0, 256, mod["cgs1"], mod["csh1"])]
    for sl in range(3):
        units.append((sl, xb_rows, sl * 2048, 2048, mod["gs1"], mod["sh1"]))
    units.append(("own", own_x, 0, 2048, mod["gs1"], mod["sh1"]))

    def fu_steps(ui):
        (name, src, row0, ntok, gsc, shc) = units[ui]
        NK = ntok // 8
        nkt = max(1, NK // 128)
        nk = min(NK, 128)
        steps = []
        plain = (name != "ctx")
        for kt in range(nkt):
            steps += front.steps(src, row0 + kt * 1024, min(ntok, 1024), gsc, shc, hTu, 0, 1024, plain=plain)
            steps += ua_steps(hTu, 0, 1024, nk, Ubufs[ui % 2], kt * 128, 256, bias=plain)
        return steps
    onesb = sb("onesb", 128, BF16, 2560)
    brow = sb("brow", 512, BF16, 2816)
    shb = sb("shb", KC, BF16, 3840)
    ms("pool", (onesb.ap(0, 128, 0, 1), onesb.reg(0, 128)), 1.0)
    cp("act", V(shb, 0, KC), V(mod["sh1"], 0, KC))
    pbq = c.ps_next()

    def mm_b(e):
        ins = None
        for kc in range(KC):
            ins = e.matmul(pbq.ap(0, 512, 0, 1), lhsT=shb.ap(kc, kc + 1), rhs=wA3[:, kc, :], start=(kc == 0), stop=(kc == KC - 1))
        return ins
    S.op("pe", mm_b, reads=[shb.reg(0, KC), wA.reg(0, KC * 512)], writes=[pbq.reg(0, 512)])
    cp("act", (brow.ap(0, 512, 0, 1), brow.reg(0, 512)), (pbq.ap(0, 512, 0, 1), pbq.reg(0, 512)))
    for st_ in fu_steps(0):
        st_()
    for kc in range(KC):
        ts("dve", (wA3[:, kc, :], wA.reg(kc * 512, (kc + 1) * 512)), (wA3[:, kc, :], wA.reg(kc * 512, (kc + 1) * 512)), V(mod["gs1"], kc, kc + 1), ALU.mult)
    for ui in range(4):
        (name, src, row0, ntok, gsc, shc) = units[ui]
        Ucur = Ubufs[ui % 2]
        pending = fu_steps(ui + 1)
        NK = ntok // 8
        nkt = max(1, NK // 128)
        nk = min(NK, 128)
        wl = {d: newt(32) for d in range(2)}
        Eend[name] = wl
        for d in range(2):
            kts = list(range(nkt)) if d == 0 else list(range(nkt - 1, -1, -1))
            for qq in range(4):
                prevV = None
                for ki, kt in enumerate(kts):
                    j0 = kt * 128 if d == 0 else (NK - 1 - kt * 128)
                    vsl = (ucnt["i"] % 2) * 1024
                    ucnt["i"] += 1
                    e_unit(Ucur, kt * 128, 256, nk, d, qq, j0, d == 1, (Vq, vsl), tq)
                    pos = 0 if d == 1 else nk - 1
                    if prevV is None:
                        inits = [0.0] * 8
                    else:
                        pw3 = Vq.ap(prevV, prevV + 1024).rearrange("p (n k) -> p n k", n=8)
                        inits = [(pw3[:, n, pos:pos + 1], Vq.reg(prevV + n * 128, prevV + (n + 1) * 128)) for n in range(8)]
                    scan_unit((Vq, vsl), nk, d, qq, d == 1, inits, (Vq, vsl))
                    prevV = vsl
                    if pending:
                        pending.pop(0)()
                pw3 = Vq.ap(prevV, prevV + 1024).rearrange("p (r g k) -> p r g k", r=2, g=4)
                w3_ = wl[d].ap(qq * 8, (qq + 1) * 8).rearrange("p (g r) -> p g r", g=4)
                for ri in range(2):
                    cp("act", (w3_[:, :, ri:ri + 1], wl[d].reg(qq * 8, (qq + 1) * 8)),
                       (pw3[:, ri, :, pos:pos + 1], Vq.reg(prevV, prevV + 1024)))
        while pending:
            pending.pop(0)()
    Uown = Ubufs[4 % 2]
    t1, t2 = s5["t12"]
    upows = s5["upows"]

    def half(x, d):
        return (x[0][:, d * G2:(d + 1) * G2], x[1])

    def conj_mul(outr, outi, ar, ai, br, bi, ta, tb):
        tt("dve", ta, ar, br, ALU.mult)
        tt("dve", tb, ai, bi, ALU.mult)
        tt("dve", outr, ta, tb, ALU.add)
        tt("dve", ta, ar, bi, ALU.mult)
        tt("dve", tb, ai, br, ALU.mult)
        tt("dve", outi, ta, tb, ALU.subtract)
    u255r, u255i, u31r, u31i = sv(), sv(), sv(), sv()
    conj_mul(u255r, u255i, upows[0][0], upows[0][1], upows[8][0], upows[8][1], t1, t2)
    conj_mul(u31r, u31i, upows[0][0], upows[0][1], upows[5][0], upows[5][1], t1, t2)
    r256, Ar, Ai = sv(), sv(), sv()
    cp("dve", r256, rho8)
    for _ in range(8):
        tt("dve", r256, r256, r256, ALU.mult)
    tt("dve", Ar, upows[8][0], r256, ALU.mult)
    stt("dve", Ai, upows[8][1], -1.0, r256, ALU.mult, ALU.mult)
    oh = small(4)
    dma("sp", V(oh, 0, 4), din["onehot"][:, :])
    carry = {}
    h1, h2 = small(16), small(16)
    H1, H2 = V(h1, 0, 16), V(h2, 0, 16)
    sbuf_ = [(V(small(16), 0, 16), V(small(16), 0, 16)) for _ in range(2)]
    nbuf_ = [(V(small(16), 0, 16), V(small(16), 0, 16)) for _ in range(2)]
    cr_, ci_ = small(16), small(16)
    CR, CI = V(cr_, 0, 16), V(ci_, 0, 16)
    for d in range(2):
        def send(name, slot):
            w3 = Eend[name][d].ap(0, 32).rearrange("p (g r) -> p g r", g=G2)
            wr_, wi_ = (w3[:, :, 0], Eend[name][d].reg(0, 32)), (w3[:, :, 1], Eend[name][d].reg(0, 32))
            ur_, ui_ = (u31r, u31i) if name == "ctx" else (u255r, u255i)
            conj_mul(sbuf_[slot][0], sbuf_[slot][1], half(ur_, d), half(ui_, d), wr_, wi_, H1, H2)
            return sbuf_[slot]
        cur = send("ctx", 0)
        ms("dve", CR, 0.0)
        ms("dve", CI, 0.0)
        order = [0, 1, 2, 3] if d == 0 else [3, 2, 1, 0]
        for si_, pos_ in enumerate(order):
            stt("dve", CR, cur[0], V(oh, pos_, pos_ + 1), CR, ALU.mult, ALU.add)
            stt("dve", CI, cur[1], V(oh, pos_, pos_ + 1), CI, ALU.mult, ALU.add)
            if si_ == 3:
                break
            slot = pos_ if d == 0 else pos_ - 1
            e_ = send(slot, 1)
            NR, NI = nbuf_[si_ % 2]
            tt("dve", H1, half(Ar, d), cur[0], ALU.mult)
            tt("dve", H2, half(Ai, d), cur[1], ALU.mult)
            tt("dve", NR, H1, H2, ALU.subtract)
            tt("dve", NR, NR, e_[0], ALU.add)
            tt("dve", H1, half(Ar, d), cur[1], ALU.mult)
            tt("dve", H2, half(Ai, d), cur[0], ALU.mult)
            tt("dve", NI, H1, H2, ALU.add)
            tt("dve", NI, NI, e_[1], ALU.add)
            cur = (NR, NI)
        cpk, w0 = small(32), small(32)
        c3 = cpk.ap(0, 32).rearrange("p (g r) -> p g r", g=G2)
        w03 = w0.ap(0, 32).rearrange("p (g r) -> p g r", g=G2)
        cp("dve", (c3[:, :, 0], cpk.reg(0, 32)), CR)
        cp("dve", (c3[:, :, 1], cpk.reg(0, 32)), CI)
        conj_mul((w03[:, :, 0], w0.reg(0, 32)), (w03[:, :, 1], w0.reg(0, 32)), half(upows[0][0], d), half(upows[0][1], d), CR, CI, H1, H2)
        carry[d] = (cpk, w0)
        c.dump("carry_%d" % d, cpk, 0, 32)

    SIN = {0: sb("SINf", 8 * 1024, BF16, 53248), 1: sb("SINb", 8 * 1024, BF16, 94208)}
    Vo2 = sb("Vo2", 2 * 1024, F32, 69632)
    Uu = Uown
    c.dump("Uown", Uu, 0, 32 * 256)
    NK = 256
    for d in range(2):
        kts = [0, 1] if d == 0 else [1, 0]
        cpk, w0 = carry[d]
        for qq in range(4):
            for kt in kts:
                j0 = kt * 128 if d == 0 else (NK - 1 - kt * 128)
                e_unit(Uu, kt * 128, 256, 128, d, qq, j0, d == 1, (Vo2, kt * 1024), tq)
            lastw = small(8)
            for ki, kt in enumerate(kts):
                j0 = kt * 128 if d == 0 else (NK - 1 - kt * 128)
                if ki == 0:
                    w03 = w0.ap(0, 32).rearrange("p (g r) -> p g r", g=G2)
                    inits = [(w03[:, qq * 4 + n % 4, n // 4:n // 4 + 1], w0.reg(0, 32)) for n in range(8)]
                else:
                    inits = [V(lastw, n, n + 1) for n in range(8)]
                scan_unit((Vo2, kt * 1024), 128, d, qq, d == 1, inits, (Vo2, kt * 1024))
                if ki == 0:
                    pos = 0 if d == 1 else 127
                    cp("act", (lastw.ap(0, 8).rearrange("p (n o) -> p n o", o=1), lastw.reg(0, 8)),
                       (Vo2.ap(kt * 1024, (kt + 1) * 1024).rearrange("p (n k) -> p n k", n=8)[:, :, pos:pos + 1], Vo2.reg(kt * 1024, (kt + 1) * 1024)))
                n0 = d * G2 + qq * 4
                if d == 0:
                    tr = tb4[:, 0, n0:n0 + 4, j0:j0 + 128]
                    ti = tb4[:, 1, n0:n0 + 4, j0:j0 + 128]
                else:
                    lo = j0 - 127
                    tr = tb4[:, 0, n0:n0 + 4, lo:lo + 128][:, :, ::-1]
                    ti = tb4[:, 1, n0:n0 + 4, lo:lo + 128][:, :, ::-1]
                TR, TI = (tr, TB.reg(0, 2 * NDG * NJ)), (ti, TB.reg(0, 2 * NDG * NJ))
                wre = (Vo2.ap(kt * 1024, kt * 1024 + 512).rearrange("p (g k) -> p g k", g=4), Vo2.reg(kt * 1024, kt * 1024 + 512))
                wim = (Vo2.ap(kt * 1024 + 512, (kt + 1) * 1024).rearrange("p (g k) -> p g k", g=4), Vo2.reg(kt * 1024 + 512, (kt + 1) * 1024))
                ore = (Vq.ap(0, 512).rearrange("p (g k) -> p g k", g=4), Vq.reg(0, 512))
                oim = (Vq.ap(512, 1024).rearrange("p (g k) -> p g k", g=4), Vq.reg(512, 1024))
                ta = (tq.ap(0, 512).rearrange("p (g k) -> p g k", g=4), tq.reg(0, 512))
                tb_ = (tq.ap(512, 1024).rearrange("p (g k) -> p g k", g=4), tq.reg(512, 1024))
                tt("dve", ore, wre, TR, ALU.mult)
                tt("dve", ta, wim, TI, ALU.mult)
                tt("dve", oim, wim, TR, ALU.mult)
                tt("dve", tb_, wre, TI, ALU.mult)
                tt("dve", ore, ore, ta, ALU.add)
                tt("dve", oim, oim, tb_, ALU.subtract)
                sin = SIN[d]
                base = (kt * 4 + qq) * 1024
                s4 = sin.ap(base, base + 1024).rearrange("p (g r k) -> p g r k", g=4, r=2)
                sreg = sin.reg(base, base + 1024)
                c3 = cpk.ap(0, 32).rearrange("p (g r) -> p g r", g=G2)
                okt = 1 - kt
                nb_ = (okt * 4 + qq) * 1024
                nx4 = sin.ap(nb_, nb_ + 1024).rearrange("p (g r k) -> p g r k", g=4, r=2)
                for ri in range(2):
                    v3 = (Vq.ap(ri * 512, (ri + 1) * 512).rearrange("p (g k) -> p g k", g=4), Vq.reg(ri * 512, (ri + 1) * 512))
                    if d == 0:
                        cp("act", (s4[:, :, ri, 1:128], sreg), (v3[0][:, :, 0:127], v3[1]))
                        if kt == 0:
                            cp("act", (s4[:, :, ri, 0:1], sreg), (c3[:, qq * 4:(qq + 1) * 4, ri:ri + 1], cpk.reg(0, 32)))
                            cp("act", (nx4[:, :, ri, 0:1], sin.reg(nb_, nb_ + 1024)), (v3[0][:, :, 127:128], v3[1]))
                    else:
                        cp("act", (s4[:, :, ri, 0:127], sreg), (v3[0][:, :, 1:128], v3[1]))
                        if kt == 1:
                            cp("act", (s4[:, :, ri, 127:128], sreg), (c3[:, qq * 4:(qq + 1) * 4, ri:ri + 1], cpk.reg(0, 32)))
                            cp("act", (nx4[:, :, ri, 127:128], sin.reg(nb_, nb_ + 1024)), (v3[0][:, :, 0:1], v3[1]))
    c.dump("SINf", SIN[0], 0, 8 * 1024)
    c.dump("SINb", SIN[1], 0, 8 * 1024)

    FP = sb("FP", 128 * 128, BF16, 151552)
    fp4 = FP.ap(0, 128 * 128).rearrange("p (n h x) -> p n h x", h=2, x=128)
    ftb3 = FTb.ap(0, 8192).rearrange("p (n x) -> p n x", x=128)
    hm = s5["hm"]
    for gh in range(2):
        ts("dve", (fp4[:, :, gh, :], FP.reg(0, 128 * 128)), (ftb3, FTb.reg(0, 8192)), V(hm, gh, gh + 1), ALU.mult)
    Ytok = sb("Ytok", 2 * 8 * 512, BF16, 118784)
    yT = sb("yT", 4 * NTOK, BF16, 135168)
    u3 = Uu.ap(0, 32 * 256).rearrange("p (g k) -> p g k", g=32)
    fp3 = FP.ap(0, 128 * 128).rearrange("p (n x) -> p n x", x=128)
    for kt in range(2):
        for g0 in range(0, 32, 4):
            pb = c.ps_next()

            def mmY(e, pb=pb, g0=g0, kt=kt):
                ins = None
                for gq in range(4):
                    g = g0 + gq
                    g2, gh = g // 2, g % 2
                    qq, gl = g2 // 4, g2 % 4
                    o_ = pb.ap(gq * 128, (gq + 1) * 128)
                    ins = e.matmul(o_, lhsT=u3[:, g, kt * 128:(kt + 1) * 128], rhs=MT.ap(g * 128, (g + 1) * 128), start=True, stop=False)
                    for d in range(2):
                        for ri in range(2):
                            lo = (kt * 4 + qq) * 1024 + (gl * 2 + ri) * 128
                            n = ((d * G2 + g2) * 2 + ri) * 2 + gh
                            ins = e.matmul(o_, lhsT=SIN[d].ap(lo, lo + 128), rhs=fp3[:, n, :], start=False, stop=(d == 1 and ri == 1))
                return ins
            S.op("pe", mmY, reads=[Uu.reg(0, 32 * 256), MT.reg(0, 32 * 128), SIN[0].reg(0, 8192), SIN[1].reg(0, 8192), FP.reg(0, 128 * 128)],
                 writes=[pb.reg(0, 512)])
            y4 = Ytok.ap(kt * 4096, (kt + 1) * 4096).rearrange("p (i g x) -> p i g x", i=8, g=32)
            for gq in range(4):
                act((y4[:, :, g0 + gq, :], Ytok.reg(kt * 4096, (kt + 1) * 4096)),
                    (pb.ap(gq * 128, (gq + 1) * 128).rearrange("p (i x) -> p i x", i=8), pb.reg(0, 512)), AF.Gelu_apprx_tanh)
    yT3 = yT.ap(0, 4 * NTOK).rearrange("p (q t) -> p q t", q=4)
    for kt in range(2):
        for q4 in range(4):
            pb = c.ps_next()
            pbb = pb.t[:, :].bitcast(BF16)

            def trY(e, pbb=pbb, kt=kt, q4=q4):
                ins = None
                for i in range(8):
                    lo = kt * 4096 + i * 512 + q4 * 128
                    ins = e.transpose(out=pbb[:, i * 128:(i + 1) * 128], in_=Ytok.ap(lo, lo + 128), identity=c.identb.ap(0, 128))
                return ins
            S.op("pe", trY, reads=[Ytok.reg(0, 2 * 4096), c.identb.reg(0, 128)], writes=[pb.reg(0, 512)])
            cp("dve", (yT3[:, q4, kt * 1024:(kt + 1) * 1024].rearrange("p (k i) -> p k i", i=8), yT.reg(q4 * NTOK + kt * 1024, q4 * NTOK + (kt + 1) * 1024)),
               (pbb[:, 0:1024].rearrange("p (i k) -> p k i", i=8), pb.reg(0, 512)))
    c.dump("yT", yT, 0, 4 * NTOK)
    return yT


def build_nc():
    nc = bass.Bass("TRN2", target_bir_lowering=False)

    def din_(name, shape):
        return nc.dram_tensor(name, list(shape), F32, kind="ExternalInput").ap()
    din = {}
    for name, shape in INPUT_SHAPES.items():
        din[name] = din_(name, shape)
    out = nc.dram_tensor("out", [NTOK, D], F32, kind="ExternalOutput").ap()
    S = Sched()
    with ExitStack() as es:
        c = make_ctx(nc, es, S, DEBUG, DBG_WORDS)
        c.din = din
        sb = c.sb
        c.identf = sb("identf", 128, F32, 0)
        c.onesf = sb("onesf", 128, F32, 512)
        c.identb = sb("identb", 128, BF16, 1024)
        c.dma("sp", V(c.identf, 0, 128), din["ident"][:, :])
        c.dma("pool", V(c.identb, 0, 128), din["ident"][:, :])
        c.ms("dve", V(c.onesf, 0, 128), 1.0)
        c.st = sb("st", 128, F32, 2048)
        epsb = sb("epsb", 1, F32, 1792)
        c.ms("pool", V(epsb, 0, 1), EPS)
        c.epsc = V(epsb, 0, 1)
        if STAGE in ("full", "nos5", "none"):
            x1d = nc.dram_tensor("x1d", [NTOK, D], F32, kind="Internal").ap()
            yT = None
            mod = mod_phase(c)
            if STAGE == "full":
                s5 = s5_precompute(c, din)
            mod["finish"]()
            if STAGE == "full":
                yT = s5_run(c, s5, mod, din["xb"], din["ctx"], None, din["x"])
            if STAGE == "none":
                x1d = din["x"]
            else:
                backend(c, mod, yT, din["x"], x1d, use_s5=(STAGE == "full"))
            ffn_phase(c, mod, x1d, out)
        if STAGE in ("s5pre", "s5ana"):
            if STAGE == "s5ana":
                mod = mod_phase(c)
            s5 = s5_precompute(c, din)
            if STAGE == "s5ana":
                mod["finish"]()
            if STAGE == "s5ana":
                s5_run(c, s5, mod, din["xb"], din["ctx"], None, din["x"])
            yo = sb("yo", D, F32, 4096)
            c.ms("dve", V(yo, 0, D), 0.0)
            for tt_ in range(NT):
                S.op("sp", lambda e, tt_=tt_: e.dma_start(out=out[tt_ * 128:(tt_ + 1) * 128, :], in_=yo.ap(0, D)),
                     reads=[yo.reg(0, D)], writes=[R("dram_out", tt_ * 128, (tt_ + 1) * 128)], dma=True, key=("st", 0))
        S.op("sp", None, reads=[R("dram_out", 0, NTOK), R("dram_dbg", 0, DBG_WORDS)])
        S.analyse()
        sems = {e: es.enter_context(nc.semaphore("sem_" + e)) for e in ("pe", "act", "dve", "pool", "sp")}
        dsems = {k: es.enter_context(nc.semaphore("dsem%d" % i)) for i, k in enumerate(S.dma_keys)}
        print("ops", len(S.ops), "dma sems", len(dsems))
        with nc.Block() as block:
            @block.sync
            def _(e):
                S.emit("sp", e, sems, dsems)

            @block.scalar
            def _(e):
                S.emit("act", e, sems, dsems)

            @block.vector
            def _(e):
                S.emit("dve", e, sems, dsems)

            @block.gpsimd
            def _(e):
                S.emit("pool", e, sems, dsems)

            @block.tensor
            def _(e):
                S.emit("pe", e, sems, dsems)
    return nc


INPUT_SHAPES = {
    "x": [NTOK, D], "ident": [128, 128],
    "s5sm": [128, 96], "s5B": [128, 1024], "s5C": [128, 1024], "hm": [128, 2], "dcol": [128, 32],
    "maskf": [128, 128], "maskb": [128, 128],
    "xb": [3 * NTOK, D], "ctx": [256, D], "cc": [128, KC * 64], "w_mod": [D, 6 * D], "b_mod": [1, 6 * D],
    "onehot": [128, 4], "n1col": [128, KC], "n2col": [128, KC], "final_norm_g": [1, D], "w_in": [D, 3 * D],
    "w_branch_a": [512, D], "w_branch_b": [512, D], "w_glu": [512, 512], "pool_w": [4, 128, 128], "poolB": [4, 128, 128],
    "w_out": [D, D], "invc": [1, 512], "sc2": [128, 8], "w_ffn_in": [D, 2 * FH], "w_ffn_out": [FH, D],
}


def host_layout(inputs):
    f = lambda a: np.ascontiguousarray(np.asarray(a), dtype=np.float32)
    cm = {}
    cm["ident"] = np.eye(128, dtype=np.float32)

    def pm(a):
        a = f(a)
        sh = a.shape
        a = a.reshape(2, 16, 2, 64, *sh[3:])
        a = np.moveaxis(a, (2, 3), (0, 1))
        return np.ascontiguousarray(a.reshape(128, -1))
    are = pm(inputs["s5_a_re"][0])
    aim = pm(inputs["s5_a_im"][0])
    ldt = pm(np.broadcast_to(f(inputs["s5_log_dt"][0])[:, :, None], (2, 32, 64)))
    cm["s5sm"] = np.concatenate([are, aim, ldt], axis=1)
    cm["s5B"] = np.concatenate([pm(inputs["s5_b_re"][0]), pm(inputs["s5_b_im"][0])], axis=1)
    cre = np.swapaxes(f(inputs["s5_c_re"][0]), 2, 3)
    cim = np.swapaxes(f(inputs["s5_c_im"][0]), 2, 3)
    cm["s5C"] = np.concatenate([pm(cre), pm(cim)], axis=1)
    hm = np.zeros((128, 2), np.float32)
    hm[:64, 0] = 1.0
    hm[64:, 1] = 1.0
    cm["hm"] = hm
    dsk = f(inputs["s5_d"][0]).reshape(32, 16)
    cm["dcol"] = np.ascontiguousarray(np.tile(dsk.T[None, :, :], (8, 1, 1)).reshape(128, 32))
    ii = np.arange(128) // 16
    cm["maskf"] = (ii[:, None] <= ii[None, :]).astype(np.float32)
    cm["maskb"] = (ii[:, None] >= ii[None, :]).astype(np.float32)
    cm["w_mod"] = f(inputs["w_mod"][0])
    cm["b_mod"] = f(inputs["b_mod"][0]).reshape(1, 6 * D)
    cm["n1col"] = f(f(inputs["norm1_g"][0]).reshape(KC, 128).T)
    cm["n2col"] = f(f(inputs["norm2_g"][0]).reshape(KC, 128).T)
    cm["final_norm_g"] = f(inputs["final_norm_g"]).reshape(1, D)
    cm["w_in"] = f(inputs["w_in"][0])
    for k in ("w_branch_a", "w_branch_b", "w_glu", "pool_w", "w_out", "w_ffn_in", "w_ffn_out"):
        cm[k] = f(inputs[k][0])
    pos = np.arange(64)
    PB = np.zeros((4, 128, 128), np.float32)
    invc = np.zeros((1, 4, 128), np.float32)
    for jw, w in enumerate((2, 4, 8, 16)):
        lo = np.clip(pos - w // 2, 0, 63)
        hi = np.clip(pos + w - 1 - w // 2, 0, 63) + 1
        blk = ((pos[:, None] >= lo[None, :]) & (pos[:, None] < hi[None, :])).astype(np.float32)
        cnt = (hi - lo).astype(np.float32)
        blk = blk - np.diag(cnt)
        PB[jw, :64, :64] = blk
        PB[jw, 64:, 64:] = blk
        invc[0, jw, :64] = 1.0 / cnt
        invc[0, jw, 64:] = 1.0 / cnt
    cm["poolB"] = PB
    cm["invc"] = invc.reshape(1, 512)
    sc2 = np.zeros((128, 8), np.float32)
    sc2[:, 0:4] = f(inputs["pool_scale"][0]).reshape(4, 128).T
    sc2[:, 4:8] = f(inputs["b_glu"][0]).reshape(4, 128).T
    cm["sc2"] = sc2
    return cm


def kernel(**inputs):
    f = lambda a: np.ascontiguousarray(np.asarray(a), dtype=np.float32)
    x = f(inputs["x"])
    nc = build_nc()
    common = host_layout(inputs)
    in_maps = []
    for cid in range(8):
        b, j = cid // 4, cid % 4
        m = {k: v for k, v in common.items() if k in INPUT_SHAPES}
        m["x"] = np.ascontiguousarray(x[b, j * NTOK:(j + 1) * NTOK, :])
        m["xb"] = np.ascontiguousarray(np.concatenate([x[b, i * NTOK:(i + 1) * NTOK] for i in range(4) if i != j], axis=0))
        m["ctx"] = f(inputs["ctx"][b])
        cc = np.zeros((128, KC, 64), np.float32)
        cc[:, :, 0] = f(inputs["c"])[b].reshape(KC, 128).T
        cc[:, :, 32] = f(inputs["c_ctx"]).reshape(KC, 128).T
        m["cc"] = cc.reshape(128, KC * 64)
        oh = np.zeros((128, 4), np.float32)
        oh[:, j] = 1.0
        m["onehot"] = oh
        in_maps.append(m)
    res = run_bass_kernel_spmd(nc, in_maps, core_ids=list(range(8)))
    if DEBUG:
        global DBG_OUT
        DBG_OUT = [res.results[cid]["dbg"] for cid in range(8)]
    outp = np.empty((2, 8192, D), np.float32)
    for cid in range(8):
        b, j = cid // 4, cid % 4
        outp[b, j * NTOK:(j + 1) * NTOK, :] = res.results[cid]["out"]
    return outp
```

```python
import numpy as np
from contextlib import ExitStack
import concourse.bass as bass
import concourse.mybir as mybir
from concourse.bass_utils import run_bass_kernel_spmd

F32 = mybir.dt.float32
BF16 = mybir.dt.bfloat16
ALU = mybir.AluOpType
AF = mybir.ActivationFunctionType

D = 1024
KC = D // 128
NTOK = 2048
NT = NTOK // 128
FH = 2816
HT = FH // 128
EPS = 1e-6
MIXER = "full"
DEBUG = False
DBG_WORDS = 131072
STAGE = "full"
DBG_LAYOUT = {}


class Sched:
    def __init__(self):
        self.ops = []

    def op(self, eng, fn, reads=(), writes=(), dma=False, key=None):
        if dma:
            key = (eng, writes[0] if key is None else key)
        self.ops.append(dict(eng=eng, fn=fn, reads=list(reads), writes=list(writes),
                             dma=dma, key=key, deps=set(), sig=False))
        return len(self.ops) - 1

    @staticmethod
    def _ov(a, b):
        return a[0] == b[0] and a[1] < b[2] and b[1] < a[2]

    @staticmethod
    def _cov(a, b):
        return a[0] == b[0] and a[1] <= b[1] and b[2] <= a[2]

    def analyse(self):
        wr, rd = {}, {}
        for i, o in enumerate(self.ops):
            for r in o["reads"]:
                for (reg, j) in wr.get(r[0], []):
                    if self._ov(reg, r):
                        o["deps"].add(j)
            for w in o["writes"]:
                for (reg, j) in wr.get(w[0], []):
                    if self._ov(reg, w):
                        o["deps"].add(j)
                for (reg, j) in rd.get(w[0], []):
                    if self._ov(reg, w):
                        o["deps"].add(j)
            o["deps"].discard(i)
            for w in o["writes"]:
                wr[w[0]] = [(reg, j) for (reg, j) in wr.get(w[0], []) if not self._cov(w, reg)]
                rd[w[0]] = [(reg, j) for (reg, j) in rd.get(w[0], []) if not self._cov(w, reg)]
                wr[w[0]].append((w, i))
            for r in o["reads"]:
                rd.setdefault(r[0], []).append((r, i))
        for o in self.ops:
            for j in o["deps"]:
                self.ops[j]["sig"] = True
        cnt, dcnt = {}, {}
        self.dma_keys = []
        for o in self.ops:
            if o["dma"]:
                k = o["key"]
                if k not in dcnt:
                    dcnt[k] = 0
                    self.dma_keys.append(k)
                dcnt[k] += 1
                o["seq"] = dcnt[k]
            elif o["fn"] is not None and o["sig"]:
                cnt[o["eng"]] = cnt.get(o["eng"], 0) + 1
                o["seq"] = cnt[o["eng"]]

    def emit(self, eng, h, sems, dsems):
        waited = {}
        for o in self.ops:
            if o["eng"] != eng:
                continue
            need = {}
            for j in sorted(o["deps"]):
                p = self.ops[j]
                if p["dma"]:
                    kk = ("d", p["key"])
                    need[kk] = max(need.get(kk, 0), 16 * p["seq"])
                elif p["fn"] is not None:
                    kk = ("e", p["eng"])
                    need[kk] = max(need.get(kk, 0), p["seq"])
            for kk, v in need.items():
                if waited.get(kk, 0) >= v:
                    continue
                waited[kk] = v
                h.wait_ge(dsems[kk[1]] if kk[0] == "d" else sems[kk[1]], v)
            if o["fn"] is None:
                continue
            ins = o["fn"](h)
            if o["dma"]:
                ins.then_inc(dsems[o["key"]], 16)
            elif o["sig"]:
                ins.then_inc(sems[o["eng"]], 1)


def R(space, lo, hi):
    return (space, int(lo), int(hi))


class Buf:
    def __init__(self, space, t, n, dt=None, boff=0, esz=4):
        self.space, self.n, self.boff, self.esz = space, n, boff, esz
        self.t = t if dt is None else t[:, boff // 4:(boff + n * esz) // 4].bitcast(dt)

    def ap(self, lo, hi, p0=0, p1=128):
        return self.t[p0:p1, lo:hi]

    def reg(self, lo, hi):
        return R(self.space, self.boff + lo * self.esz, self.boff + hi * self.esz)


import math

G2 = 16
NDG = 32
PI = math.pi


class Ctx:
    pass


def make_ctx(nc, es, S, debug, dbg_words):
    c = Ctx()
    c.nc, c.S = nc, S
    ARENA_BYTES = 209920
    c.ARENA_BYTES = ARENA_BYTES
    arena = es.enter_context(nc.sbuf_tensor("arena", [128, ARENA_BYTES // 4], F32))
    c.arena = arena

    def sb(name, n, dt, off):
        esz = 4 if dt == F32 else 2
        assert off % 4 == 0 and (n * esz) % 4 == 0 and off + n * esz <= ARENA_BYTES, (name, off, n)
        return Buf("A", arena, n, dt, off, esz)
    c.sb = sb
    c.psb = [Buf("ps%d" % i, es.enter_context(nc.psum_tensor("ps%d" % i, [128, 512], F32))[:, :], 512) for i in range(8)]
    pst = {"i": 0}

    def ps_next(grp=None):
        if grp is None:
            b = c.psb[pst["i"] % 8]
            pst["i"] += 1
            return b
        k = pst.setdefault(grp, 0)
        pst[grp] = k + 1
        return c.psb[(0 if grp == "a" else 4) + k % 4]
    c.ps_next = ps_next

    c.dbg = nc.dram_tensor("dbg", [128, dbg_words], F32, kind="ExternalOutput").ap() if debug else None
    dst = {"off": 0}

    def dump(name, buf, lo, hi):
        if not debug:
            return
        b0, b1 = buf.boff + lo * buf.esz, buf.boff + hi * buf.esz
        nw = (b1 - b0) // 4
        o = dst["off"]
        dst["off"] += nw
        assert dst["off"] <= dbg_words, name
        DBG_LAYOUT[name] = (o, nw, buf.esz)
        S.op("sp", lambda e: e.dma_start(out=c.dbg[:, o:o + nw], in_=arena[:, b0 // 4:b1 // 4]),
             reads=[R("A", b0, b1)], writes=[R("dram_dbg", o, o + nw)], dma=True, key=("dbg", name))
    c.dump = dump

    def _rd(*xs):
        return [x[1] for x in xs if isinstance(x, tuple)]

    def _a(x):
        return x[0] if isinstance(x, tuple) else x

    def tt(eng, out, in0, in1, op):
        S.op(eng, lambda e: e.tensor_tensor(out=out[0], in0=in0[0], in1=in1[0], op=op), reads=_rd(in0, in1), writes=[out[1]])

    def ts(eng, out, in0, s1, op0, s2=None, op1=None):
        if op1 is None:
            S.op(eng, lambda e: e.tensor_scalar(out=out[0], in0=in0[0], scalar1=_a(s1), scalar2=None, op0=op0), reads=_rd(in0, s1), writes=[out[1]])
        else:
            S.op(eng, lambda e: e.tensor_scalar(out=out[0], in0=in0[0], scalar1=_a(s1), scalar2=_a(s2), op0=op0, op1=op1), reads=_rd(in0, s1, s2), writes=[out[1]])

    def stt(eng, out, in0, sc, in1, op0, op1):
        S.op(eng, lambda e: e.scalar_tensor_tensor(out=out[0], in0=in0[0], scalar=_a(sc), in1=in1[0], op0=op0, op1=op1), reads=_rd(in0, sc, in1), writes=[out[1]])

    def act(out, in_, func, scale=None, bias=None, accum=None):
        kw = {}
        if scale is not None:
            kw["scale"] = _a(scale)
        if bias is not None:
            kw["bias"] = _a(bias)
        if accum is not None:
            kw["accum_out"] = accum[0]
        wr = [out[1]] + ([accum[1]] if accum is not None else [])
        S.op("act", lambda e: e.activation(out=out[0], in_=in_[0], func=func, **kw), reads=_rd(in_, scale, bias), writes=wr)

    def cp(eng, out, in_):
        if eng == "act":
            act(out, in_, AF.Copy)
        else:
            S.op(eng, lambda e: e.tensor_copy(out=out[0], in_=in_[0]), reads=_rd(in_), writes=[out[1]])

    def ms(eng, out, val):
        S.op(eng, lambda e: e.memset(out[0], val), writes=[out[1]])

    def dma(eng, out, in_ap, in_reg=None):
        S.op(eng, lambda e: e.dma_start(out=out[0], in_=in_ap), reads=([in_reg] if in_reg else []), writes=[out[1]], dma=True)
    c.tt, c.ts, c.stt, c.act, c.cp, c.ms, c.dma = tt, ts, stt, act, cp, ms, dma
    return c


def V(buf, lo, hi, pat=None, **kw):
    ap = buf.ap(lo, hi)
    if pat:
        ap = ap.rearrange(pat, **kw)
    return (ap, buf.reg(lo, hi))


def s5_precompute(c, din):
    sb, tt, ts, stt, act, cp, ms, dma = c.sb, c.tt, c.ts, c.stt, c.act, c.cp, c.ms, c.dma
    S = c.S
    o = {}
    BASE = 20480
    ET = sb("ET", 8192, F32, 20480)
    FT = sb("FT", 8192, F32, 53248)
    YP = sb("YP", 8192, F32, 86016)
    EP = sb("EP", 128 * 128, BF16, 118784)
    X1 = 151552
    MT = sb("MT", 32 * 128, BF16, 184320)
    SBI = sb("SBI", 1024, F32, 192512)
    SCI = sb("SCI", 1024, F32, 196608)
    o.update(EP=EP, MT=MT)
    sm_off = {"o": 200704}

    def small(n=NDG):
        b = sb("sm", n, F32, sm_off["o"])
        sm_off["o"] += n * 4
        return b
    tmp4 = [sb("tmp%d" % i, 512, F32, 16384 + i * 2048) for i in range(2)]

    s5sm = small(96)
    dma("sp", V(s5sm, 0, 96), din["s5sm"][:, :])
    dma("sp", V(SBI, 0, 1024), din["s5B"][:, :])
    dma("sp", V(SCI, 0, 1024), din["s5C"][:, :])
    are, aim, ldt = V(s5sm, 0, 32), V(s5sm, 32, 64), V(s5sm, 64, 96)
    hm = small(2)
    dma("sp", V(hm, 0, 2), din["hm"][:, :])
    dcol = small(32)
    dma("sp", V(dcol, 0, 32), din["dcol"][:, :])

    def sv():
        b = small()
        return V(b, 0, NDG)
    dt_, al, th, ea = sv(), sv(), sv(), sv()
    xq = sv()
    ts("dve", xq, ldt, 1.0 / 16.0, ALU.mult)
    fct = [1.0]
    for k_ in range(1, 11):
        fct.append(fct[-1] * k_)
    ms("dve", dt_, 1.0 / fct[10])
    for k_ in range(9, -1, -1):
        tt("dve", dt_, dt_, xq, ALU.mult)
        ts("dve", dt_, dt_, 1.0 / fct[k_], ALU.add)
    for _ in range(4):
        tt("dve", dt_, dt_, dt_, ALU.mult)
    tt("dve", al, are, dt_, ALU.mult)
    tt("dve", th, aim, dt_, ALU.mult)
    ms("dve", ea, 1.0 / 720.0)
    for cf in (1.0 / 120.0, 1.0 / 24.0, 1.0 / 6.0, 0.5, 1.0, 1.0):
        tt("dve", ea, ea, al, ALU.mult)
        ts("dve", ea, ea, cf, ALU.add)
    sn, cs, wk, mk = sv(), sv(), sv(), sv()
    w2 = sv()
    ts("dve", wk, th, 1.0 / 16.0, ALU.mult)
    tt("dve", w2, wk, wk, ALU.mult)
    fact = [1.0]
    for k_ in range(1, 17):
        fact.append(fact[-1] * k_)
    ms("dve", sn, -1.0 / fact[15])
    for k_ in range(6, -1, -1):
        tt("dve", sn, sn, w2, ALU.mult)
        ts("dve", sn, sn, ((-1.0) ** k_) / fact[2 * k_ + 1], ALU.add)
    tt("dve", sn, sn, wk, ALU.mult)
    ms("dve", cs, 1.0 / fact[16])
    for k_ in range(7, -1, -1):
        tt("dve", cs, cs, w2, ALU.mult)
        ts("dve", cs, cs, ((-1.0) ** k_) / fact[2 * k_], ALU.add)
    for _ in range(4):
        tt("dve", mk, sn, cs, ALU.mult)
        tt("dve", wk, cs, cs, ALU.mult)
        tt("dve", w2, sn, sn, ALU.mult)
        tt("dve", cs, wk, w2, ALU.subtract)
        ts("dve", sn, mk, 2.0, ALU.mult)
    PWr, PWi = sb("PWr", 512, F32, X1), sb("PWi", 512, F32, X1 + 2048)
    PBr, PBi = sb("PBr", 512, F32, X1 + 4096), sb("PBi", 512, F32, X1 + 6144)

    def pw(buf, k):
        return (buf.ap(0, NDG * 16).rearrange("p (n k) -> p n k", k=16)[:, :, k + 7], buf.reg(0, NDG * 16))
    t1, t2 = sv(), sv()

    def cmul(outr, outi, ar, ai, br, bi, eng="dve", ta=None, tb=None):
        ta = ta or t1
        tb = tb or t2
        tt(eng, ta, ar, br, ALU.mult)
        tt(eng, tb, ai, bi, ALU.mult)
        tt(eng, outr, ta, tb, ALU.subtract)
        tt(eng, ta, ar, bi, ALU.mult)
        tt(eng, tb, ai, br, ALU.mult)
        tt(eng, outi, ta, tb, ALU.add)
    ms("dve", pw(PWr, 0), 1.0)
    ms("dve", pw(PWi, 0), 0.0)
    tt("dve", pw(PWr, 1), ea, cs, ALU.mult)
    tt("dve", pw(PWi, 1), ea, sn, ALU.mult)
    for k in range(2, 9):
        cmul(pw(PWr, k), pw(PWi, k), pw(PWr, k - 1), pw(PWi, k - 1), pw(PWr, 1), pw(PWi, 1))
    e2, e21 = sv(), sv()
    tt("dve", e21, ea, ea, ALU.mult)
    S.op("dve", lambda e: e.reciprocal(out=e21[0], in_=e21[0]), reads=[e21[1]], writes=[e21[1]])
    cp("dve", e2, e21)
    for k in range(1, 8):
        if k > 1:
            tt("dve", e2, e2, e21, ALU.mult)
        tt("dve", pw(PWr, -k), pw(PWr, k), e2, ALU.mult)
        stt("dve", pw(PWi, -k), pw(PWi, k), -1.0, e2, ALU.mult, ALU.mult)
    nr, den, cr, ci = sv(), sv(), sv(), sv()
    ts("dve", nr, pw(PWr, 1), -1.0, ALU.add)
    tt("dve", den, are, are, ALU.mult)
    tt("dve", t1, aim, aim, ALU.mult)
    tt("dve", den, den, t1, ALU.add)
    S.op("dve", lambda e: e.reciprocal(out=den[0], in_=den[0]), reads=[den[1]], writes=[den[1]])
    tt("dve", t1, nr, are, ALU.mult)
    tt("dve", t2, pw(PWi, 1), aim, ALU.mult)
    tt("dve", cr, t1, t2, ALU.add)
    tt("dve", cr, cr, den, ALU.mult)
    tt("dve", t1, pw(PWi, 1), are, ALU.mult)
    tt("dve", t2, nr, aim, ALU.mult)
    tt("dve", ci, t1, t2, ALU.subtract)
    tt("dve", ci, ci, den, ALU.mult)
    pwr3 = (PWr.ap(0, 512).rearrange("p (n k) -> p n k", k=16), PWr.reg(0, 512))
    pwi3 = (PWi.ap(0, 512).rearrange("p (n k) -> p n k", k=16), PWi.reg(0, 512))
    pbr3 = (PBr.ap(0, 512).rearrange("p (n k) -> p n k", k=16), PBr.reg(0, 512))
    pbi3 = (PBi.ap(0, 512).rearrange("p (n k) -> p n k", k=16), PBi.reg(0, 512))
    crb = (cr[0].unsqueeze(2).to_broadcast([128, NDG, 16]), cr[1])
    cib = (ci[0].unsqueeze(2).to_broadcast([128, NDG, 16]), ci[1])
    ta3 = (tmp4[0].ap(0, 512).rearrange("p (n k) -> p n k", k=16), tmp4[0].reg(0, 512))
    tb3 = (tmp4[1].ap(0, 512).rearrange("p (n k) -> p n k", k=16), tmp4[1].reg(0, 512))
    cmul(pbr3, pbi3, pwr3, pwi3, crb, cib, ta=ta3, tb=tb3)

    PWrn = sb("PWrn", 512, F32, X1 + 8192)
    ts("dve", V(PWrn, 0, 512), V(PWr, 0, 512), -1.0, ALU.mult)
    def tab(buf, d, ri, i):
        base = d * 4096 + ri * 128 + i * 16
        full = buf.ap(d * 4096, (d + 1) * 4096).rearrange("p (g r x) -> p g r x", g=G2, r=2)
        return (full[:, :, ri, i * 16:(i + 1) * 16], buf.reg(d * 4096, (d + 1) * 4096))

    def pslot(buf, d, slot):
        v3 = buf.ap(0, 512).rearrange("p (n k) -> p n k", k=16)
        return (v3[:, d * G2:(d + 1) * G2, slot:slot + 1].to_broadcast([128, G2, 16]), buf.reg(0, 512))

    def bc(buf, part, d):
        lo = part * 512 + d * 256
        return (buf.ap(lo, lo + 256).rearrange("p (g x) -> p g x", g=G2), buf.reg(lo, lo + 256))
    tq = [(tmp4[j // 2].ap((j % 2) * 256, (j % 2) * 256 + 256).rearrange("p (g x) -> p g x", g=G2),
           tmp4[j // 2].reg((j % 2) * 256, (j % 2) * 256 + 256)) for j in range(4)]
    for d in range(2):
        for i in range(8):
            sE = (14 - i) if d == 0 else (7 + i)
            sF = (8 + i) if d == 0 else (15 - i)
            sY = i if d == 0 else (7 - i)
            Br, Bi = bc(SBI, 0, d), bc(SBI, 1, d)
            Cr, Ci = bc(SCI, 0, d), bc(SCI, 1, d)
            pr, pi_ = pslot(PBr, d, sE), pslot(PBi, d, sE)
            tt("dve", tq[0], pr, Br, ALU.mult)
            tt("dve", tq[1], pi_, Bi, ALU.mult)
            tt("dve", tab(ET, d, 0, i), tq[0], tq[1], ALU.subtract)
            tt("dve", tq[0], pr, Bi, ALU.mult)
            tt("dve", tq[1], pi_, Br, ALU.mult)
            tt("dve", tab(ET, d, 1, i), tq[0], tq[1], ALU.add)
            for (TB, sl, eng, qa, qb) in ((FT, sF, "dve", tq[0], tq[1]), (YP, sY, "dve", tq[2], tq[3])):
                pr, pi_, prn = pslot(PWr, d, sl), pslot(PWi, d, sl), pslot(PWrn, d, sl)
                tt(eng, qa, pr, Cr, ALU.mult)
                tt(eng, qb, pi_, Ci, ALU.mult)
                tt(eng, tab(TB, d, 0, i), qa, qb, ALU.subtract)
                tt(eng, qa, prn, Ci, ALU.mult)
                tt(eng, qb, pi_, Cr, ALU.mult)
                tt(eng, tab(TB, d, 1, i), qa, qb, ALU.subtract)
    o_ea = ea
    p8r, p8i = sv(), sv()
    cp("dve", p8r, pw(PWr, 8))
    cp("dve", p8i, pw(PWi, 8))
    c.dump("ET", ET, 0, 8192)
    c.dump("FT", FT, 0, 8192)
    c.dump("YP", YP, 0, 8192)

    identf = c.identf
    ms("pool", V(EP, 0, 128 * 128), 0.0)
    ep4 = EP.ap(0, 128 * 128).rearrange("p (n h x) -> p n h x", h=2, x=128)
    for n0 in range(0, 64, 4):
        pb = c.ps_next()

        def trE(e, pb=pb, n0=n0):
            ins = None
            for q in range(4):
                ins = e.transpose(out=pb.ap(q * 128, (q + 1) * 128), in_=ET.ap((n0 + q) * 128, (n0 + q + 1) * 128), identity=identf.ap(0, 128))
            return ins
        S.op("pe", trE, reads=[ET.reg(n0 * 128, (n0 + 4) * 128), identf.reg(0, 128)], writes=[pb.reg(0, 512)])
        p3 = pb.ap(0, 512).rearrange("p (q x) -> p q x", q=4)
        for gh in range(2):
            eng = "act"
            c.cp(eng, (ep4[:, n0:n0 + 4, gh, gh * 64:(gh + 1) * 64], EP.reg(n0 * 256, (n0 + 4) * 256)),
                 (p3[:, :, gh * 64:(gh + 1) * 64], pb.reg(0, 512)))
    c.dump("EP", EP, 0, 128 * 128)

    mkf, mkb = sb("mkf", 128, F32, 16384), sb("mkb", 128, F32, 16896)
    dma("sp", V(mkf, 0, 128), c.din["maskf"][:, :])
    dma("sp", V(mkb, 0, 128), c.din["maskb"][:, :])
    YPP = sb("YPP", 8192, F32, X1)
    m1 = sb("m1", 128, F32, 17408)
    m2 = sb("m2", 128, F32, 17920)
    for gh in range(2):
        act(V(YPP, 0, 8192), V(YP, 0, 8192), AF.Copy, scale=V(hm, gh, gh + 1))
        for g2 in range(G2):
            g = 2 * g2 + gh
            pb = c.ps_next()

            def mmM(e, pb=pb, g2=g2):
                ins = None
                for d in range(2):
                    for ri in range(2):
                        lo = ((d * G2 + g2) * 2 + ri) * 128
                        ins = e.matmul(pb.ap(d * 128, (d + 1) * 128), lhsT=ET.ap(lo, lo + 128), rhs=YPP.ap(lo, lo + 128),
                                       start=(ri == 0), stop=(ri == 1))
                return ins
            S.op("pe", mmM, reads=[ET.reg(0, 8192), YPP.reg(0, 8192)], writes=[pb.reg(0, 256)])
            tt("dve", V(m1, 0, 128), (pb.ap(0, 128), pb.reg(0, 128)), V(mkf, 0, 128), ALU.mult)
            tt("dve", V(m2, 0, 128), (pb.ap(128, 256), pb.reg(128, 256)), V(mkb, 0, 128), ALU.mult)
            tt("dve", V(m1, 0, 128), V(m1, 0, 128), V(m2, 0, 128), ALU.add)
            stt("dve", V(MT, g * 128, (g + 1) * 128), V(identf, 0, 128), V(dcol, g, g + 1), V(m1, 0, 128), ALU.mult, ALU.add)
    c.dump("MT", MT, 0, 32 * 128)

    FTb = sb("FTb", 64 * 128, BF16, 20480)
    act(V(FTb, 0, 8192), V(FT, 0, 8192), AF.Copy)
    rho8, rinv, ur, ui = sv(), sv(), sv(), sv()
    tt("dve", rho8, o_ea, o_ea, ALU.mult)
    tt("dve", rho8, rho8, rho8, ALU.mult)
    tt("dve", rho8, rho8, rho8, ALU.mult)
    S.op("dve", lambda e: e.reciprocal(out=rinv[0], in_=rho8[0]), reads=[rho8[1]], writes=[rinv[1]])
    tt("dve", ur, p8r, rinv, ALU.mult)
    stt("dve", ui, p8i, -1.0, rinv, ALU.mult, ALU.mult)
    nq1, nq2 = sv(), sv()

    def unit(xr, xi):
        tt("dve", nq1, xr, xr, ALU.mult)
        tt("dve", nq2, xi, xi, ALU.mult)
        tt("dve", nq1, nq1, nq2, ALU.add)
        ts("dve", nq1, nq1, -0.5, ALU.mult, 1.5, ALU.add)
        tt("dve", xr, xr, nq1, ALU.mult)
        tt("dve", xi, xi, nq1, ALU.mult)
    unit(ur, ui)
    NJ = 256
    TRf = sb("TRf", NDG * NJ, F32, 53248)
    TIf = sb("TIf", NDG * NJ, F32, 53248 + NDG * NJ * 4)
    tr3 = TRf.ap(0, NDG * NJ).rearrange("p (n j) -> p n j", j=NJ)
    ti3 = TIf.ap(0, NDG * NJ).rearrange("p (n j) -> p n j", j=NJ)
    rR, rI = TRf.reg(0, NDG * NJ), TIf.reg(0, NDG * NJ)
    ms("dve", (tr3[:, :, 0:1], rR), 1.0)
    ms("dve", (ti3[:, :, 0:1], rI), 0.0)
    upr, upi = ur, ui
    upows = [(ur, ui)]
    ta_b = sb("tdA", NDG * 128, F32, X1)
    tb_b = sb("tdB", NDG * 128, F32, X1 + NDG * 128 * 4)
    for s_ in range(8):
        b = 1 << s_
        ta = (ta_b.ap(0, NDG * b).rearrange("p (n j) -> p n j", j=b), ta_b.reg(0, NDG * b))
        tb = (tb_b.ap(0, NDG * b).rearrange("p (n j) -> p n j", j=b), tb_b.reg(0, NDG * b))
        ubr = (upr[0].unsqueeze(2).to_broadcast([128, NDG, b]), upr[1])
        ubi = (upi[0].unsqueeze(2).to_broadcast([128, NDG, b]), upi[1])
        cmul((tr3[:, :, b:2 * b], rR), (ti3[:, :, b:2 * b], rI), (tr3[:, :, 0:b], rR), (ti3[:, :, 0:b], rI), ubr, ubi, ta=ta, tb=tb)
        nr_, ni_ = sv(), sv()
        cmul(nr_, ni_, upr, upi, upr, upi)
        unit(nr_, ni_)
        upr, upi = nr_, ni_
        upows.append((nr_, ni_))
    TB = sb("TB", 2 * NDG * NJ, BF16, X1)
    act(V(TB, 0, NDG * NJ), V(TRf, 0, NDG * NJ), AF.Copy)
    act(V(TB, NDG * NJ, 2 * NDG * NJ), V(TIf, 0, NDG * NJ), AF.Copy)
    o.update(FTb=FTb, TB=TB, rho8=rho8, u256=(upr, upi), u1=(ur, ui), al=al, ea=ea, upows=upows, hm=hm, small=small, sv=sv, cmul=cmul, t12=(t1, t2))
    c.dump("TRf", TRf, 0, NDG * NJ)
    c.dump("TIf", TIf, 0, NDG * NJ)
    return o


def mod_phase(c):
    S, sb, tt, ts, stt, act, cp, ms, dma = c.S, c.sb, c.tt, c.ts, c.stt, c.act, c.cp, c.ms, c.dma
    din = c.din
    ccs = sb("ccs", KC * 64, F32, 98304)
    ccb = sb("ccb", KC * 64, BF16, 100352)
    modrow = sb("modrow", 6 * D, F32, 53248)
    NWM = 2
    wmb = sb("wmb", NWM * KC * 512, BF16, 77824)
    dma("sp", V(ccs, 0, KC * 64), din["cc"][:, :])
    act(V(ccb, 0, KC * 64), V(ccs, 0, KC * 64), AF.Silu)
    wmod_v = din["w_mod"].rearrange("(k p) n -> p k n", p=128)
    bmrow = sb("bmrow", 6 * D, BF16, 102400)
    osel = sb("osel", 64, BF16, 101376)
    dma("pool", (bmrow.ap(0, 6 * D, 0, 1), bmrow.reg(0, 6 * D)), din["b_mod"][0:1, :])
    ms("pool", (osel.ap(0, 64, 0, 1), osel.reg(0, 64)), 0.0)
    ms("pool", (osel.ap(0, 1, 0, 1), osel.reg(0, 64)), 1.0)
    ms("pool", (osel.ap(32, 33, 0, 1), osel.reg(0, 64)), 1.0)
    for nb in range(12):
        sl = nb % NWM
        wlo, whi = sl * KC * 512, (sl + 1) * KC * 512
        wv = wmb.ap(wlo, whi).rearrange("p (k n) -> p k n", k=KC)
        dma("pool", (wv, wmb.reg(wlo, whi)), wmod_v[:, :, nb * 512:(nb + 1) * 512])
        pb = c.ps_next()

        def mm_mod(e, wv=wv, pb=pb, nb=nb):
            ins = None
            for kc in range(KC):
                ins = e.matmul(pb.ap(0, 512, 0, 64), lhsT=ccb.ap(kc * 64, (kc + 1) * 64), rhs=wv[:, kc, :], start=(kc == 0), stop=False)
            ins = e.matmul(pb.ap(0, 512, 0, 64), lhsT=osel.ap(0, 64, 0, 1), rhs=bmrow.ap(nb * 512, (nb + 1) * 512, 0, 1), start=False, stop=True)
            return ins
        S.op("pe", mm_mod, reads=[ccb.reg(0, KC * 64), wmb.reg(wlo, whi), osel.reg(0, 64), bmrow.reg(0, 6 * D)], writes=[pb.reg(0, 512)])
        act((modrow.ap(nb * 512, (nb + 1) * 512, 0, 64), modrow.reg(nb * 512, (nb + 1) * 512)), (pb.ap(0, 512, 0, 64), pb.reg(0, 512)), AF.Copy)
    modcol = sb("modcol", 4 * KC * 2, F32, 1536)
    for qi, q in enumerate((0, 1, 3, 4)):
        pb = c.ps_next()

        def tr_mod(e, pb=pb, q=q):
            ins = None
            for kc in range(KC):
                ins = e.transpose(out=pb.ap(kc * 64, (kc + 1) * 64), in_=modrow.ap(q * D + kc * 128, q * D + (kc + 1) * 128, 0, 64),
                                  identity=c.identf.ap(0, 64, 0, 64))
            return ins
        S.op("pe", tr_mod, reads=[modrow.reg(q * D, (q + 1) * D), c.identf.reg(0, 128)], writes=[pb.reg(0, 512)])
        act((modcol.ap(qi * KC * 2, (qi + 1) * KC * 2).rearrange("p (k c) -> p k c", k=KC), modcol.reg(qi * KC * 2, (qi + 1) * KC * 2)),
            (pb.ap(0, 512).rearrange("p (k c) -> p k c", k=KC)[:, :, 0:33:32], pb.reg(0, 512)), AF.Copy)
    gbc = sb("gbc", 2 * D, F32, 8192)
    for gi, q in enumerate((2, 5)):
        for hf in range(2):
            pb = c.ps_next()
            lo = q * D + hf * 512
            S.op("pe", lambda e, pb=pb, lo=lo: e.matmul(pb.ap(0, 512), lhsT=c.onesf.ap(0, 128, 0, 1), rhs=modrow.ap(lo, lo + 512, 0, 1), start=True, stop=True),
                 reads=[c.onesf.reg(0, 128), modrow.reg(lo, lo + 512)], writes=[pb.reg(0, 512)])
            glo = gi * D + hf * 512
            act(V(gbc, glo, glo + 512), (pb.ap(0, 512), pb.reg(0, 512)), AF.Copy)
    cols = sb("cols", 8 * KC, F32, 1280)
    n1c, n2c = V(cols, 0, KC), V(cols, KC, 2 * KC)
    dma("sp", n1c, din["n1col"][:, :])
    dma("sp", n2c, din["n2col"][:, :])
    mc = modcol.ap(0, 4 * KC * 2).rearrange("p (q k c) -> p q k c", q=4, k=KC)
    mr = modcol.reg(0, 4 * KC * 2)
    m = {"gbc": gbc}
    names = ["gs1", "sh1", "gs2", "sh2", "cgs1", "csh1"]
    for i, nm in enumerate(names):
        m[nm] = Buf("A", c.arena, KC, F32, 1280 + (2 + i) * KC * 4, 4)
    def finish():
        stt("dve", V(m["gs1"], 0, KC), (mc[:, 1, :, 0], mr), 1.0, n1c, ALU.add, ALU.mult)
        cp("dve", V(m["sh1"], 0, KC), (mc[:, 0, :, 0], mr))
        stt("dve", V(m["gs2"], 0, KC), (mc[:, 3, :, 0], mr), 1.0, n2c, ALU.add, ALU.mult)
        cp("dve", V(m["sh2"], 0, KC), (mc[:, 2, :, 0], mr))
        stt("dve", V(m["cgs1"], 0, KC), (mc[:, 1, :, 1], mr), 1.0, n1c, ALU.add, ALU.mult)
        cp("dve", V(m["csh1"], 0, KC), (mc[:, 0, :, 1], mr))
    m["finish"] = finish
    gf = sb("gf", D, F32, 4096)
    dma("sp", V(gf, 0, D), din["final_norm_g"].partition_broadcast(128).rearrange("p o n -> p (o n)"))
    m["gf"] = gf
    return m


def make_front(c, xs, xnb, junk, evac="dve", psg=None):
    S, ts, act = c.S, c.ts, c.act
    stc = {"i": 0}

    def front(src, row0, ntok, gsc, shc, hdst, hoff, hlen):
        for st_ in front_steps(src, row0, ntok, gsc, shc, hdst, hoff, hlen):
            st_()

    def front_steps(src, row0, ntok, gsc, shc, hdst, hoff, hlen, plain=False):
        hv = hdst.ap(0, KC * hlen).rearrange("p (k t) -> p k t", k=KC)
        ntile = ntok // 128
        steps = []
        for t0 in range(0, ntile, 2):
            grp = list(range(t0, min(t0 + 2, ntile)))
            info = {}
            steps.append(lambda grp=grp, info=info: pairA(src, row0, hdst, hoff, hlen, hv, grp, info))
            steps.append(lambda grp=grp, info=info: pairB(gsc, shc, hdst, hoff, hlen, hv, grp, info, plain))
        return steps

    def pairA(src, row0, hdst, hoff, hlen, hv, grp, info):
        if True:
            for t in grp:
                sl = t % 2
                sbase = (stc["i"] % 32) * 4
                stc["i"] += 1
                xa = V(xs, sl * D, (sl + 1) * D)
                c.dma("sp", xa, src[row0 + t * 128:row0 + (t + 1) * 128, :])
                info[t] = (sl, xa, V(c.st, sbase, sbase + 1), V(c.st, sbase + 1, sbase + 2))
            for t in grp:
                sl, xa, ss, s2 = info[t]
                act((hv[:, :, hoff + t * 128:hoff + (t + 1) * 128], hdst.reg(0, KC * hlen)),
                    (xa[0].rearrange("p (k x) -> p k x", k=KC), xa[1]), AF.Square, accum=ss)
            for t in grp:
                sl, xa, ss, s2 = info[t]
                act(s2, ss, AF.Ln, scale=1.0 / D, bias=c.epsc)
            for t in grp:
                sl, xa, ss, s2 = info[t]
                act(s2, s2, AF.Exp, scale=-0.5)
            for t in grp:
                sl, xa, ss, s2 = info[t]
                act(V(xnb, sl * D, (sl + 1) * D), xa, AF.Copy, scale=s2)

    def pairB(gsc, shc, hdst, hoff, hlen, hv, grp, info, plain=False):
        if True:
            for t in grp:
                sl, xa, ss, s2 = info[t]
                pb = c.ps_next(psg)
                pbb = pb.t[:, :].bitcast(BF16)

                def tr_x(e, pbb=pbb, sl=sl):
                    ins = None
                    for kc in range(KC):
                        ins = e.transpose(out=pbb[:, kc * 128:(kc + 1) * 128], in_=xnb.ap(sl * D + kc * 128, sl * D + (kc + 1) * 128), identity=c.identb.ap(0, 128))
                    return ins
                S.op("pe", tr_x, reads=[xnb.reg(sl * D, (sl + 1) * D), c.identb.reg(0, 128)], writes=[pb.reg(0, 512)])
                if plain:
                    c.cp("act", (hv[:, :, hoff + t * 128:hoff + (t + 1) * 128], hdst.reg(0, KC * hlen)),
                         (pbb[:, 0:1024].rearrange("p (k x) -> p k x", k=KC), pb.reg(0, 512)))
                    continue
                for kc in range(KC):
                    lo = kc * hlen + hoff + t * 128
                    ts("dve", (hv[:, kc, hoff + t * 128:hoff + (t + 1) * 128], hdst.reg(lo, lo + 128)),
                       (pbb[:, kc * 128:(kc + 1) * 128], pb.reg(0, 512)), V(gsc, kc, kc + 1), ALU.mult, V(shc, kc, kc + 1), ALU.add)
    front.steps = front_steps
    return front


def backend(c, mod, yT, own_x, x1d, use_s5=True):
    S, sb, tt, ts, stt, act, cp, ms, dma = c.S, c.sb, c.tt, c.ts, c.stt, c.act, c.cp, c.ms, c.dma
    din = c.din
    wG = sb("wG", KC * 2048, BF16, 20480)
    wB = sb("wB", KC * 512, BF16, 53248)
    wbb = sb("wbb", 4 * D, BF16, 61440)
    wba = sb("wba", 4 * D, BF16, 69632)
    wo = sb("wo", KC * D, BF16, 77824)
    wglu = sb("wglu", 4 * 512, BF16, 94208)
    poolw = sb("poolw", 4 * 128, BF16, 98304)
    Bmat = sb("Bmat", 4 * 128, BF16, 99328)
    invc = sb("invc", 4 * 128, F32, 100352)
    sc2 = sb("sc2", 8, F32, 102400)
    hTc = sb("hTc", KC * 512, BF16, 102912)
    UBc = sb("UBc", 4 * 512, BF16, 111104)
    PM = sb("PM", 4 * 512, BF16, 115200)
    ZT = sb("ZT", 4 * 512, BF16, 119296)
    zT = sb("zT", 4 * 512, BF16, 123392)
    sg = sb("sg", 2 * 512, BF16, 127488)
    tmpf = sb("tmpf", 2 * 512, F32, 129536)
    MGc = sb("MGc", KC * 512, BF16, 151552)
    xs = sb("xsb", 2 * D, F32, 159744)
    xnb = sb("xnbb", 2 * D, BF16, 167936)
    junk = sb("junkb", D, BF16, 188416)
    xr = sb("xr", 2 * D, F32, 172032)
    x1o = sb("x1o", 2 * D, F32, 180224)
    front = make_front(c, xs, xnb, junk, evac="dve")
    w_in_v = din["w_in"].rearrange("(k p) n -> p k n", p=128)
    dma("pool", V(wB, 0, KC * 512, "p (k n) -> p k n", k=KC), w_in_v[:, :, 512:1024])
    dma("pool", V(wG, 0, KC * 2048, "p (k n) -> p k n", k=KC), w_in_v[:, :, 1024:3072])
    dma("pool", V(wbb, 0, 4 * D, "p (k n) -> p k n", k=4), din["w_branch_b"].rearrange("(k p) n -> p k n", p=128))
    dma("pool", V(wba, 0, 4 * D, "p (k n) -> p k n", k=4), din["w_branch_a"].rearrange("(k p) n -> p k n", p=128))
    dma("pool", V(wglu, 0, 4 * 512, "p (k n) -> p k n", k=4), din["w_glu"].rearrange("(k p) n -> p k n", p=128))
    dma("pool", V(poolw, 0, 4 * 128, "p (k n) -> p k n", k=4), din["pool_w"].rearrange("j c n -> c j n"))
    dma("pool", V(Bmat, 0, 4 * 128, "p (k n) -> p k n", k=4), din["poolB"].rearrange("j c n -> c j n"))
    dma("pool", V(wo, 0, KC * D, "p (k n) -> p k n", k=KC), din["w_out"].rearrange("(k p) n -> p k n", p=128))
    dma("sp", V(invc, 0, 512), din["invc"].partition_broadcast(128).rearrange("p o n -> p (o n)"))
    dma("sp", V(sc2, 0, 8), din["sc2"][:, :])
    wG3 = wG.ap(0, KC * 2048).rearrange("p (k n) -> p k n", k=KC)
    wB3 = wB.ap(0, KC * 512).rearrange("p (k n) -> p k n", k=KC)
    wbb3 = wbb.ap(0, 4 * D).rearrange("p (k n) -> p k n", k=4)
    wba3 = wba.ap(0, 4 * D).rearrange("p (k n) -> p k n", k=4)
    wo3 = wo.ap(0, KC * D).rearrange("p (k n) -> p k n", k=KC)
    wglu3 = wglu.ap(0, 4 * 512).rearrange("p (k n) -> p k n", k=4)
    pw3 = poolw.ap(0, 512).rearrange("p (k n) -> p k n", k=4)
    bm3 = Bmat.ap(0, 512).rearrange("p (k n) -> p k n", k=4)
    hv = hTc.ap(0, KC * 512).rearrange("p (k t) -> p k t", k=KC)
    ub3 = UBc.ap(0, 4 * 512).rearrange("p (q n) -> p q n", q=4)
    pm3 = PM.ap(0, 4 * 512).rearrange("p (j t) -> p j t", j=4)
    zt3 = ZT.ap(0, 4 * 512).rearrange("p (j t) -> p j t", j=4)
    zz3 = zT.ap(0, 4 * 512).rearrange("p (j t) -> p j t", j=4)
    mg3 = MGc.ap(0, KC * 512).rearrange("p (m t) -> p m t", m=KC)
    yT3 = yT.ap(0, 4 * NTOK).rearrange("p (q t) -> p q t", q=4) if yT is not None else None
    gbc = mod["gbc"]
    pend = []
    for st_ in front.steps(own_x, 0, 512, mod["gs1"], mod["sh1"], hTc, 0, 512):
        st_()
    for n in range(NT // 4):
        for q in range(4):
            pb = c.ps_next()

            def mm(e, pb=pb, q=q):
                ins = None
                for kc in range(KC):
                    ins = e.matmul(pb.ap(0, 512), lhsT=hv[:, kc, q * 128:(q + 1) * 128], rhs=wB3[:, kc, :], start=(kc == 0), stop=(kc == KC - 1))
                return ins
            S.op("pe", mm, reads=[hTc.reg(0, KC * 512), wB.reg(0, KC * 512)], writes=[pb.reg(0, 512)])
            cp("act", (ub3[:, q, :], UBc.reg(q * 512, (q + 1) * 512)), (pb.ap(0, 512), pb.reg(0, 512)))
        for q in range(4):
            pb = c.ps_next()

            def mmP(e, pb=pb, q=q):
                ins = None
                for jw in range(4):
                    ins = e.matmul(pb.ap(jw * 128, (jw + 1) * 128), lhsT=ub3[:, q, jw * 128:(jw + 1) * 128], rhs=bm3[:, jw, :], start=True, stop=True)
                return ins
            S.op("pe", mmP, reads=[UBc.reg(q * 512, (q + 1) * 512), Bmat.reg(0, 512)], writes=[pb.reg(0, 512)])
            tt("dve", (pm3[:, :, q * 128:(q + 1) * 128], PM.reg(0, 4 * 512)), (pb.ap(0, 512).rearrange("p (j t) -> p j t", j=4), pb.reg(0, 512)),
               V(invc, 0, 512, "p (j t) -> p j t", j=4), ALU.mult)
        for jw in range(4):
            pb = c.ps_next()
            S.op("pe", lambda e, pb=pb, jw=jw: e.matmul(pb.ap(0, 512), lhsT=pw3[:, jw, :], rhs=pm3[:, jw, :], start=True, stop=True),
                 reads=[poolw.reg(0, 512), PM.reg(jw * 512, (jw + 1) * 512)], writes=[pb.reg(0, 512)])
            ts("dve", (zt3[:, jw, :], ZT.reg(jw * 512, (jw + 1) * 512)), (pb.ap(0, 512), pb.reg(0, 512)), V(sc2, jw, jw + 1), ALU.mult)
        if use_s5:
            for m4 in range(4):
                pb = c.ps_next()

                def mmG(e, pb=pb, m4=m4, n=n):
                    ins = None
                    for q4 in range(4):
                        ins = e.matmul(pb.ap(0, 512), lhsT=wglu3[:, q4, m4 * 128:(m4 + 1) * 128], rhs=yT3[:, q4, n * 512:(n + 1) * 512], start=(q4 == 0), stop=(q4 == 3))
                    return ins
                S.op("pe", mmG, reads=[wglu.reg(0, 2048), yT.reg(0, 4 * NTOK)], writes=[pb.reg(0, 512)])
                so = (m4 % 2) * 512
                act(V(sg, so, so + 512), (pb.ap(0, 512), pb.reg(0, 512)), AF.Sigmoid, bias=V(sc2, 4 + m4, 5 + m4))
                tt("dve", (zz3[:, m4, :], zT.reg(m4 * 512, (m4 + 1) * 512)), (yT3[:, m4, n * 512:(n + 1) * 512], yT.reg(0, 4 * NTOK)), V(sg, so, so + 512), ALU.mult)
        for m in range(KC):
            pbs = {}
            if use_s5:
                pa, pga = c.ps_next(), c.ps_next()

                def mmA(e, pa=pa, pga=pga, m=m):
                    ins = None
                    for q4 in range(4):
                        ins = e.matmul(pa.ap(0, 512), lhsT=wba3[:, q4, m * 128:(m + 1) * 128], rhs=zz3[:, q4, :], start=(q4 == 0), stop=(q4 == 3))
                    for kc in range(KC):
                        ins = e.matmul(pga.ap(0, 512), lhsT=wG3[:, kc, m * 128:(m + 1) * 128], rhs=hv[:, kc, :], start=(kc == 0), stop=(kc == KC - 1))
                    return ins
                S.op("pe", mmA, reads=[wba.reg(0, 4 * D), zT.reg(0, 2048), wG.reg(0, KC * 2048), hTc.reg(0, KC * 512)], writes=[pa.reg(0, 512), pga.reg(0, 512)])
                act(V(sg, 0, 512), (pga.ap(0, 512), pga.reg(0, 512)), AF.Sigmoid)
                tt("dve", V(tmpf, 0, 512), (pa.ap(0, 512), pa.reg(0, 512)), V(sg, 0, 512), ALU.mult)
            pbb_, pgb = c.ps_next(), c.ps_next()

            def mmB(e, pbb_=pbb_, pgb=pgb, m=m):
                ins = None
                for jw in range(4):
                    ins = e.matmul(pbb_.ap(0, 512), lhsT=wbb3[:, jw, m * 128:(m + 1) * 128], rhs=zt3[:, jw, :], start=(jw == 0), stop=(jw == 3))
                for kc in range(KC):
                    ins = e.matmul(pgb.ap(0, 512), lhsT=wG3[:, kc, 1024 + m * 128:1024 + (m + 1) * 128], rhs=hv[:, kc, :], start=(kc == 0), stop=(kc == KC - 1))
                return ins
            S.op("pe", mmB, reads=[wbb.reg(0, 4 * D), ZT.reg(0, 2048), wG.reg(0, KC * 2048), hTc.reg(0, KC * 512)], writes=[pbb_.reg(0, 512), pgb.reg(0, 512)])
            act(V(sg, 512, 1024), (pgb.ap(0, 512), pgb.reg(0, 512)), AF.Sigmoid)
            if use_s5:
                tt("dve", V(tmpf, 512, 1024), (pbb_.ap(0, 512), pbb_.reg(0, 512)), V(sg, 512, 1024), ALU.mult)
                tt("pool", (mg3[:, m, :], MGc.reg(m * 512, (m + 1) * 512)), V(tmpf, 0, 512), V(tmpf, 512, 1024), ALU.add)
            else:
                tt("dve", (mg3[:, m, :], MGc.reg(m * 512, (m + 1) * 512)), (pbb_.ap(0, 512), pbb_.reg(0, 512)), V(sg, 512, 1024), ALU.mult)
        pend = front.steps(own_x, (n + 1) * 512, 512, mod["gs1"], mod["sh1"], hTc, 0, 512) if n + 1 < NT // 4 else []
        for q in range(4):
            if pend:
                pend.pop(0)()
            tt_ = n * 4 + q
            sl = tt_ % 2
            xa = V(xr, sl * D, (sl + 1) * D)
            dma("sp", xa, own_x[tt_ * 128:(tt_ + 1) * 128, :])
            for hf in range(2):
                pb = c.ps_next()

                def mmO(e, pb=pb, q=q, hf=hf):
                    ins = None
                    for m in range(KC):
                        ins = e.matmul(pb.ap(0, 512), lhsT=mg3[:, m, q * 128:(q + 1) * 128], rhs=wo3[:, m, hf * 512:(hf + 1) * 512], start=(m == 0), stop=(m == KC - 1))
                    return ins
                S.op("pe", mmO, reads=[MGc.reg(0, KC * 512), wo.reg(0, KC * D)], writes=[pb.reg(0, 512)])
                lo = sl * D + hf * 512
                tt("dve", V(x1o, lo, lo + 512), (pb.ap(0, 512), pb.reg(0, 512)), V(gbc, hf * 512, (hf + 1) * 512), ALU.mult)
                tt("pool", V(x1o, lo, lo + 512), V(x1o, lo, lo + 512), V(xr, lo, lo + 512), ALU.add)
            S.op("pool", lambda e, tt_=tt_, sl=sl: e.dma_start(out=x1d[tt_ * 128:(tt_ + 1) * 128, :], in_=x1o.ap(sl * D, (sl + 1) * D)),
                 reads=[x1o.reg(sl * D, (sl + 1) * D)], writes=[R("dram_x1", tt_ * 128, (tt_ + 1) * 128)], dma=True, key=("x1st", sl))


def ffn_phase(c, mod, x1d, out):
    S, sb, tt, ts, stt, act, cp, ms, dma = c.S, c.sb, c.tt, c.ts, c.stt, c.act, c.cp, c.ms, c.dma
    din = c.din
    wfi = sb("wfi", KC * 2 * FH, BF16, 20480)
    wfo = sb("wfo", HT * D, BF16, 110592)
    actT = sb("actT", HT * 512, BF16, 155648)
    h2T = sb("h2T", KC * 512, BF16, 178176)
    x1s = sb("x1s", 4 * D, F32, 186368)
    xnb = sb("xnbf", D, BF16, 202752)
    sgt = sb("sgt", 2 * 512, BF16, 204800)
    junk = sb("junkf", D, BF16, 206848)
    yo = sb("yo", D, F32, 8192)
    gbc, gf, gs2, sh2c = mod["gbc"], mod["gf"], mod["gs2"], mod["sh2"]
    wfi_v = din["w_ffn_in"].rearrange("(k p) n -> p k n", p=128)
    wfi3w = wfi.ap(0, KC * 2 * FH).rearrange("p (k n) -> p k n", k=KC)
    HB = 2
    wfi_regs = {}
    for h0 in range(0, HT, HB):
        for half in range(2):
            c0 = half * FH + h0 * 128
            c1 = c0 + HB * 128
            S.op("pool", lambda e, c0=c0, c1=c1: e.dma_start(out=wfi3w[:, :, c0:c1], in_=wfi_v[:, :, c0:c1]),
                 writes=[R("A", wfi.boff + (kc * 2 * FH + c0) * 2, wfi.boff + (kc * 2 * FH + c1) * 2) for kc in range(KC)],
                 dma=True, key=("wfi", h0, half))
    wfo_v = din["w_ffn_out"].rearrange("(k p) n -> p k n", p=128)
    for h0 in range(0, HT, 11):
        lo, hi = h0 * D, (h0 + 11) * D
        dma("pool", V(wfo, lo, hi, "p (k n) -> p k n", k=11), wfo_v[:, h0:h0 + 11, :])
    wfi3 = wfi.ap(0, KC * 2 * FH).rearrange("p (k n) -> p k n", k=KC)
    wfo3 = wfo.ap(0, HT * D).rearrange("p (k n) -> p k n", k=HT)
    st = c.st
    stc = {"i": 0}

    def rstd_ops(xa):
        base = (stc["i"] % 32) * 4
        stc["i"] += 1
        ss, s2 = V(st, base, base + 1), V(st, base + 1, base + 2)
        act(V(junk, 0, D), xa, AF.Square, accum=ss)
        act(s2, ss, AF.Ln, scale=1.0 / D, bias=c.epsc)
        act(s2, s2, AF.Exp, scale=-0.5)
        return s2

    def load_x1(tt_, sl):
        S.op("sp", lambda e: e.dma_start(out=x1s.ap(sl * D, (sl + 1) * D), in_=x1d[tt_ * 128:(tt_ + 1) * 128, :]),
             reads=[R("dram_x1", tt_ * 128, (tt_ + 1) * 128)], writes=[x1s.reg(sl * D, (sl + 1) * D)], dma=True)
    h2v = h2T.ap(0, KC * 512).rearrange("p (k t) -> p k t", k=KC)

    def front_a(n, q):
        tt_ = n * 4 + q
        sl = tt_ % 2
        load_x1(tt_, sl)
        xa = V(x1s, sl * D, (sl + 1) * D)
        rs = rstd_ops(xa)
        act(V(xnb, 0, D), xa, AF.Copy, scale=rs)

    def front_b(n, q):
        pb = c.ps_next()
        pbb = pb.t[:, :].bitcast(BF16)

        def tr_x(e, pbb=pbb):
            ins = None
            for kc in range(KC):
                ins = e.transpose(out=pbb[:, kc * 128:(kc + 1) * 128], in_=xnb.ap(kc * 128, (kc + 1) * 128), identity=c.identb.ap(0, 128))
            return ins
        S.op("pe", tr_x, reads=[xnb.reg(0, D), c.identb.reg(0, 128)], writes=[pb.reg(0, 512)])
        for kc in range(KC):
            ts("dve", (h2v[:, kc, q * 128:(q + 1) * 128], h2T.reg(kc * 512 + q * 128, kc * 512 + (q + 1) * 128)),
               (pbb[:, kc * 128:(kc + 1) * 128], pb.reg(0, 512)), V(gs2, kc, kc + 1), ALU.mult, V(sh2c, kc, kc + 1), ALU.add)

    def front_tile(n, q):
        front_a(n, q)
        front_b(n, q)

    def hidden(n):
        for hh in range(HT):
            pg, pu = c.ps_next(), c.ps_next()

            def mm_gu(e, pg=pg, pu=pu, hh=hh):
                ins = None
                for kc in range(KC):
                    ins = e.matmul(pg.ap(0, 512), lhsT=wfi3[:, kc, hh * 128:(hh + 1) * 128], rhs=h2v[:, kc, :], start=(kc == 0), stop=(kc == KC - 1))
                for kc in range(KC):
                    ins = e.matmul(pu.ap(0, 512), lhsT=wfi3[:, kc, FH + hh * 128:FH + (hh + 1) * 128], rhs=h2v[:, kc, :], start=(kc == 0), stop=(kc == KC - 1))
                return ins
            rd = [h2T.reg(0, KC * 512)]
            for kc in range(KC):
                for half in range(2):
                    c0 = half * FH + hh * 128
                    rd.append(R("A", wfi.boff + (kc * 2 * FH + c0) * 2, wfi.boff + (kc * 2 * FH + c0 + 128) * 2))
            S.op("pe", mm_gu, reads=rd, writes=[pg.reg(0, 512), pu.reg(0, 512)])
            so = (hh % 2) * 512
            act(V(sgt, so, so + 512), (pg.ap(0, 512), pg.reg(0, 512)), AF.Silu)
            tt("dve", V(actT, hh * 512, (hh + 1) * 512), (pu.ap(0, 512), pu.reg(0, 512)), V(sgt, so, so + 512), ALU.mult)

    def tail_tile(n, q):
        tt_ = n * 4 + q
        sl = 2 + tt_ % 2
        load_x1(tt_, sl)
        xa = V(x1s, sl * D, (sl + 1) * D)
        for hf in range(2):
            po = c.ps_next()

            def mm_o(e, po=po, q=q, hf=hf):
                ins = None
                for hh in range(HT):
                    ins = e.matmul(po.ap(0, 512), lhsT=actT.ap(hh * 512 + q * 128, hh * 512 + (q + 1) * 128), rhs=wfo3[:, hh, hf * 512:(hf + 1) * 512],
                                   start=(hh == 0), stop=(hh == HT - 1))
                return ins
            S.op("pe", mm_o, reads=[actT.reg(0, HT * 512), wfo.reg(0, HT * D)], writes=[po.reg(0, 512)])
            tt("dve", V(yo, hf * 512, (hf + 1) * 512), (po.ap(0, 512), po.reg(0, 512)), V(gbc, D + hf * 512, D + (hf + 1) * 512), ALU.mult)
        tt("pool", V(yo, 0, D), V(yo, 0, D), xa, ALU.add)
        rs = rstd_ops(V(yo, 0, D))
        stt("dve", V(yo, 0, D), V(yo, 0, D), rs, V(gf, 0, D), ALU.mult, ALU.mult)
        S.op("pool", lambda e, tt_=tt_: e.dma_start(out=out[tt_ * 128:(tt_ + 1) * 128, :], in_=yo.ap(0, D)),
             reads=[yo.reg(0, D)], writes=[R("dram_out", tt_ * 128, (tt_ + 1) * 128)], dma=True, key=("st", 0))

    NCH = NT // 4
    for q in range(4):
        front_tile(0, q)
    for n in range(NCH):
        hidden(n)
        for q in range(4):
            if n + 1 < NCH:
                front_a(n + 1, q)
            tail_tile(n, q)
            if n + 1 < NCH:
                front_b(n + 1, q)


def s5_run(c, s5, mod, xb_rows, ctx_rows, seg_of_slot, own_x):
    S, sb, tt, ts, stt, act, cp, ms, dma = c.S, c.sb, c.tt, c.ts, c.stt, c.act, c.cp, c.ms, c.dma
    din = c.din
    EP, FTb, MT, TB, rho8 = s5["EP"], s5["FTb"], s5["MT"], s5["TB"], s5["rho8"]
    cmul = s5["cmul"]
    p2 = {"o": 16384}

    def small(n=NDG):
        assert p2["o"] + n * 4 <= 20480, "small pool overflow"
        b = sb("sm2", n, F32, p2["o"])
        p2["o"] += n * 4
        assert p2["o"] <= 20480 and not (2560 < p2["o"] <= 16384 and p2["o"] > 4096), p2["o"]
        return b

    def sv():
        return V(small(), 0, NDG)
    NJ = 256
    hTu = sb("hTu", KC * 1024, BF16, 53248)
    Xu = sb("Xu", 32 * 128, BF16, 69632)
    Uu = sb("Uu", 32 * 256, BF16, 77824)
    Uu2 = sb("Uu2", 32 * 256, BF16, 36864)
    Ubufs = [Uu, Uu2]
    xs = sb("xs", 2 * D, F32, 94208)
    xnb = sb("xnb", 2 * D, BF16, 102400)
    junk = None
    wA = sb("wA", KC * 512, BF16, 106496)
    Vq = sb("Vq", 2048, F32, 192512)
    tq = sb("tq", 1024, F32, 114688)
    ucnt = {"i": 0}
    st = c.st
    wA3 = wA.ap(0, KC * 512).rearrange("p (k n) -> p k n", k=KC)
    dma("pool", V(wA, 0, KC * 512, "p (k n) -> p k n", k=KC), din["w_in"].rearrange("(k p) n -> p k n", p=128)[:, :, 0:512])
    front = make_front(c, xs, xnb, junk, psg="a")

    def ua_steps(hsrc, hoff, hlen, nk, Udst, uoff, ulen, bias=False):
        hv = hsrc.ap(0, KC * hlen).rearrange("p (k t) -> p k t", k=KC)
        x4 = Xu.ap(0, 32 * 128).rearrange("p (g i x) -> p g i x", g=32, i=8)

        def step_a():
            for i in range(8):
                pb = c.ps_next("a")

                def mm(e, pb=pb, i=i):
                    ins = None
                    for kc in range(KC):
                        ins = e.matmul(pb.ap(0, 512, 0, nk), lhsT=hv[:, kc, hoff + i:hoff + 8 * nk:8], rhs=wA3[:, kc, :],
                                       start=(kc == 0), stop=(kc == KC - 1 and not bias))
                    if bias:
                        ins = e.matmul(pb.ap(0, 512, 0, nk), lhsT=onesb.ap(0, nk, 0, 1), rhs=brow.ap(0, 512, 0, 1), start=False, stop=True)
                    return ins
                S.op("pe", mm, reads=[hsrc.reg(0, KC * hlen), wA.reg(0, KC * 512), onesb.reg(0, 128), brow.reg(0, 512)], writes=[pb.reg(0, 512)])
                cp("act", (x4[0:nk, :, i, :], Xu.reg(0, 32 * 128)), (pb.ap(0, 512, 0, nk).rearrange("p (g x) -> p g x", g=32), pb.reg(0, 512)))

        def step_b():
            u3 = Udst.ap(0, 32 * ulen).rearrange("p (g k) -> p g k", g=32)
            for g0 in range(0, 32, 8):
                pb = c.ps_next("a")
                pbb = pb.t[:, :].bitcast(BF16)

                def trU(e, pbb=pbb, g0=g0):
                    ins = None
                    for q in range(8):
                        ins = e.transpose(out=pbb[:, q * 128:q * 128 + nk], in_=Xu.ap((g0 + q) * 128, (g0 + q + 1) * 128, 0, nk),
                                          identity=c.identb.ap(0, nk, 0, nk))
                    return ins
                S.op("pe", trU, reads=[Xu.reg(0, 32 * 128), c.identb.reg(0, 128)], writes=[pb.reg(0, 512)])
                cp("act", (u3[:, g0:g0 + 8, uoff:uoff + nk], Udst.reg(0, 32 * ulen)),
                   (pbb[:, 0:1024].rearrange("p (q k) -> p q k", q=8)[:, :, 0:nk], pb.reg(0, 512)))
        return [step_a, step_b]

    ep5 = EP.ap(0, 128 * 128).rearrange("p (d g r h x) -> p d g r h x", d=2, g=G2, r=2, h=2)
    tb4 = TB.ap(0, 2 * NDG * NJ).rearrange("p (c n j) -> p c n j", c=2, n=NDG)

    def e_unit(Usrc, uoff, ulen, nk, d, qq, j0, rev, Vdst, tmp):
        u3 = Usrc.ap(0, 32 * ulen).rearrange("p (g k) -> p g k", g=32)
        pbs = [c.ps_next("b"), c.ps_next("b")]

        def mm(e):
            ins = None
            for ri in range(2):
                for gq in range(4):
                    g2 = qq * 4 + gq
                    for gh in range(2):
                        ins = e.matmul(pbs[ri].ap(gq * 128, gq * 128 + nk), lhsT=ep5[:, d, g2, ri, gh, :],
                                       rhs=u3[:, 2 * g2 + gh, uoff:uoff + nk], start=(gh == 0), stop=(gh == 1))
            return ins
        S.op("pe", mm, reads=[EP.reg(0, 128 * 128), Usrc.reg(0, 32 * ulen)], writes=[pbs[0].reg(0, 512), pbs[1].reg(0, 512)])
        n0 = d * G2 + qq * 4
        if not rev:
            tr = tb4[:, 0, n0:n0 + 4, j0:j0 + nk]
            ti = tb4[:, 1, n0:n0 + 4, j0:j0 + nk]
        else:
            lo = j0 - nk + 1
            tr = tb4[:, 0, n0:n0 + 4, lo:lo + nk][:, :, ::-1]
            ti = tb4[:, 1, n0:n0 + 4, lo:lo + nk][:, :, ::-1]
        TR, TI = (tr, TB.reg(0, 2 * NDG * NJ)), (ti, TB.reg(0, 2 * NDG * NJ))
        sre = (pbs[0].ap(0, 512).rearrange("p (g k) -> p g k", g=4)[:, :, 0:nk], pbs[0].reg(0, 512))
        sim = (pbs[1].ap(0, 512).rearrange("p (g k) -> p g k", g=4)[:, :, 0:nk], pbs[1].reg(0, 512))
        vb, lo_ = Vdst
        vre = (vb.ap(lo_, lo_ + 512).rearrange("p (g k) -> p g k", g=4)[:, :, 0:nk], vb.reg(lo_, lo_ + 512))
        vim = (vb.ap(lo_ + 512, lo_ + 1024).rearrange("p (g k) -> p g k", g=4)[:, :, 0:nk], vb.reg(lo_ + 512, lo_ + 1024))
        ta = (tmp.ap(0, 512).rearrange("p (g k) -> p g k", g=4)[:, :, 0:nk], tmp.reg(0, 512))
        tb_ = (tmp.ap(512, 1024).rearrange("p (g k) -> p g k", g=4)[:, :, 0:nk], tmp.reg(512, 1024))
        tt("dve", vre, sre, TR, ALU.mult)
        tt("dve", ta, sim, TI, ALU.mult)
        tt("dve", vim, sim, TR, ALU.mult)
        tt("dve", tb_, sre, TI, ALU.mult)
        tt("dve", vre, vre, ta, ALU.subtract)
        tt("dve", vim, vim, tb_, ALU.add)

    def scan_unit(Vsrc, nk, d, qq, rev, inits, Wdst, lasts=None):
        vb, vlo = Vsrc
        wb, wlo = Wdst
        v3 = vb.ap(vlo, vlo + 1024).rearrange("p (n k) -> p n k", n=8)
        w3 = wb.ap(wlo, wlo + 1024).rearrange("p (n k) -> p n k", n=8)
        for ri in range(2):
            for gq in range(4):
                n_ = d * G2 + qq * 4 + gq
                coef = (rho8[0][:, n_:n_ + 1].to_broadcast([128, nk]), rho8[1])
                n = ri * 4 + gq
                dv = v3[:, n, 0:nk]
                ov = w3[:, n, 0:nk]
                if rev:
                    dv, ov = dv[:, ::-1], ov[:, ::-1]
                ini = inits[n]
                rd = [vb.reg(vlo + ri * 512, vlo + (ri + 1) * 512), coef[1]] + ([ini[1]] if isinstance(ini, tuple) else [])
                S.op("dve", lambda e, ov=ov, dv=dv, coef=coef, ini=ini: e.tensor_tensor_scan(
                    out=ov, data0=coef[0], data1=dv, initial=(ini[0] if isinstance(ini, tuple) else ini), op0=ALU.mult, op1=ALU.add),
                    reads=rd, writes=[wb.reg(wlo + n * 128, wlo + (n + 1) * 128)])

    def newt(n):
        return small(n)
    Eend = {}
    units = [("ctx", ctx_rows, 0, 256, mod["cgs1"], mod["csh1"])]
    for sl in range(3):
        units.append((sl, xb_rows, sl * 2048, 2048, mod["gs1"], mod["sh1"]))
    units.append(("own", own_x, 0, 2048, mod["gs1"], mod["sh1"]))

    def fu_steps(ui):
        (name, src, row0, ntok, gsc, shc) = units[ui]
        NK = ntok // 8
        nkt = max(1, NK // 128)
        nk = min(NK, 128)
        steps = []
        plain = (name != "ctx")
        for kt in range(nkt):
            steps += front.steps(src, row0 + kt * 1024, min(ntok, 1024), gsc, shc, hTu, 0, 1024, plain=plain)
            steps += ua_steps(hTu, 0, 1024, nk, Ubufs[ui % 2], kt * 128, 256, bias=plain)
        return steps
    onesb = sb("onesb", 128, BF16, 2560)
    brow = sb("brow", 512, BF16, 2816)
    shb = sb("shb", KC, BF16, 3840)
    ms("pool", (onesb.ap(0, 128, 0, 1), onesb.reg(0, 128)), 1.0)
    cp("act", V(shb, 0, KC), V(mod["sh1"], 0, KC))
    pbq = c.ps_next()

    def mm_b(e):
        ins = None
        for kc in range(KC):
            ins = e.matmul(pbq.ap(0, 512, 0, 1), lhsT=shb.ap(kc, kc + 1), rhs=wA3[:, kc, :], start=(kc == 0), stop=(kc == KC - 1))
        return ins
    S.op("pe", mm_b, reads=[shb.reg(0, KC), wA.reg(0, KC * 512)], writes=[pbq.reg(0, 512)])
    cp("act", (brow.ap(0, 512, 0, 1), brow.reg(0, 512)), (pbq.ap(0, 512, 0, 1), pbq.reg(0, 512)))
    for st_ in fu_steps(0):
        st_()
    for kc in range(KC):
        ts("dve", (wA3[:, kc, :], wA.reg(kc * 512, (kc + 1) * 512)), (wA3[:, kc, :], wA.reg(kc * 512, (kc + 1) * 512)), V(mod["gs1"], kc, kc + 1), ALU.mult)
    for ui in range(4):
        (name, src, row0, ntok, gsc, shc) = units[ui]
        Ucur = Ubufs[ui % 2]
        pending = fu_steps(ui + 1)
        NK = ntok // 8
        nkt = max(1, NK // 128)
        nk = min(NK, 128)
        wl = {d: newt(32) for d in range(2)}
        Eend[name] = wl
        for d in range(2):
            kts = list(range(nkt)) if d == 0 else list(range(nkt - 1, -1, -1))
            for qq in range(4):
                prevV = None
                for ki, kt in enumerate(kts):
                    j0 = kt * 128 if d == 0 else (NK - 1 - kt * 128)
                    vsl = (ucnt["i"] % 2) * 1024
                    ucnt["i"] += 1
                    e_unit(Ucur, kt * 128, 256, nk, d, qq, j0, d == 1, (Vq, vsl), tq)
                    pos = 0 if d == 1 else nk - 1
                    if prevV is None:
                        inits = [0.0] * 8
                    else:
                        pw3 = Vq.ap(prevV, prevV + 1024).rearrange("p (n k) -> p n k", n=8)
                        inits = [(pw3[:, n, pos:pos + 1], Vq.reg(prevV + n * 128, prevV + (n + 1) * 128)) for n in range(8)]
                    scan_unit((Vq, vsl), nk, d, qq, d == 1, inits, (Vq, vsl))
                    prevV = vsl
                    if pending:
                        pending.pop(0)()
                pw3 = Vq.ap(prevV, prevV + 1024).rearrange("p (r g k) -> p r g k", r=2, g=4)
                w3_ = wl[d].ap(qq * 8, (qq + 1) * 8).rearrange("p (g r) -> p g r", g=4)
                for ri in range(2):
                    cp("act", (w3_[:, :, ri:ri + 1], wl[d].reg(qq * 8, (qq + 1) * 8)),
                       (pw3[:, ri, :, pos:pos + 1], Vq.reg(prevV, prevV + 1024)))
        while pending:
            pending.pop(0)()
    Uown = Ubufs[4 % 2]
    t1, t2 = s5["t12"]
    upows = s5["upows"]

    def half(x, d):
        return (x[0][:, d * G2:(d + 1) * G2], x[1])

    def conj_mul(outr, outi, ar, ai, br, bi, ta, tb):
        tt("dve", ta, ar, br, ALU.mult)
        tt("dve", tb, ai, bi, ALU.mult)
        tt("dve", outr, ta, tb, ALU.add)
        tt("dve", ta, ar, bi, ALU.mult)
        tt("dve", tb, ai, br, ALU.mult)
        tt("dve", outi, ta, tb, ALU.subtract)
    u255r, u255i, u31r, u31i = sv(), sv(), sv(), sv()
    conj_mul(u255r, u255i, upows[0][0], upows[0][1], upows[8][0], upows[8][1], t1, t2)
    conj_mul(u31r, u31i, upows[0][0], upows[0][1], upows[5][0], upows[5][1], t1, t2)
    r256, Ar, Ai = sv(), sv(), sv()
    cp("dve", r256, rho8)
    for _ in range(8):
        tt("dve", r256, r256, r256, ALU.mult)
    tt("dve", Ar, upows[8][0], r256, ALU.mult)
    stt("dve", Ai, upows[8][1], -1.0, r256, ALU.mult, ALU.mult)
    oh = small(4)
    dma("sp", V(oh, 0, 4), din["onehot"][:, :])
    carry = {}
    h1, h2 = small(16), small(16)
    H1, H2 = V(h1, 0, 16), V(h2, 0, 16)
    sbuf_ = [(V(small(16), 0, 16), V(small(16), 0, 16)) for _ in range(2)]
    nbuf_ = [(V(small(16), 0, 16), V(small(16), 0, 16)) for _ in range(2)]
    cr_, ci_ = small(16), small(16)
    CR, CI = V(cr_, 0, 16), V(ci_, 0, 16)
    for d in range(2):
        def send(name, slot):
            w3 = Eend[name][d].ap(0, 32).rearrange("p (g r) -> p g r", g=G2)
            wr_, wi_ = (w3[:, :, 0], Eend[name][d].reg(0, 32)), (w3[:, :, 1], Eend[name][d].reg(0, 32))
            ur_, ui_ = (u31r, u31i) if name == "ctx" else (u255r, u255i)
            conj_mul(sbuf_[slot][0], sbuf_[slot][1], half(ur_, d), half(ui_, d), wr_, wi_, H1, H2)
            return sbuf_[slot]
        cur = send("ctx", 0)
        ms("dve", CR, 0.0)
        ms("dve", CI, 0.0)
        order = [0, 1, 2, 3] if d == 0 else [3, 2, 1, 0]
        for si_, pos_ in enumerate(order):
            stt("dve", CR, cur[0], V(oh, pos_, pos_ + 1), CR, ALU.mult, ALU.add)
            stt("dve", CI, cur[1], V(oh, pos_, pos_ + 1), CI, ALU.mult, ALU.add)
            if si_ == 3:
                break
            slot = pos_ if d == 0 else pos_ - 1
            e_ = send(slot, 1)
            NR, NI = nbuf_[si_ % 2]
            tt("dve", H1, half(Ar, d), cur[0], ALU.mult)
            tt("dve", H2, half(Ai, d), cur[1], ALU.mult)
            tt("dve", NR, H1, H2, ALU.subtract)
            tt("dve", NR, NR, e_[0], ALU.add)
            tt("dve", H1, half(Ar, d), cur[1], ALU.mult)
            tt("dve", H2, half(Ai, d), cur[0], ALU.mult)
            tt("dve", NI, H1, H2, ALU.add)
            tt("dve", NI, NI, e_[1], ALU.add)
            cur = (NR, NI)
        cpk, w0 = small(32), small(32)
        c3 = cpk.ap(0, 32).rearrange("p (g r) -> p g r", g=G2)
        w03 = w0.ap(0, 32).rearrange("p (g r) -> p g r", g=G2)
        cp("dve", (c3[:, :, 0], cpk.reg(0, 32)), CR)
        cp("dve", (c3[:, :, 1], cpk.reg(0, 32)), CI)
        conj_mul((w03[:, :, 0], w0.reg(0, 32)), (w03[:, :, 1], w0.reg(0, 32)), half(upows[0][0], d), half(upows[0][1], d), CR, CI, H1, H2)
        carry[d] = (cpk, w0)
        c.dump("carry_%d" % d, cpk, 0, 32)

    SIN = {0: sb("SINf", 8 * 1024, BF16, 53248), 1: sb("SINb", 8 * 1024, BF16, 94208)}
    Vo2 = sb("Vo2", 2 * 1024, F32, 69632)
    Uu = Uown
    c.dump("Uown", Uu, 0, 32 * 256)
    NK = 256
    for d in range(2):
        kts = [0, 1] if d == 0 else [1, 0]
        cpk, w0 = carry[d]
        for qq in range(4):
            for kt in kts:
                j0 = kt * 128 if d == 0 else (NK - 1 - kt * 128)
                e_unit(Uu, kt * 128, 256, 128, d, qq, j0, d == 1, (Vo2, kt * 1024), tq)
            lastw = small(8)
            for ki, kt in enumerate(kts):
                j0 = kt * 128 if d == 0 else (NK - 1 - kt * 128)
                if ki == 0:
                    w03 = w0.ap(0, 32).rearrange("p (g r) -> p g r", g=G2)
                    inits = [(w03[:, qq * 4 + n % 4, n // 4:n // 4 + 1], w0.reg(0, 32)) for n in range(8)]
                else:
                    inits = [V(lastw, n, n + 1) for n in range(8)]
                scan_unit((Vo2, kt * 1024), 128, d, qq, d == 1, inits, (Vo2, kt * 1024))
                if ki == 0:
                    pos = 0 if d == 1 else 127
                    cp("act", (lastw.ap(0, 8).rearrange("p (n o) -> p n o", o=1), lastw.reg(0, 8)),
                       (Vo2.ap(kt * 1024, (kt + 1) * 1024).rearrange("p (n k) -> p n k", n=8)[:, :, pos:pos + 1], Vo2.reg(kt * 1024, (kt + 1) * 1024)))
                n0 = d * G2 + qq * 4
                if d == 0:
                    tr = tb4[:, 0, n0:n0 + 4, j0:j0 + 128]
                    ti = tb4[:, 1, n0:n0 + 4, j0:j0 + 128]
                else:
                    lo = j0 - 127
                    tr = tb4[:, 0, n0:n0 + 4, lo:lo + 128][:, :, ::-1]
                    ti = tb4[:, 1, n0:n0 + 4, lo:lo + 128][:, :, ::-1]
                TR, TI = (tr, TB.reg(0, 2 * NDG * NJ)), (ti, TB.reg(0, 2 * NDG * NJ))
                wre = (Vo2.ap(kt * 1024, kt * 1024 + 512).rearrange("p (g k) -> p g k", g=4), Vo2.reg(kt * 1024, kt * 1024 + 512))
                wim = (Vo2.ap(kt * 1024 + 512, (kt + 1) * 1024).rearrange("p (g k) -> p g k", g=4), Vo2.reg(kt * 1024 + 512, (kt + 1) * 1024))
                ore = (Vq.ap(0, 512).rearrange("p (g k) -> p g k", g=4), Vq.reg(0, 512))
                oim = (Vq.ap(512, 1024).rearrange("p (g k) -> p g k", g=4), Vq.reg(512, 1024))
                ta = (tq.ap(0, 512).rearrange("p (g k) -> p g k", g=4), tq.reg(0, 512))
                tb_ = (tq.ap(512, 1024).rearrange("p (g k) -> p g k", g=4), tq.reg(512, 1024))
                tt("dve", ore, wre, TR, ALU.mult)
                tt("dve", ta, wim, TI, ALU.mult)
                tt("dve", oim, wim, TR, ALU.mult)
                tt("dve", tb_, wre, TI, ALU.mult)
                tt("dve", ore, ore, ta, ALU.add)
                tt("dve", oim, oim, tb_, ALU.subtract)
                sin = SIN[d]
                base = (kt * 4 + qq) * 1024
                s4 = sin.ap(base, base + 1024).rearrange("p (g r k) -> p g r k", g=4, r=2)
                sreg = sin.reg(base, base + 1024)
                c3 = cpk.ap(0, 32).rearrange("p (g r) -> p g r", g=G2)
                okt = 1 - kt
                nb_ = (okt * 4 + qq) * 1024
                nx4 = sin.ap(nb_, nb_ + 1024).rearrange("p (g r k) -> p g r k", g=4, r=2)
                for ri in range(2):
                    v3 = (Vq.ap(ri * 512, (ri + 1) * 512).rearrange("p (g k) -> p g k", g=4), Vq.reg(ri * 512, (ri + 1) * 512))
                    if d == 0:
                        cp("act", (s4[:, :, ri, 1:128], sreg), (v3[0][:, :, 0:127], v3[1]))
                        if kt == 0:
                            cp("act", (s4[:, :, ri, 0:1], sreg), (c3[:, qq * 4:(qq + 1) * 4, ri:ri + 1], cpk.reg(0, 32)))
                            cp("act", (nx4[:, :, ri, 0:1], sin.reg(nb_, nb_ + 1024)), (v3[0][:, :, 127:128], v3[1]))
                    else:
                        cp("act", (s4[:, :, ri, 0:127], sreg), (v3[0][:, :, 1:128], v3[1]))
                        if kt == 1:
                            cp("act", (s4[:, :, ri, 127:128], sreg), (c3[:, qq * 4:(qq + 1) * 4, ri:ri + 1], cpk.reg(0, 32)))
                            cp("act", (nx4[:, :, ri, 127:128], sin.reg(nb_, nb_ + 1024)), (v3[0][:, :, 0:1], v3[1]))
    c.dump("SINf", SIN[0], 0, 8 * 1024)
    c.dump("SINb", SIN[1], 0, 8 * 1024)

    FP = sb("FP", 128 * 128, BF16, 151552)
    fp4 = FP.ap(0, 128 * 128).rearrange("p (n h x) -> p n h x", h=2, x=128)
    ftb3 = FTb.ap(0, 8192).rearrange("p (n x) -> p n x", x=128)
    hm = s5["hm"]
    for gh in range(2):
        ts("dve", (fp4[:, :, gh, :], FP.reg(0, 128 * 128)), (ftb3, FTb.reg(0, 8192)), V(hm, gh, gh + 1), ALU.mult)
    Ytok = sb("Ytok", 2 * 8 * 512, BF16, 118784)
    yT = sb("yT", 4 * NTOK, BF16, 135168)
    u3 = Uu.ap(0, 32 * 256).rearrange("p (g k) -> p g k", g=32)
    fp3 = FP.ap(0, 128 * 128).rearrange("p (n x) -> p n x", x=128)
    for kt in range(2):
        for g0 in range(0, 32, 4):
            pb = c.ps_next()

            def mmY(e, pb=pb, g0=g0, kt=kt):
                ins = None
                for gq in range(4):
                    g = g0 + gq
                    g2, gh = g // 2, g % 2
                    qq, gl = g2 // 4, g2 % 4
                    o_ = pb.ap(gq * 128, (gq + 1) * 128)
                    ins = e.matmul(o_, lhsT=u3[:, g, kt * 128:(kt + 1) * 128], rhs=MT.ap(g * 128, (g + 1) * 128), start=True, stop=False)
                    for d in range(2):
                        for ri in range(2):
                            lo = (kt * 4 + qq) * 1024 + (gl * 2 + ri) * 128
                            n = ((d * G2 + g2) * 2 + ri) * 2 + gh
                            ins = e.matmul(o_, lhsT=SIN[d].ap(lo, lo + 128), rhs=fp3[:, n, :], start=False, stop=(d == 1 and ri == 1))
                return ins
            S.op("pe", mmY, reads=[Uu.reg(0, 32 * 256), MT.reg(0, 32 * 128), SIN[0].reg(0, 8192), SIN[1].reg(0, 8192), FP.reg(0, 128 * 128)],
                 writes=[pb.reg(0, 512)])
            y4 = Ytok.ap(kt * 4096, (kt + 1) * 4096).rearrange("p (i g x) -> p i g x", i=8, g=32)
            for gq in range(4):
                act((y4[:, :, g0 + gq, :], Ytok.reg(kt * 4096, (kt + 1) * 4096)),
                    (pb.ap(gq * 128, (gq + 1) * 128).rearrange("p (i x) -> p i x", i=8), pb.reg(0, 512)), AF.Gelu_apprx_tanh)
    yT3 = yT.ap(0, 4 * NTOK).rearrange("p (q t) -> p q t", q=4)
    for kt in range(2):
        for q4 in range(4):
            pb = c.ps_next()
            pbb = pb.t[:, :].bitcast(BF16)

            def trY(e, pbb=pbb, kt=kt, q4=q4):
                ins = None
                for i in range(8):
                    lo = kt * 4096 + i * 512 + q4 * 128
                    ins = e.transpose(out=pbb[:, i * 128:(i + 1) * 128], in_=Ytok.ap(lo, lo + 128), identity=c.identb.ap(0, 128))
                return ins
            S.op("pe", trY, reads=[Ytok.reg(0, 2 * 4096), c.identb.reg(0, 128)], writes=[pb.reg(0, 512)])
            cp("dve", (yT3[:, q4, kt * 1024:(kt + 1) * 1024].rearrange("p (k i) -> p k i", i=8), yT.reg(q4 * NTOK + kt * 1024, q4 * NTOK + (kt + 1) * 1024)),
               (pbb[:, 0:1024].rearrange("p (i k) -> p k i", i=8), pb.reg(0, 512)))
    c.dump("yT", yT, 0, 4 * NTOK)
    return yT


def build_nc():
    nc = bass.Bass("TRN2", target_bir_lowering=False)

    def din_(name, shape):
        return nc.dram_tensor(name, list(shape), F32, kind="ExternalInput").ap()
    din = {}
    for name, shape in INPUT_SHAPES.items():
        din[name] = din_(name, shape)
    out = nc.dram_tensor("out", [NTOK, D], F32, kind="ExternalOutput").ap()
    S = Sched()
    with ExitStack() as es:
        c = make_ctx(nc, es, S, DEBUG, DBG_WORDS)
        c.din = din
        sb = c.sb
        c.identf = sb("identf", 128, F32, 0)
        c.onesf = sb("onesf", 128, F32, 512)
        c.identb = sb("identb", 128, BF16, 1024)
        c.dma("sp", V(c.identf, 0, 128), din["ident"][:, :])
        c.dma("pool", V(c.identb, 0, 128), din["ident"][:, :])
        c.ms("dve", V(c.onesf, 0, 128), 1.0)
        c.st = sb("st", 128, F32, 2048)
        epsb = sb("epsb", 1, F32, 1792)
        c.ms("pool", V(epsb, 0, 1), EPS)
        c.epsc = V(epsb, 0, 1)
        if STAGE in ("full", "nos5", "none"):
            x1d = nc.dram_tensor("x1d", [NTOK, D], F32, kind="Internal").ap()
            yT = None
            mod = mod_phase(c)
            if STAGE == "full":
                s5 = s5_precompute(c, din)
            mod["finish"]()
            if STAGE == "full":
                yT = s5_run(c, s5, mod, din["xb"], din["ctx"], None, din["x"])
            if STAGE == "none":
                x1d = din["x"]
            else:
                backend(c, mod, yT, din["x"], x1d, use_s5=(STAGE == "full"))
            ffn_phase(c, mod, x1d, out)
        if STAGE in ("s5pre", "s5ana"):
            if STAGE == "s5ana":
                mod = mod_phase(c)
            s5 = s5_precompute(c, din)
            if STAGE == "s5ana":
                mod["finish"]()
            if STAGE == "s5ana":
                s5_run(c, s5, mod, din["xb"], din["ctx"], None, din["x"])
            yo = sb("yo", D, F32, 4096)
            c.ms("dve", V(yo, 0, D), 0.0)
            for tt_ in range(NT):
                S.op("sp", lambda e, tt_=tt_: e.dma_start(out=out[tt_ * 128:(tt_ + 1) * 128, :], in_=yo.ap(0, D)),
                     reads=[yo.reg(0, D)], writes=[R("dram_out", tt_ * 128, (tt_ + 1) * 128)], dma=True, key=("st", 0))
        S.op("sp", None, reads=[R("dram_out", 0, NTOK), R("dram_dbg", 0, DBG_WORDS)])
        S.analyse()
        sems = {e: es.enter_context(nc.semaphore("sem_" + e)) for e in ("pe", "act", "dve", "pool", "sp")}
        dsems = {k: es.enter_context(nc.semaphore("dsem%d" % i)) for i, k in enumerate(S.dma_keys)}
        print("ops", len(S.ops), "dma sems", len(dsems))
        with nc.Block() as block:
            @block.sync
            def _(e):
                S.emit("sp", e, sems, dsems)

            @block.scalar
            def _(e):
                S.emit("act", e, sems, dsems)

            @block.vector
            def _(e):
                S.emit("dve", e, sems, dsems)

            @block.gpsimd
            def _(e):
                S.emit("pool", e, sems, dsems)

            @block.tensor
            def _(e):
                S.emit("pe", e, sems, dsems)
    return nc


INPUT_SHAPES = {
    "x": [NTOK, D], "ident": [128, 128],
    "s5sm": [128, 96], "s5B": [128, 1024], "s5C": [128, 1024], "hm": [128, 2], "dcol": [128, 32],
    "maskf": [128, 128], "maskb": [128, 128],
    "xb": [3 * NTOK, D], "ctx": [256, D], "cc": [128, KC * 64], "w_mod": [D, 6 * D], "b_mod": [1, 6 * D],
    "onehot": [128, 4], "n1col": [128, KC], "n2col": [128, KC], "final_norm_g": [1, D], "w_in": [D, 3 * D],
    "w_branch_a": [512, D], "w_branch_b": [512, D], "w_glu": [512, 512], "pool_w": [4, 128, 128], "poolB": [4, 128, 128],
    "w_out": [D, D], "invc": [1, 512], "sc2": [128, 8], "w_ffn_in": [D, 2 * FH], "w_ffn_out": [FH, D],
}


def host_layout(inputs):
    f = lambda a: np.ascontiguousarray(np.asarray(a), dtype=np.float32)
    cm = {}
    cm["ident"] = np.eye(128, dtype=np.float32)

    def pm(a):
        a = f(a)
        sh = a.shape
        a = a.reshape(2, 16, 2, 64, *sh[3:])
        a = np.moveaxis(a, (2, 3), (0, 1))
        return np.ascontiguousarray(a.reshape(128, -1))
    are = pm(inputs["s5_a_re"][0])
    aim = pm(inputs["s5_a_im"][0])
    ldt = pm(np.broadcast_to(f(inputs["s5_log_dt"][0])[:, :, None], (2, 32, 64)))
    cm["s5sm"] = np.concatenate([are, aim, ldt], axis=1)
    cm["s5B"] = np.concatenate([pm(inputs["s5_b_re"][0]), pm(inputs["s5_b_im"][0])], axis=1)
    cre = np.swapaxes(f(inputs["s5_c_re"][0]), 2, 3)
    cim = np.swapaxes(f(inputs["s5_c_im"][0]), 2, 3)
    cm["s5C"] = np.concatenate([pm(cre), pm(cim)], axis=1)
    hm = np.zeros((128, 2), np.float32)
    hm[:64, 0] = 1.0
    hm[64:, 1] = 1.0
    cm["hm"] = hm
    dsk = f(inputs["s5_d"][0]).reshape(32, 16)
    cm["dcol"] = np.ascontiguousarray(np.tile(dsk.T[None, :, :], (8, 1, 1)).reshape(128, 32))
    ii = np.arange(128) // 16
    cm["maskf"] = (ii[:, None] <= ii[None, :]).astype(np.float32)
    cm["maskb"] = (ii[:, None] >= ii[None, :]).astype(np.float32)
    cm["w_mod"] = f(inputs["w_mod"][0])
    cm["b_mod"] = f(inputs["b_mod"][0]).reshape(1, 6 * D)
    cm["n1col"] = f(f(inputs["norm1_g"][0]).reshape(KC, 128).T)
    cm["n2col"] = f(f(inputs["norm2_g"][0]).reshape(KC, 128).T)
    cm["final_norm_g"] = f(inputs["final_norm_g"]).reshape(1, D)
    cm["w_in"] = f(inputs["w_in"][0])
    for k in ("w_branch_a", "w_branch_b", "w_glu", "pool_w", "w_out", "w_ffn_in", "w_ffn_out"):
        cm[k] = f(inputs[k][0])
    pos = np.arange(64)
    PB = np.zeros((4, 128, 128), np.float32)
    invc = np.zeros((1, 4, 128), np.float32)
    for jw, w in enumerate((2, 4, 8, 16)):
        lo = np.clip(pos - w // 2, 0, 63)
        hi = np.clip(pos + w - 1 - w // 2, 0, 63) + 1
        blk = ((pos[:, None] >= lo[None, :]) & (pos[:, None] < hi[None, :])).astype(np.float32)
        cnt = (hi - lo).astype(np.float32)
        blk = blk - np.diag(cnt)
        PB[jw, :64, :64] = blk
        PB[jw, 64:, 64:] = blk
        invc[0, jw, :64] = 1.0 / cnt
        invc[0, jw, 64:] = 1.0 / cnt
    cm["poolB"] = PB
    cm["invc"] = invc.reshape(1, 512)
    sc2 = np.zeros((128, 8), np.float32)
    sc2[:, 0:4] = f(inputs["pool_scale"][0]).reshape(4, 128).T
    sc2[:, 4:8] = f(inputs["b_glu"][0]).reshape(4, 128).T
    cm["sc2"] = sc2
    return cm


def kernel(**inputs):
    f = lambda a: np.ascontiguousarray(np.asarray(a), dtype=np.float32)
    x = f(inputs["x"])
    nc = build_nc()
    common = host_layout(inputs)
    in_maps = []
    for cid in range(8):
        b, j = cid // 4, cid % 4
        m = {k: v for k, v in common.items() if k in INPUT_SHAPES}
        m["x"] = np.ascontiguousarray(x[b, j * NTOK:(j + 1) * NTOK, :])
        m["xb"] = np.ascontiguousarray(np.concatenate([x[b, i * NTOK:(i + 1) * NTOK] for i in range(4) if i != j], axis=0))
        m["ctx"] = f(inputs["ctx"][b])
        cc = np.zeros((128, KC, 64), np.float32)
        cc[:, :, 0] = f(inputs["c"])[b].reshape(KC, 128).T
        cc[:, :, 32] = f(inputs["c_ctx"]).reshape(KC, 128).T
        m["cc"] = cc.reshape(128, KC * 64)
        oh = np.zeros((128, 4), np.float32)
        oh[:, j] = 1.0
        m["onehot"] = oh
        in_maps.append(m)
    res = run_bass_kernel_spmd(nc, in_maps, core_ids=list(range(8)))
    if DEBUG:
        global DBG_OUT
        DBG_OUT = [res.results[cid]["dbg"] for cid in range(8)]
    outp = np.empty((2, 8192, D), np.float32)
    for cid in range(8):
        b, j = cid // 4, cid % 4
        outp[b, j * NTOK:(j + 1) * NTOK, :] = res.results[cid]["out"]
    return outp
```

```python
import numpy as np
from contextlib import ExitStack
import concourse.bass as bass
import concourse.mybir as mybir
from concourse.bass_utils import run_bass_kernel_spmd

F32 = mybir.dt.float32
BF16 = mybir.dt.bfloat16
ALU = mybir.AluOpType
AF = mybir.ActivationFunctionType

D = 1024
KC = D // 128
NTOK = 2048
NT = NTOK // 128
FH = 2816
HT = FH // 128
EPS = 1e-6
MIXER = "full"
DEBUG = False
DBG_WORDS = 131072
STAGE = "full"
DBG_LAYOUT = {}


class Sched:
    def __init__(self):
        self.ops = []

    def op(self, eng, fn, reads=(), writes=(), dma=False, key=None):
        if dma:
            key = (eng, writes[0] if key is None else key)
        self.ops.append(dict(eng=eng, fn=fn, reads=list(reads), writes=list(writes),
                             dma=dma, key=key, deps=set(), sig=False))
        return len(self.ops) - 1

    @staticmethod
    def _ov(a, b):
        return a[0] == b[0] and a[1] < b[2] and b[1] < a[2]

    @staticmethod
    def _cov(a, b):
        return a[0] == b[0] and a[1] <= b[1] and b[2] <= a[2]

    def analyse(self):
        wr, rd = {}, {}
        for i, o in enumerate(self.ops):
            for r in o["reads"]:
                for (reg, j) in wr.get(r[0], []):
                    if self._ov(reg, r):
                        o["deps"].add(j)
            for w in o["writes"]:
                for (reg, j) in wr.get(w[0], []):
                    if self._ov(reg, w):
                        o["deps"].add(j)
                for (reg, j) in rd.get(w[0], []):
                    if self._ov(reg, w):
                        o["deps"].add(j)
            o["deps"].discard(i)
            for w in o["writes"]:
                wr[w[0]] = [(reg, j) for (reg, j) in wr.get(w[0], []) if not self._cov(w, reg)]
                rd[w[0]] = [(reg, j) for (reg, j) in rd.get(w[0], []) if not self._cov(w, reg)]
                wr[w[0]].append((w, i))
            for r in o["reads"]:
                rd.setdefault(r[0], []).append((r, i))
        for o in self.ops:
            for j in o["deps"]:
                self.ops[j]["sig"] = True
        cnt, dcnt = {}, {}
        self.dma_keys = []
        for o in self.ops:
            if o["dma"]:
                k = o["key"]
                if k not in dcnt:
                    dcnt[k] = 0
                    self.dma_keys.append(k)
                dcnt[k] += 1
                o["seq"] = dcnt[k]
            elif o["fn"] is not None and o["sig"]:
                cnt[o["eng"]] = cnt.get(o["eng"], 0) + 1
                o["seq"] = cnt[o["eng"]]

    def emit(self, eng, h, sems, dsems):
        waited = {}
        for o in self.ops:
            if o["eng"] != eng:
                continue
            need = {}
            for j in sorted(o["deps"]):
                p = self.ops[j]
                if p["dma"]:
                    kk = ("d", p["key"])
                    need[kk] = max(need.get(kk, 0), 16 * p["seq"])
                elif p["fn"] is not None:
                    kk = ("e", p["eng"])
                    need[kk] = max(need.get(kk, 0), p["seq"])
            for kk, v in need.items():
                if waited.get(kk, 0) >= v:
                    continue
                waited[kk] = v
                h.wait_ge(dsems[kk[1]] if kk[0] == "d" else sems[kk[1]], v)
            if o["fn"] is None:
                continue
            ins = o["fn"](h)
            if o["dma"]:
                ins.then_inc(dsems[o["key"]], 16)
            elif o["sig"]:
                ins.then_inc(sems[o["eng"]], 1)


def R(space, lo, hi):
    return (space, int(lo), int(hi))


class Buf:
    def __init__(self, space, t, n, dt=None, boff=0, esz=4):
        self.space, self.n, self.boff, self.esz = space, n, boff, esz
        self.t = t if dt is None else t[:, boff // 4:(boff + n * esz) // 4].bitcast(dt)

    def ap(self, lo, hi, p0=0, p1=128):
        return self.t[p0:p1, lo:hi]

    def reg(self, lo, hi):
        return R(self.space, self.boff + lo * self.esz, self.boff + hi * self.esz)


import math

G2 = 16
NDG = 32
PI = math.pi


class Ctx:
    pass


def make_ctx(nc, es, S, debug, dbg_words):
    c = Ctx()
    c.nc, c.S = nc, S
    ARENA_BYTES = 209920
    c.ARENA_BYTES = ARENA_BYTES
    arena = es.enter_context(nc.sbuf_tensor("arena", [128, ARENA_BYTES // 4], F32))
    c.arena = arena

    def sb(name, n, dt, off):
        esz = 4 if dt == F32 else 2
        assert off % 4 == 0 and (n * esz) % 4 == 0 and off + n * esz <= ARENA_BYTES, (name, off, n)
        return Buf("A", arena, n, dt, off, esz)
    c.sb = sb
    c.psb = [Buf("ps%d" % i, es.enter_context(nc.psum_tensor("ps%d" % i, [128, 512], F32))[:, :], 512) for i in range(8)]
    pst = {"i": 0}

    def ps_next(grp=None):
        if grp is None:
            b = c.psb[pst["i"] % 8]
            pst["i"] += 1
            return b
        k = pst.setdefault(grp, 0)
        pst[grp] = k + 1
        return c.psb[(0 if grp == "a" else 4) + k % 4]
    c.ps_next = ps_next

    c.dbg = nc.dram_tensor("dbg", [128, dbg_words], F32, kind="ExternalOutput").ap() if debug else None
    dst = {"off": 0}

    def dump(name, buf, lo, hi):
        if not debug:
            return
        b0, b1 = buf.boff + lo * buf.esz, buf.boff + hi * buf.esz
        nw = (b1 - b0) // 4
        o = dst["off"]
        dst["off"] += nw
        assert dst["off"] <= dbg_words, name
        DBG_LAYOUT[name] = (o, nw, buf.esz)
        S.op("sp", lambda e: e.dma_start(out=c.dbg[:, o:o + nw], in_=arena[:, b0 // 4:b1 // 4]),
             reads=[R("A", b0, b1)], writes=[R("dram_dbg", o, o + nw)], dma=True, key=("dbg", name))
    c.dump = dump

    def _rd(*xs):
        return [x[1] for x in xs if isinstance(x, tuple)]

    def _a(x):
        return x[0] if isinstance(x, tuple) else x

    def tt(eng, out, in0, in1, op):
        S.op(eng, lambda e: e.tensor_tensor(out=out[0], in0=in0[0], in1=in1[0], op=op), reads=_rd(in0, in1), writes=[out[1]])

    def ts(eng, out, in0, s1, op0, s2=None, op1=None):
        if op1 is None:
            S.op(eng, lambda e: e.tensor_scalar(out=out[0], in0=in0[0], scalar1=_a(s1), scalar2=None, op0=op0), reads=_rd(in0, s1), writes=[out[1]])
        else:
            S.op(eng, lambda e: e.tensor_scalar(out=out[0], in0=in0[0], scalar1=_a(s1), scalar2=_a(s2), op0=op0, op1=op1), reads=_rd(in0, s1, s2), writes=[out[1]])

    def stt(eng, out, in0, sc, in1, op0, op1):
        S.op(eng, lambda e: e.scalar_tensor_tensor(out=out[0], in0=in0[0], scalar=_a(sc), in1=in1[0], op0=op0, op1=op1), reads=_rd(in0, sc, in1), writes=[out[1]])

    def act(out, in_, func, scale=None, bias=None, accum=None):
        kw = {}
        if scale is not None:
            kw["scale"] = _a(scale)
        if bias is not None:
            kw["bias"] = _a(bias)
        if accum is not None:
            kw["accum_out"] = accum[0]
        wr = [out[1]] + ([accum[1]] if accum is not None else [])
        S.op("act", lambda e: e.activation(out=out[0], in_=in_[0], func=func, **kw), reads=_rd(in_, scale, bias), writes=wr)

    def cp(eng, out, in_):
        if eng == "act":
            act(out, in_, AF.Copy)
        else:
            S.op(eng, lambda e: e.tensor_copy(out=out[0], in_=in_[0]), reads=_rd(in_), writes=[out[1]])

    def ms(eng, out, val):
        S.op(eng, lambda e: e.memset(out[0], val), writes=[out[1]])

    def dma(eng, out, in_ap, in_reg=None):
        S.op(eng, lambda e: e.dma_start(out=out[0], in_=in_ap), reads=([in_reg] if in_reg else []), writes=[out[1]], dma=True)
    c.tt, c.ts, c.stt, c.act, c.cp, c.ms, c.dma = tt, ts, stt, act, cp, ms, dma
    return c


def V(buf, lo, hi, pat=None, **kw):
    ap = buf.ap(lo, hi)
    if pat:
        ap = ap.rearrange(pat, **kw)
    return (ap, buf.reg(lo, hi))


def s5_precompute(c, din):
    sb, tt, ts, stt, act, cp, ms, dma = c.sb, c.tt, c.ts, c.stt, c.act, c.cp, c.ms, c.dma
    S = c.S
    o = {}
    BASE = 20480
    ET = sb("ET", 8192, F32, 20480)
    FT = sb("FT", 8192, F32, 53248)
    YP = sb("YP", 8192, F32, 86016)
    EP = sb("EP", 128 * 128, BF16, 118784)
    X1 = 151552
    MT = sb("MT", 32 * 128, BF16, 184320)
    SBI = sb("SBI", 1024, F32, 192512)
    SCI = sb("SCI", 1024, F32, 196608)
    o.update(EP=EP, MT=MT)
    sm_off = {"o": 200704}

    def small(n=NDG):
        b = sb("sm", n, F32, sm_off["o"])
        sm_off["o"] += n * 4
        return b
    tmp4 = [sb("tmp%d" % i, 512, F32, 16384 + i * 2048) for i in range(2)]

    s5sm = small(96)
    dma("sp", V(s5sm, 0, 96), din["s5sm"][:, :])
    dma("sp", V(SBI, 0, 1024), din["s5B"][:, :])
    dma("sp", V(SCI, 0, 1024), din["s5C"][:, :])
    are, aim, ldt = V(s5sm, 0, 32), V(s5sm, 32, 64), V(s5sm, 64, 96)
    hm = small(2)
    dma("sp", V(hm, 0, 2), din["hm"][:, :])
    dcol = small(32)
    dma("sp", V(dcol, 0, 32), din["dcol"][:, :])

    def sv():
        b = small()
        return V(b, 0, NDG)
    dt_, al, th, ea = sv(), sv(), sv(), sv()
    xq = sv()
    ts("dve", xq, ldt, 1.0 / 16.0, ALU.mult)
    fct = [1.0]
    for k_ in range(1, 11):
        fct.append(fct[-1] * k_)
    ms("dve", dt_, 1.0 / fct[10])
    for k_ in range(9, -1, -1):
        tt("dve", dt_, dt_, xq, ALU.mult)
        ts("dve", dt_, dt_, 1.0 / fct[k_], ALU.add)
    for _ in range(4):
        tt("dve", dt_, dt_, dt_, ALU.mult)
    tt("dve", al, are, dt_, ALU.mult)
    tt("dve", th, aim, dt_, ALU.mult)
    ms("dve", ea, 1.0 / 720.0)
    for cf in (1.0 / 120.0, 1.0 / 24.0, 1.0 / 6.0, 0.5, 1.0, 1.0):
        tt("dve", ea, ea, al, ALU.mult)
        ts("dve", ea, ea, cf, ALU.add)
    sn, cs, wk, mk = sv(), sv(), sv(), sv()
    w2 = sv()
    ts("dve", wk, th, 1.0 / 16.0, ALU.mult)
    tt("dve", w2, wk, wk, ALU.mult)
    fact = [1.0]
    for k_ in range(1, 17):
        fact.append(fact[-1] * k_)
    ms("dve", sn, -1.0 / fact[15])
    for k_ in range(6, -1, -1):
        tt("dve", sn, sn, w2, ALU.mult)
        ts("dve", sn, sn, ((-1.0) ** k_) / fact[2 * k_ + 1], ALU.add)
    tt("dve", sn, sn, wk, ALU.mult)
    ms("dve", cs, 1.0 / fact[16])
    for k_ in range(7, -1, -1):
        tt("dve", cs, cs, w2, ALU.mult)
        ts("dve", cs, cs, ((-1.0) ** k_) / fact[2 * k_], ALU.add)
    for _ in range(4):
        tt("dve", mk, sn, cs, ALU.mult)
        tt("dve", wk, cs, cs, ALU.mult)
        tt("dve", w2, sn, sn, ALU.mult)
        tt("dve", cs, wk, w2, ALU.subtract)
        ts("dve", sn, mk, 2.0, ALU.mult)
    PWr, PWi = sb("PWr", 512, F32, X1), sb("PWi", 512, F32, X1 + 2048)
    PBr, PBi = sb("PBr", 512, F32, X1 + 4096), sb("PBi", 512, F32, X1 + 6144)

    def pw(buf, k):
        return (buf.ap(0, NDG * 16).rearrange("p (n k) -> p n k", k=16)[:, :, k + 7], buf.reg(0, NDG * 16))
    t1, t2 = sv(), sv()

    def cmul(outr, outi, ar, ai, br, bi, eng="dve", ta=None, tb=None):
        ta = ta or t1
        tb = tb or t2
        tt(eng, ta, ar, br, ALU.mult)
        tt(eng, tb, ai, bi, ALU.mult)
        tt(eng, outr, ta, tb, ALU.subtract)
        tt(eng, ta, ar, bi, ALU.mult)
        tt(eng, tb, ai, br, ALU.mult)
        tt(eng, outi, ta, tb, ALU.add)
    ms("dve", pw(PWr, 0), 1.0)
    ms("dve", pw(PWi, 0), 0.0)
    tt("dve", pw(PWr, 1), ea, cs, ALU.mult)
    tt("dve", pw(PWi, 1), ea, sn, ALU.mult)
    for k in range(2, 9):
        cmul(pw(PWr, k), pw(PWi, k), pw(PWr, k - 1), pw(PWi, k - 1), pw(PWr, 1), pw(PWi, 1))
    e2, e21 = sv(), sv()
    tt("dve", e21, ea, ea, ALU.mult)
    S.op("dve", lambda e: e.reciprocal(out=e21[0], in_=e21[0]), reads=[e21[1]], writes=[e21[1]])
    cp("dve", e2, e21)
    for k in range(1, 8):
        if k > 1:
            tt("dve", e2, e2, e21, ALU.mult)
        tt("dve", pw(PWr, -k), pw(PWr, k), e2, ALU.mult)
        stt("dve", pw(PWi, -k), pw(PWi, k), -1.0, e2, ALU.mult, ALU.mult)
    nr, den, cr, ci = sv(), sv(), sv(), sv()
    ts("dve", nr, pw(PWr, 1), -1.0, ALU.add)
    tt("dve", den, are, are, ALU.mult)
    tt("dve", t1, aim, aim, ALU.mult)
    tt("dve", den, den, t1, ALU.add)
    S.op("dve", lambda e: e.reciprocal(out=den[0], in_=den[0]), reads=[den[1]], writes=[den[1]])
    tt("dve", t1, nr, are, ALU.mult)
    tt("dve", t2, pw(PWi, 1), aim, ALU.mult)
    tt("dve", cr, t1, t2, ALU.add)
    tt("dve", cr, cr, den, ALU.mult)
    tt("dve", t1, pw(PWi, 1), are, ALU.mult)
    tt("dve", t2, nr, aim, ALU.mult)
    tt("dve", ci, t1, t2, ALU.subtract)
    tt("dve", ci, ci, den, ALU.mult)
    pwr3 = (PWr.ap(0, 512).rearrange("p (n k) -> p n k", k=16), PWr.reg(0, 512))
    pwi3 = (PWi.ap(0, 512).rearrange("p (n k) -> p n k", k=16), PWi.reg(0, 512))
    pbr3 = (PBr.ap(0, 512).rearrange("p (n k) -> p n k", k=16), PBr.reg(0, 512))
    pbi3 = (PBi.ap(0, 512).rearrange("p (n k) -> p n k", k=16), PBi.reg(0, 512))
    crb = (cr[0].unsqueeze(2).to_broadcast([128, NDG, 16]), cr[1])
    cib = (ci[0].unsqueeze(2).to_broadcast([128, NDG, 16]), ci[1])
    ta3 = (tmp4[0].ap(0, 512).rearrange("p (n k) -> p n k", k=16), tmp4[0].reg(0, 512))
    tb3 = (tmp4[1].ap(0, 512).rearrange("p (n k) -> p n k", k=16), tmp4[1].reg(0, 512))
    cmul(pbr3, pbi3, pwr3, pwi3, crb, cib, ta=ta3, tb=tb3)

    PWrn = sb("PWrn", 512, F32, X1 + 8192)
    ts("dve", V(PWrn, 0, 512), V(PWr, 0, 512), -1.0, ALU.mult)
    def tab(buf, d, ri, i):
        base = d * 4096 + ri * 128 + i * 16
        full = buf.ap(d * 4096, (d + 1) * 4096).rearrange("p (g r x) -> p g r x", g=G2, r=2)
        return (full[:, :, ri, i * 16:(i + 1) * 16], buf.reg(d * 4096, (d + 1) * 4096))

    def pslot(buf, d, slot):
        v3 = buf.ap(0, 512).rearrange("p (n k) -> p n k", k=16)
        return (v3[:, d * G2:(d + 1) * G2, slot:slot + 1].to_broadcast([128, G2, 16]), buf.reg(0, 512))

    def bc(buf, part, d):
        lo = part * 512 + d * 256
        return (buf.ap(lo, lo + 256).rearrange("p (g x) -> p g x", g=G2), buf.reg(lo, lo + 256))
    tq = [(tmp4[j // 2].ap((j % 2) * 256, (j % 2) * 256 + 256).rearrange("p (g x) -> p g x", g=G2),
           tmp4[j // 2].reg((j % 2) * 256, (j % 2) * 256 + 256)) for j in range(4)]
    for d in range(2):
        for i in range(8):
            sE = (14 - i) if d == 0 else (7 + i)
            sF = (8 + i) if d == 0 else (15 - i)
            sY = i if d == 0 else (7 - i)
            Br, Bi = bc(SBI, 0, d), bc(SBI, 1, d)
            Cr, Ci = bc(SCI, 0, d), bc(SCI, 1, d)
            pr, pi_ = pslot(PBr, d, sE), pslot(PBi, d, sE)
            tt("dve", tq[0], pr, Br, ALU.mult)
            tt("dve", tq[1], pi_, Bi, ALU.mult)
            tt("dve", tab(ET, d, 0, i), tq[0], tq[1], ALU.subtract)
            tt("dve", tq[0], pr, Bi, ALU.mult)
            tt("dve", tq[1], pi_, Br, ALU.mult)
            tt("dve", tab(ET, d, 1, i), tq[0], tq[1], ALU.add)
            for (TB, sl, eng, qa, qb) in ((FT, sF, "dve", tq[0], tq[1]), (YP, sY, "dve", tq[2], tq[3])):
                pr, pi_, prn = pslot(PWr, d, sl), pslot(PWi, d, sl), pslot(PWrn, d, sl)
                tt(eng, qa, pr, Cr, ALU.mult)
                tt(eng, qb, pi_, Ci, ALU.mult)
                tt(eng, tab(TB, d, 0, i), qa, qb, ALU.subtract)
                tt(eng, qa, prn, Ci, ALU.mult)
                tt(eng, qb, pi_, Cr, ALU.mult)
                tt(eng, tab(TB, d, 1, i), qa, qb, ALU.subtract)
    o_ea = ea
    p8r, p8i = sv(), sv()
    cp("dve", p8r, pw(PWr, 8))
    cp("dve", p8i, pw(PWi, 8))
    c.dump("ET", ET, 0, 8192)
    c.dump("FT", FT, 0, 8192)
    c.dump("YP", YP, 0, 8192)

    identf = c.identf
    ms("pool", V(EP, 0, 128 * 128), 0.0)
    ep4 = EP.ap(0, 128 * 128).rearrange("p (n h x) -> p n h x", h=2, x=128)
    for n0 in range(0, 64, 4):
        pb = c.ps_next()

        def trE(e, pb=pb, n0=n0):
            ins = None
            for q in range(4):
                ins = e.transpose(out=pb.ap(q * 128, (q + 1) * 128), in_=ET.ap((n0 + q) * 128, (n0 + q + 1) * 128), identity=identf.ap(0, 128))
            return ins
        S.op("pe", trE, reads=[ET.reg(n0 * 128, (n0 + 4) * 128), identf.reg(0, 128)], writes=[pb.reg(0, 512)])
        p3 = pb.ap(0, 512).rearrange("p (q x) -> p q x", q=4)
        for gh in range(2):
            eng = "act"
            c.cp(eng, (ep4[:, n0:n0 + 4, gh, gh * 64:(gh + 1) * 64], EP.reg(n0 * 256, (n0 + 4) * 256)),
                 (p3[:, :, gh * 64:(gh + 1) * 64], pb.reg(0, 512)))
    c.dump("EP", EP, 0, 128 * 128)

    mkf, mkb = sb("mkf", 128, F32, 16384), sb("mkb", 128, F32, 16896)
    dma("sp", V(mkf, 0, 128), c.din["maskf"][:, :])
    dma("sp", V(mkb, 0, 128), c.din["maskb"][:, :])
    YPP = sb("YPP", 8192, F32, X1)
    m1 = sb("m1", 128, F32, 17408)
    m2 = sb("m2", 128, F32, 17920)
    for gh in range(2):
        act(V(YPP, 0, 8192), V(YP, 0, 8192), AF.Copy, scale=V(hm, gh, gh + 1))
        for g2 in range(G2):
            g = 2 * g2 + gh
            pb = c.ps_next()

            def mmM(e, pb=pb, g2=g2):
                ins = None
                for d in range(2):
                    for ri in range(2):
                        lo = ((d * G2 + g2) * 2 + ri) * 128
                        ins = e.matmul(pb.ap(d * 128, (d + 1) * 128), lhsT=ET.ap(lo, lo + 128), rhs=YPP.ap(lo, lo + 128),
                                       start=(ri == 0), stop=(ri == 1))
                return ins
            S.op("pe", mmM, reads=[ET.reg(0, 8192), YPP.reg(0, 8192)], writes=[pb.reg(0, 256)])
            tt("dve", V(m1, 0, 128), (pb.ap(0, 128), pb.reg(0, 128)), V(mkf, 0, 128), ALU.mult)
            tt("dve", V(m2, 0, 128), (pb.ap(128, 256), pb.reg(128, 256)), V(mkb, 0, 128), ALU.mult)
            tt("dve", V(m1, 0, 128), V(m1, 0, 128), V(m2, 0, 128), ALU.add)
            stt("dve", V(MT, g * 128, (g + 1) * 128), V(identf, 0, 128), V(dcol, g, g + 1), V(m1, 0, 128), ALU.mult, ALU.add)
    c.dump("MT", MT, 0, 32 * 128)

    FTb = sb("FTb", 64 * 128, BF16, 20480)
    act(V(FTb, 0, 8192), V(FT, 0, 8192), AF.Copy)
    rho8, rinv, ur, ui = sv(), sv(), sv(), sv()
    tt("dve", rho8, o_ea, o_ea, ALU.mult)
    tt("dve", rho8, rho8, rho8, ALU.mult)
    tt("dve", rho8, rho8, rho8, ALU.mult)
    S.op("dve", lambda e: e.reciprocal(out=rinv[0], in_=rho8[0]), reads=[rho8[1]], writes=[rinv[1]])
    tt("dve", ur, p8r, rinv, ALU.mult)
    stt("dve", ui, p8i, -1.0, rinv, ALU.mult, ALU.mult)
    nq1, nq2 = sv(), sv()

    def unit(xr, xi):
        tt("dve", nq1, xr, xr, ALU.mult)
        tt("dve", nq2, xi, xi, ALU.mult)
        tt("dve", nq1, nq1, nq2, ALU.add)
        ts("dve", nq1, nq1, -0.5, ALU.mult, 1.5, ALU.add)
        tt("dve", xr, xr, nq1, ALU.mult)
        tt("dve", xi, xi, nq1, ALU.mult)
    unit(ur, ui)
    NJ = 256
    TRf = sb("TRf", NDG * NJ, F32, 53248)
    TIf = sb("TIf", NDG * NJ, F32, 53248 + NDG * NJ * 4)
    tr3 = TRf.ap(0, NDG * NJ).rearrange("p (n j) -> p n j", j=NJ)
    ti3 = TIf.ap(0, NDG * NJ).rearrange("p (n j) -> p n j", j=NJ)
    rR, rI = TRf.reg(0, NDG * NJ), TIf.reg(0, NDG * NJ)
    ms("dve", (tr3[:, :, 0:1], rR), 1.0)
    ms("dve", (ti3[:, :, 0:1], rI), 0.0)
    upr, upi = ur, ui
    upows = [(ur, ui)]
    ta_b = sb("tdA", NDG * 128, F32, X1)
    tb_b = sb("tdB", NDG * 128, F32, X1 + NDG * 128 * 4)
    for s_ in range(8):
        b = 1 << s_
        ta = (ta_b.ap(0, NDG * b).rearrange("p (n j) -> p n j", j=b), ta_b.reg(0, NDG * b))
        tb = (tb_b.ap(0, NDG * b).rearrange("p (n j) -> p n j", j=b), tb_b.reg(0, NDG * b))
        ubr = (upr[0].unsqueeze(2).to_broadcast([128, NDG, b]), upr[1])
        ubi = (upi[0].unsqueeze(2).to_broadcast([128, NDG, b]), upi[1])
        cmul((tr3[:, :, b:2 * b], rR), (ti3[:, :, b:2 * b], rI), (tr3[:, :, 0:b], rR), (ti3[:, :, 0:b], rI), ubr, ubi, ta=ta, tb=tb)
        nr_, ni_ = sv(), sv()
        cmul(nr_, ni_, upr, upi, upr, upi)
        unit(nr_, ni_)
        upr, upi = nr_, ni_
        upows.append((nr_, ni_))
    TB = sb("TB", 2 * NDG * NJ, BF16, X1)
    act(V(TB, 0, NDG * NJ), V(TRf, 0, NDG * NJ), AF.Copy)
    act(V(TB, NDG * NJ, 2 * NDG * NJ), V(TIf, 0, NDG * NJ), AF.Copy)
    o.update(FTb=FTb, TB=TB, rho8=rho8, u256=(upr, upi), u1=(ur, ui), al=al, ea=ea, upows=upows, hm=hm, small=small, sv=sv, cmul=cmul, t12=(t1, t2))
    c.dump("TRf", TRf, 0, NDG * NJ)
    c.dump("TIf", TIf, 0, NDG * NJ)
    return o


def mod_phase(c):
    S, sb, tt, ts, stt, act, cp, ms, dma = c.S, c.sb, c.tt, c.ts, c.stt, c.act, c.cp, c.ms, c.dma
    din = c.din
    ccs = sb("ccs", KC * 64, F32, 98304)
    ccb = sb("ccb", KC * 64, BF16, 100352)
    modrow = sb("modrow", 6 * D, F32, 53248)
    NWM = 2
    wmb = sb("wmb", NWM * KC * 512, BF16, 77824)
    dma("sp", V(ccs, 0, KC * 64), din["cc"][:, :])
    act(V(ccb, 0, KC * 64), V(ccs, 0, KC * 64), AF.Silu)
    wmod_v = din["w_mod"].rearrange("(k p) n -> p k n", p=128)
    bmrow = sb("bmrow", 6 * D, BF16, 102400)
    osel = sb("osel", 64, BF16, 101376)
    dma("pool", (bmrow.ap(0, 6 * D, 0, 1), bmrow.reg(0, 6 * D)), din["b_mod"][0:1, :])
    ms("pool", (osel.ap(0, 64, 0, 1), osel.reg(0, 64)), 0.0)
    ms("pool", (osel.ap(0, 1, 0, 1), osel.reg(0, 64)), 1.0)
    ms("pool", (osel.ap(32, 33, 0, 1), osel.reg(0, 64)), 1.0)
    for nb in range(12):
        sl = nb % NWM
        wlo, whi = sl * KC * 512, (sl + 1) * KC * 512
        wv = wmb.ap(wlo, whi).rearrange("p (k n) -> p k n", k=KC)
        dma("pool", (wv, wmb.reg(wlo, whi)), wmod_v[:, :, nb * 512:(nb + 1) * 512])
        pb = c.ps_next()

        def mm_mod(e, wv=wv, pb=pb, nb=nb):
            ins = None
            for kc in range(KC):
                ins = e.matmul(pb.ap(0, 512, 0, 64), lhsT=ccb.ap(kc * 64, (kc + 1) * 64), rhs=wv[:, kc, :], start=(kc == 0), stop=False)
            ins = e.matmul(pb.ap(0, 512, 0, 64), lhsT=osel.ap(0, 64, 0, 1), rhs=bmrow.ap(nb * 512, (nb + 1) * 512, 0, 1), start=False, stop=True)
            return ins
        S.op("pe", mm_mod, reads=[ccb.reg(0, KC * 64), wmb.reg(wlo, whi), osel.reg(0, 64), bmrow.reg(0, 6 * D)], writes=[pb.reg(0, 512)])
        act((modrow.ap(nb * 512, (nb + 1) * 512, 0, 64), modrow.reg(nb * 512, (nb + 1) * 512)), (pb.ap(0, 512, 0, 64), pb.reg(0, 512)), AF.Copy)
    modcol = sb("modcol", 4 * KC * 2, F32, 1536)
    for qi, q in enumerate((0, 1, 3, 4)):
        pb = c.ps_next()

        def tr_mod(e, pb=pb, q=q):
            ins = None
            for kc in range(KC):
                ins = e.transpose(out=pb.ap(kc * 64, (kc + 1) * 64), in_=modrow.ap(q * D + kc * 128, q * D + (kc + 1) * 128, 0, 64),
                                  identity=c.identf.ap(0, 64, 0, 64))
            return ins
        S.op("pe", tr_mod, reads=[modrow.reg(q * D, (q + 1) * D), c.identf.reg(0, 128)], writes=[pb.reg(0, 512)])
        act((modcol.ap(qi * KC * 2, (qi + 1) * KC * 2).rearrange("p (k c) -> p k c", k=KC), modcol.reg(qi * KC * 2, (qi + 1) * KC * 2)),
            (pb.ap(0, 512).rearrange("p (k c) -> p k c", k=KC)[:, :, 0:33:32], pb.reg(0, 512)), AF.Copy)
    gbc = sb("gbc", 2 * D, F32, 8192)
    for gi, q in enumerate((2, 5)):
        for hf in range(2):
            pb = c.ps_next()
            lo = q * D + hf * 512
            S.op("pe", lambda e, pb=pb, lo=lo: e.matmul(pb.ap(0, 512), lhsT=c.onesf.ap(0, 128, 0, 1), rhs=modrow.ap(lo, lo + 512, 0, 1), start=True, stop=True),
                 reads=[c.onesf.reg(0, 128), modrow.reg(lo, lo + 512)], writes=[pb.reg(0, 512)])
            glo = gi * D + hf * 512
            act(V(gbc, glo, glo + 512), (pb.ap(0, 512), pb.reg(0, 512)), AF.Copy)
    cols = sb("cols", 8 * KC, F32, 1280)
    n1c, n2c = V(cols, 0, KC), V(cols, KC, 2 * KC)
    dma("sp", n1c, din["n1col"][:, :])
    dma("sp", n2c, din["n2col"][:, :])
    mc = modcol.ap(0, 4 * KC * 2).rearrange("p (q k c) -> p q k c", q=4, k=KC)
    mr = modcol.reg(0, 4 * KC * 2)
    m = {"gbc": gbc}
    names = ["gs1", "sh1", "gs2", "sh2", "cgs1", "csh1"]
    for i, nm in enumerate(names):
        m[nm] = Buf("A", c.arena, KC, F32, 1280 + (2 + i) * KC * 4, 4)
    def finish():
        stt("dve", V(m["gs1"], 0, KC), (mc[:, 1, :, 0], mr), 1.0, n1c, ALU.add, ALU.mult)
        cp("dve", V(m["sh1"], 0, KC), (mc[:, 0, :, 0], mr))
        stt("dve", V(m["gs2"], 0, KC), (mc[:, 3, :, 0], mr), 1.0, n2c, ALU.add, ALU.mult)
        cp("dve", V(m["sh2"], 0, KC), (mc[:, 2, :, 0], mr))
        stt("dve", V(m["cgs1"], 0, KC), (mc[:, 1, :, 1], mr), 1.0, n1c, ALU.add, ALU.mult)
        cp("dve", V(m["csh1"], 0, KC), (mc[:, 0, :, 1], mr))
    m["finish"] = finish
    gf = sb("gf", D, F32, 4096)
    dma("sp", V(gf, 0, D), din["final_norm_g"].partition_broadcast(128).rearrange("p o n -> p (o n)"))
    m["gf"] = gf
    return m


def make_front(c, xs, xnb, junk, evac="dve", psg=None):
    S, ts, act = c.S, c.ts, c.act
    stc = {"i": 0}

    def front(src, row0, ntok, gsc, shc, hdst, hoff, hlen):
        for st_ in front_steps(src, row0, ntok, gsc, shc, hdst, hoff, hlen):
            st_()

    def front_steps(src, row0, ntok, gsc, shc, hdst, hoff, hlen, plain=False):
        hv = hdst.ap(0, KC * hlen).rearrange("p (k t) -> p k t", k=KC)
        ntile = ntok // 128
        steps = []
        for t0 in range(0, ntile, 2):
            grp = list(range(t0, min(t0 + 2, ntile)))
            info = {}
            steps.append(lambda grp=grp, info=info: pairA(src, row0, hdst, hoff, hlen, hv, grp, info))
            steps.append(lambda grp=grp, info=info: pairB(gsc, shc, hdst, hoff, hlen, hv, grp, info, plain))
        return steps

    def pairA(src, row0, hdst, hoff, hlen, hv, grp, info):
        if True:
            for t in grp:
                sl = t % 2
                sbase = (stc["i"] % 32) * 4
                stc["i"] += 1
                xa = V(xs, sl * D, (sl + 1) * D)
                c.dma("sp", xa, src[row0 + t * 128:row0 + (t + 1) * 128, :])
                info[t] = (sl, xa, V(c.st, sbase, sbase + 1), V(c.st, sbase + 1, sbase + 2))
            for t in grp:
                sl, xa, ss, s2 = info[t]
                act((hv[:, :, hoff + t * 128:hoff + (t + 1) * 128], hdst.reg(0, KC * hlen)),
                    (xa[0].rearrange("p (k x) -> p k x", k=KC), xa[1]), AF.Square, accum=ss)
            for t in grp:
                sl, xa, ss, s2 = info[t]
                act(s2, ss, AF.Ln, scale=1.0 / D, bias=c.epsc)
            for t in grp:
                sl, xa, ss, s2 = info[t]
                act(s2, s2, AF.Exp, scale=-0.5)
            for t in grp:
                sl, xa, ss, s2 = info[t]
                act(V(xnb, sl * D, (sl + 1) * D), xa, AF.Copy, scale=s2)

    def pairB(gsc, shc, hdst, hoff, hlen, hv, grp, info, plain=False):
        if True:
            for t in grp:
                sl, xa, ss, s2 = info[t]
                pb = c.ps_next(psg)
                pbb = pb.t[:, :].bitcast(BF16)

                def tr_x(e, pbb=pbb, sl=sl):
                    ins = None
                    for kc in range(KC):
                        ins = e.transpose(out=pbb[:, kc * 128:(kc + 1) * 128], in_=xnb.ap(sl * D + kc * 128, sl * D + (kc + 1) * 128), identity=c.identb.ap(0, 128))
                    return ins
                S.op("pe", tr_x, reads=[xnb.reg(sl * D, (sl + 1) * D), c.identb.reg(0, 128)], writes=[pb.reg(0, 512)])
                if plain:
                    c.cp("act", (hv[:, :, hoff + t * 128:hoff + (t + 1) * 128], hdst.reg(0, KC * hlen)),
                         (pbb[:, 0:1024].rearrange("p (k x) -> p k x", k=KC), pb.reg(0, 512)))
                    continue
                for kc in range(KC):
                    lo = kc * hlen + hoff + t * 128
                    ts("dve", (hv[:, kc, hoff + t * 128:hoff + (t + 1) * 128], hdst.reg(lo, lo + 128)),
                       (pbb[:, kc * 128:(kc + 1) * 128], pb.reg(0, 512)), V(gsc, kc, kc + 1), ALU.mult, V(shc, kc, kc + 1), ALU.add)
    front.steps = front_steps
    return front


def backend(c, mod, yT, own_x, x1d, use_s5=True):
    S, sb, tt, ts, stt, act, cp, ms, dma = c.S, c.sb, c.tt, c.ts, c.stt, c.act, c.cp, c.ms, c.dma
    din = c.din
    wG = sb("wG", KC * 2048, BF16, 20480)
    wB = sb("wB", KC * 512, BF16, 53248)
    wbb = sb("wbb", 4 * D, BF16, 61440)
    wba = sb("wba", 4 * D, BF16, 69632)
    wo = sb("wo", KC * D, BF16, 77824)
    wglu = sb("wglu", 4 * 512, BF16, 94208)
    poolw = sb("poolw", 4 * 128, BF16, 98304)
    Bmat = sb("Bmat", 4 * 128, BF16, 99328)
    invc = sb("invc", 4 * 128, F32, 100352)
    sc2 = sb("sc2", 8, F32, 102400)
    hTc = sb("hTc", KC * 512, BF16, 102912)
    UBc = sb("UBc", 4 * 512, BF16, 111104)
    PM = sb("PM", 4 * 512, BF16, 115200)
    ZT = sb("ZT", 4 * 512, BF16, 119296)
    zT = sb("zT", 4 * 512, BF16, 123392)
    sg = sb("sg", 2 * 512, BF16, 127488)
    tmpf = sb("tmpf", 2 * 512, F32, 129536)
    MGc = sb("MGc", KC * 512, BF16, 151552)
    xs = sb("xsb", 2 * D, F32, 159744)
    xnb = sb("xnbb", 2 * D, BF16, 167936)
    junk = sb("junkb", D, BF16, 188416)
    xr = sb("xr", 2 * D, F32, 172032)
    x1o = sb("x1o", 2 * D, F32, 180224)
    front = make_front(c, xs, xnb, junk, evac="dve")
    w_in_v = din["w_in"].rearrange("(k p) n -> p k n", p=128)
    dma("pool", V(wB, 0, KC * 512, "p (k n) -> p k n", k=KC), w_in_v[:, :, 512:1024])
    dma("pool", V(wG, 0, KC * 2048, "p (k n) -> p k n", k=KC), w_in_v[:, :, 1024:3072])
    dma("pool", V(wbb, 0, 4 * D, "p (k n) -> p k n", k=4), din["w_branch_b"].rearrange("(k p) n -> p k n", p=128))
    dma("pool", V(wba, 0, 4 * D, "p (k n) -> p k n", k=4), din["w_branch_a"].rearrange("(k p) n -> p k n", p=128))
    dma("pool", V(wglu, 0, 4 * 512, "p (k n) -> p k n", k=4), din["w_glu"].rearrange("(k p) n -> p k n", p=128))
    dma("pool", V(poolw, 0, 4 * 128, "p (k n) -> p k n", k=4), din["pool_w"].rearrange("j c n -> c j n"))
    dma("pool", V(Bmat, 0, 4 * 128, "p (k n) -> p k n", k=4), din["poolB"].rearrange("j c n -> c j n"))
    dma("pool", V(wo, 0, KC * D, "p (k n) -> p k n", k=KC), din["w_out"].rearrange("(k p) n -> p k n", p=128))
    dma("sp", V(invc, 0, 512), din["invc"].partition_broadcast(128).rearrange("p o n -> p (o n)"))
    dma("sp", V(sc2, 0, 8), din["sc2"][:, :])
    wG3 = wG.ap(0, KC * 2048).rearrange("p (k n) -> p k n", k=KC)
    wB3 = wB.ap(0, KC * 512).rearrange("p (k n) -> p k n", k=KC)
    wbb3 = wbb.ap(0, 4 * D).rearrange("p (k n) -> p k n", k=4)
    wba3 = wba.ap(0, 4 * D).rearrange("p (k n) -> p k n", k=4)
    wo3 = wo.ap(0, KC * D).rearrange("p (k n) -> p k n", k=KC)
    wglu3 = wglu.ap(0, 4 * 512).rearrange("p (k n) -> p k n", k=4)
    pw3 = poolw.ap(0, 512).rearrange("p (k n) -> p k n", k=4)
    bm3 = Bmat.ap(0, 512).rearrange("p (k n) -> p k n", k=4)
    hv = hTc.ap(0, KC * 512).rearrange("p (k t) -> p k t", k=KC)
    ub3 = UBc.ap(0, 4 * 512).rearrange("p (q n) -> p q n", q=4)
    pm3 = PM.ap(0, 4 * 512).rearrange("p (j t) -> p j t", j=4)
    zt3 = ZT.ap(0, 4 * 512).rearrange("p (j t) -> p j t", j=4)
    zz3 = zT.ap(0, 4 * 512).rearrange("p (j t) -> p j t", j=4)
    mg3 = MGc.ap(0, KC * 512).rearrange("p (m t) -> p m t", m=KC)
    yT3 = yT.ap(0, 4 * NTOK).rearrange("p (q t) -> p q t", q=4) if yT is not None else None
    gbc = mod["gbc"]
    pend = []
    for st_ in front.steps(own_x, 0, 512, mod["gs1"], mod["sh1"], hTc, 0, 512):
        st_()
    for n in range(NT // 4):
        for q in range(4):
            pb = c.ps_next()

            def mm(e, pb=pb, q=q):
                ins = None
                for kc in range(KC):
                    ins = e.matmul(pb.ap(0, 512), lhsT=hv[:, kc, q * 128:(q + 1) * 128], rhs=wB3[:, kc, :], start=(kc == 0), stop=(kc == KC - 1))
                return ins
            S.op("pe", mm, reads=[hTc.reg(0, KC * 512), wB.reg(0, KC * 512)], writes=[pb.reg(0, 512)])
            cp("act", (ub3[:, q, :], UBc.reg(q * 512, (q + 1) * 512)), (pb.ap(0, 512), pb.reg(0, 512)))
        for q in range(4):
            pb = c.ps_next()

            def mmP(e, pb=pb, q=q):
                ins = None
                for jw in range(4):
                    ins = e.matmul(pb.ap(jw * 128, (jw + 1) * 128), lhsT=ub3[:, q, jw * 128:(jw + 1) * 128], rhs=bm3[:, jw, :], start=True, stop=True)
                return ins
            S.op("pe", mmP, reads=[UBc.reg(q * 512, (q + 1) * 512), Bmat.reg(0, 512)], writes=[pb.reg(0, 512)])
            tt("dve", (pm3[:, :, q * 128:(q + 1) * 128], PM.reg(0, 4 * 512)), (pb.ap(0, 512).rearrange("p (j t) -> p j t", j=4), pb.reg(0, 512)),
               V(invc, 0, 512, "p (j t) -> p j t", j=4), ALU.mult)
        for jw in range(4):
            pb = c.ps_next()
            S.op("pe", lambda e, pb=pb, jw=jw: e.matmul(pb.ap(0, 512), lhsT=pw3[:, jw, :], rhs=pm3[:, jw, :], start=True, stop=True),
                 reads=[poolw.reg(0, 512), PM.reg(jw * 512, (jw + 1) * 512)], writes=[pb.reg(0, 512)])
            ts("dve", (zt3[:, jw, :], ZT.reg(jw * 512, (jw + 1) * 512)), (pb.ap(0, 512), pb.reg(0, 512)), V(sc2, jw, jw + 1), ALU.mult)
        if use_s5:
            for m4 in range(4):
                pb = c.ps_next()

                def mmG(e, pb=pb, m4=m4, n=n):
                    ins = None
                    for q4 in range(4):
                        ins = e.matmul(pb.ap(0, 512), lhsT=wglu3[:, q4, m4 * 128:(m4 + 1) * 128], rhs=yT3[:, q4, n * 512:(n + 1) * 512], start=(q4 == 0), stop=(q4 == 3))
                    return ins
                S.op("pe", mmG, reads=[wglu.reg(0, 2048), yT.reg(0, 4 * NTOK)], writes=[pb.reg(0, 512)])
                so = (m4 % 2) * 512
                act(V(sg, so, so + 512), (pb.ap(0, 512), pb.reg(0, 512)), AF.Sigmoid, bias=V(sc2, 4 + m4, 5 + m4))
                tt("dve", (zz3[:, m4, :], zT.reg(m4 * 512, (m4 + 1) * 512)), (yT3[:, m4, n * 512:(n + 1) * 512], yT.reg(0, 4 * NTOK)), V(sg, so, so + 512), ALU.mult)
        for m in range(KC):
            pbs = {}
            if use_s5:
                pa, pga = c.ps_next(), c.ps_next()

                def mmA(e, pa=pa, pga=pga, m=m):
                    ins = None
                    for q4 in range(4):
                        ins = e.matmul(pa.ap(0, 512), lhsT=wba3[:, q4, m * 128:(m + 1) * 128], rhs=zz3[:, q4, :], start=(q4 == 0), stop=(q4 == 3))
                    for kc in range(KC):
                        ins = e.matmul(pga.ap(0, 512), lhsT=wG3[:, kc, m * 128:(m + 1) * 128], rhs=hv[:, kc, :], start=(kc == 0), stop=(kc == KC - 1))
                    return ins
                S.op("pe", mmA, reads=[wba.reg(0, 4 * D), zT.reg(0, 2048), wG.reg(0, KC * 2048), hTc.reg(0, KC * 512)], writes=[pa.reg(0, 512), pga.reg(0, 512)])
                act(V(sg, 0, 512), (pga.ap(0, 512), pga.reg(0, 512)), AF.Sigmoid)
                tt("dve", V(tmpf, 0, 512), (pa.ap(0, 512), pa.reg(0, 512)), V(sg, 0, 512), ALU.mult)
            pbb_, pgb = c.ps_next(), c.ps_next()

            def mmB(e, pbb_=pbb_, pgb=pgb, m=m):
                ins = None
                for jw in range(4):
                    ins = e.matmul(pbb_.ap(0, 512), lhsT=wbb3[:, jw, m * 128:(m + 1) * 128], rhs=zt3[:, jw, :], start=(jw == 0), stop=(jw == 3))
                for kc in range(KC):
                    ins = e.matmul(pgb.ap(0, 512), lhsT=wG3[:, kc, 1024 + m * 128:1024 + (m + 1) * 128], rhs=hv[:, kc, :], start=(kc == 0), stop=(kc == KC - 1))
                return ins
            S.op("pe", mmB, reads=[wbb.reg(0, 4 * D), ZT.reg(0, 2048), wG.reg(0, KC * 2048), hTc.reg(0, KC * 512)], writes=[pbb_.reg(0, 512), pgb.reg(0, 512)])
            act(V(sg, 512, 1024), (pgb.ap(0, 512), pgb.reg(0, 512)), AF.Sigmoid)
            if use_s5:
                tt("dve", V(tmpf, 512, 1024), (pbb_.ap(0, 512), pbb_.reg(0, 512)), V(sg, 512, 1024), ALU.mult)
                tt("pool", (mg3[:, m, :], MGc.reg(m * 512, (m + 1) * 512)), V(tmpf, 0, 512), V(tmpf, 512, 1024), ALU.add)
            else:
                tt("dve", (mg3[:, m, :], MGc.reg(m * 512, (m + 1) * 512)), (pbb_.ap(0, 512), pbb_.reg(0, 512)), V(sg, 512, 1024), ALU.mult)
        pend = front.steps(own_x, (n + 1) * 512, 512, mod["gs1"], mod["sh1"], hTc, 0, 512) if n + 1 < NT // 4 else []
        for q in range(4):
            if pend:
                pend.pop(0)()
            tt_ = n * 4 + q
            sl = tt_ % 2
            xa = V(xr, sl * D, (sl + 1) * D)
            dma("sp", xa, own_x[tt_ * 128:(tt_ + 1) * 128, :])
            for hf in range(2):
                pb = c.ps_next()

                def mmO(e, pb=pb, q=q, hf=hf):
                    ins = None
                    for m in range(KC):
                        ins = e.matmul(pb.ap(0, 512), lhsT=mg3[:, m, q * 128:(q + 1) * 128], rhs=wo3[:, m, hf * 512:(hf + 1) * 512], start=(m == 0), stop=(m == KC - 1))
                    return ins
                S.op("pe", mmO, reads=[MGc.reg(0, KC * 512), wo.reg(0, KC * D)], writes=[pb.reg(0, 512)])
                lo = sl * D + hf * 512
                tt("dve", V(x1o, lo, lo + 512), (pb.ap(0, 512), pb.reg(0, 512)), V(gbc, hf * 512, (hf + 1) * 512), ALU.mult)
                tt("pool", V(x1o, lo, lo + 512), V(x1o, lo, lo + 512), V(xr, lo, lo + 512), ALU.add)
            S.op("pool", lambda e, tt_=tt_, sl=sl: e.dma_start(out=x1d[tt_ * 128:(tt_ + 1) * 128, :], in_=x1o.ap(sl * D, (sl + 1) * D)),
                 reads=[x1o.reg(sl * D, (sl + 1) * D)], writes=[R("dram_x1", tt_ * 128, (tt_ + 1) * 128)], dma=True, key=("x1st", sl))


def ffn_phase(c, mod, x1d, out):
    S, sb, tt, ts, stt, act, cp, ms, dma = c.S, c.sb, c.tt, c.ts, c.stt, c.act, c.cp, c.ms, c.dma
    din = c.din
    wfi = sb("wfi", KC * 2 * FH, BF16, 20480)
    wfo = sb("wfo", HT * D, BF16, 110592)
    actT = sb("actT", HT * 512, BF16, 155648)
    h2T = sb("h2T", KC * 512, BF16, 178176)
    x1s = sb("x1s", 4 * D, F32, 186368)
    xnb = sb("xnbf", D, BF16, 202752)
    sgt = sb("sgt", 2 * 512, BF16, 204800)
    junk = sb("junkf", D, BF16, 206848)
    yos = [sb("yo", D, F32, 8192), sb("yo2", D, F32, 16384)]
    gbc, gf, gs2, sh2c = mod["gbc"], mod["gf"], mod["gs2"], mod["sh2"]
    wfi_v = din["w_ffn_in"].rearrange("(k p) n -> p k n", p=128)
    wfi3w = wfi.ap(0, KC * 2 * FH).rearrange("p (k n) -> p k n", k=KC)
    HB = 2
    wfi_regs = {}
    for h0 in range(0, HT, HB):
        for half in range(2):
            c0 = half * FH + h0 * 128
            c1 = c0 + HB * 128
            S.op("pool", lambda e, c0=c0, c1=c1: e.dma_start(out=wfi3w[:, :, c0:c1], in_=wfi_v[:, :, c0:c1]),
                 writes=[R("A", wfi.boff + (kc * 2 * FH + c0) * 2, wfi.boff + (kc * 2 * FH + c1) * 2) for kc in range(KC)],
                 dma=True, key=("wfi", h0, half))
    wfo_v = din["w_ffn_out"].rearrange("(k p) n -> p k n", p=128)
    for h0 in range(0, HT, 11):
        lo, hi = h0 * D, (h0 + 11) * D
        dma("pool", V(wfo, lo, hi, "p (k n) -> p k n", k=11), wfo_v[:, h0:h0 + 11, :])
    wfi3 = wfi.ap(0, KC * 2 * FH).rearrange("p (k n) -> p k n", k=KC)
    wfo3 = wfo.ap(0, HT * D).rearrange("p (k n) -> p k n", k=HT)
    st = c.st
    stc = {"i": 0}

    def rstd_ops(xa):
        base = (stc["i"] % 32) * 4
        stc["i"] += 1
        ss, s2 = V(st, base, base + 1), V(st, base + 1, base + 2)
        act(V(junk, 0, D), xa, AF.Square, accum=ss)
        act(s2, ss, AF.Ln, scale=1.0 / D, bias=c.epsc)
        act(s2, s2, AF.Exp, scale=-0.5)
        return s2

    def load_x1(tt_, sl):
        S.op("sp", lambda e: e.dma_start(out=x1s.ap(sl * D, (sl + 1) * D), in_=x1d[tt_ * 128:(tt_ + 1) * 128, :]),
             reads=[R("dram_x1", tt_ * 128, (tt_ + 1) * 128)], writes=[x1s.reg(sl * D, (sl + 1) * D)], dma=True)
    h2v = h2T.ap(0, KC * 512).rearrange("p (k t) -> p k t", k=KC)

    def front_a(n, q):
        tt_ = n * 4 + q
        sl = tt_ % 2
        load_x1(tt_, sl)
        xa = V(x1s, sl * D, (sl + 1) * D)
        rs = rstd_ops(xa)
        act(V(xnb, 0, D), xa, AF.Copy, scale=rs)

    def front_b(n, q):
        pb = c.ps_next()
        pbb = pb.t[:, :].bitcast(BF16)

        def tr_x(e, pbb=pbb):
            ins = None
            for kc in range(KC):
                ins = e.transpose(out=pbb[:, kc * 128:(kc + 1) * 128], in_=xnb.ap(kc * 128, (kc + 1) * 128), identity=c.identb.ap(0, 128))
            return ins
        S.op("pe", tr_x, reads=[xnb.reg(0, D), c.identb.reg(0, 128)], writes=[pb.reg(0, 512)])
        for kc in range(KC):
            ts("dve", (h2v[:, kc, q * 128:(q + 1) * 128], h2T.reg(kc * 512 + q * 128, kc * 512 + (q + 1) * 128)),
               (pbb[:, kc * 128:(kc + 1) * 128], pb.reg(0, 512)), V(gs2, kc, kc + 1), ALU.mult, V(sh2c, kc, kc + 1), ALU.add)

    def front_tile(n, q):
        front_a(n, q)
        front_b(n, q)

    def hidden(n):
        for hh in range(HT):
            pg, pu = c.ps_next(), c.ps_next()

            def mm_gu(e, pg=pg, pu=pu, hh=hh):
                ins = None
                for kc in range(KC):
                    ins = e.matmul(pg.ap(0, 512), lhsT=wfi3[:, kc, hh * 128:(hh + 1) * 128], rhs=h2v[:, kc, :], start=(kc == 0), stop=(kc == KC - 1))
                for kc in range(KC):
                    ins = e.matmul(pu.ap(0, 512), lhsT=wfi3[:, kc, FH + hh * 128:FH + (hh + 1) * 128], rhs=h2v[:, kc, :], start=(kc == 0), stop=(kc == KC - 1))
                return ins
            rd = [h2T.reg(0, KC * 512)]
            for kc in range(KC):
                for half in range(2):
                    c0 = half * FH + hh * 128
                    rd.append(R("A", wfi.boff + (kc * 2 * FH + c0) * 2, wfi.boff + (kc * 2 * FH + c0 + 128) * 2))
            S.op("pe", mm_gu, reads=rd, writes=[pg.reg(0, 512), pu.reg(0, 512)])
            so = (hh % 2) * 512
            act(V(sgt, so, so + 512), (pg.ap(0, 512), pg.reg(0, 512)), AF.Silu)
            tt("dve", V(actT, hh * 512, (hh + 1) * 512), (pu.ap(0, 512), pu.reg(0, 512)), V(sgt, so, so + 512), ALU.mult)

    def tail_tile(n, q):
        tt_ = n * 4 + q
        yo = yos[tt_ % 2]
        sl = 2 + tt_ % 2
        load_x1(tt_, sl)
        xa = V(x1s, sl * D, (sl + 1) * D)
        for hf in range(2):
            po = c.ps_next()

            def mm_o(e, po=po, q=q, hf=hf):
                ins = None
                for hh in range(HT):
                    ins = e.matmul(po.ap(0, 512), lhsT=actT.ap(hh * 512 + q * 128, hh * 512 + (q + 1) * 128), rhs=wfo3[:, hh, hf * 512:(hf + 1) * 512],
                                   start=(hh == 0), stop=(hh == HT - 1))
                return ins
            S.op("pe", mm_o, reads=[actT.reg(0, HT * 512), wfo.reg(0, HT * D)], writes=[po.reg(0, 512)])
            tt("dve", V(yo, hf * 512, (hf + 1) * 512), (po.ap(0, 512), po.reg(0, 512)), V(gbc, D + hf * 512, D + (hf + 1) * 512), ALU.mult)
        tt("pool", V(yo, 0, D), V(yo, 0, D), xa, ALU.add)
        rs = rstd_ops(V(yo, 0, D))
        stt("dve", V(yo, 0, D), V(yo, 0, D), rs, V(gf, 0, D), ALU.mult, ALU.mult)
        S.op("pool", lambda e, tt_=tt_, yo=yo: e.dma_start(out=out[tt_ * 128:(tt_ + 1) * 128, :], in_=yo.ap(0, D)),
             reads=[yo.reg(0, D)], writes=[R("dram_out", tt_ * 128, (tt_ + 1) * 128)], dma=True, key=("st", tt_ % 2))

    NCH = NT // 4
    for q in range(4):
        front_tile(0, q)
    for n in range(NCH):
        hidden(n)
        for q in range(4):
            if n + 1 < NCH:
                front_a(n + 1, q)
            tail_tile(n, q)
            if n + 1 < NCH:
                front_b(n + 1, q)


def s5_run(c, s5, mod, xb_rows, ctx_rows, seg_of_slot, own_x):
    S, sb, tt, ts, stt, act, cp, ms, dma = c.S, c.sb, c.tt, c.ts, c.stt, c.act, c.cp, c.ms, c.dma
    din = c.din
    EP, FTb, MT, TB, rho8 = s5["EP"], s5["FTb"], s5["MT"], s5["TB"], s5["rho8"]
    cmul = s5["cmul"]
    p2 = {"o": 16384}

    def small(n=NDG):
        assert p2["o"] + n * 4 <= 20480, "small pool overflow"
        b = sb("sm2", n, F32, p2["o"])
        p2["o"] += n * 4
        assert p2["o"] <= 20480 and not (2560 < p2["o"] <= 16384 and p2["o"] > 4096), p2["o"]
        return b

    def sv():
        return V(small(), 0, NDG)
    NJ = 256
    hTu = sb("hTu", KC * 1024, BF16, 53248)
    Xu = sb("Xu", 32 * 128, BF16, 69632)
    Uu = sb("Uu", 32 * 256, BF16, 77824)
    Uu2 = sb("Uu2", 32 * 256, BF16, 36864)
    Ubufs = [Uu, Uu2]
    xs = sb("xs", 2 * D, F32, 94208)
    xnb = sb("xnb", 2 * D, BF16, 102400)
    junk = None
    wA = sb("wA", KC * 512, BF16, 106496)
    Vq = sb("Vq", 2048, F32, 192512)
    tq = sb("tq", 1024, F32, 114688)
    ucnt = {"i": 0}
    st = c.st
    wA3 = wA.ap(0, KC * 512).rearrange("p (k n) -> p k n", k=KC)
    dma("pool", V(wA, 0, KC * 512, "p (k n) -> p k n", k=KC), din["w_in"].rearrange("(k p) n -> p k n", p=128)[:, :, 0:512])
    front = make_front(c, xs, xnb, junk, psg="a")

    def ua_steps(hsrc, hoff, hlen, nk, Udst, uoff, ulen, bias=False):
        hv = hsrc.ap(0, KC * hlen).rearrange("p (k t) -> p k t", k=KC)
        x4 = Xu.ap(0, 32 * 128).rearrange("p (g i x) -> p g i x", g=32, i=8)

        def step_a():
            for i in range(8):
                pb = c.ps_next("a")

                def mm(e, pb=pb, i=i):
                    ins = None
                    for kc in range(KC):
                        ins = e.matmul(pb.ap(0, 512, 0, nk), lhsT=hv[:, kc, hoff + i:hoff + 8 * nk:8], rhs=wA3[:, kc, :],
                                       start=(kc == 0), stop=(kc == KC - 1 and not bias))
                    if bias:
                        ins = e.matmul(pb.ap(0, 512, 0, nk), lhsT=onesb.ap(0, nk, 0, 1), rhs=brow.ap(0, 512, 0, 1), start=False, stop=True)
                    return ins
                S.op("pe", mm, reads=[hsrc.reg(0, KC * hlen), wA.reg(0, KC * 512), onesb.reg(0, 128), brow.reg(0, 512)], writes=[pb.reg(0, 512)])
                cp("act", (x4[0:nk, :, i, :], Xu.reg(0, 32 * 128)), (pb.ap(0, 512, 0, nk).rearrange("p (g x) -> p g x", g=32), pb.reg(0, 512)))

        def step_b():
            u3 = Udst.ap(0, 32 * ulen).rearrange("p (g k) -> p g k", g=32)
            for g0 in range(0, 32, 8):
                pb = c.ps_next("a")
                pbb = pb.t[:, :].bitcast(BF16)

                def trU(e, pbb=pbb, g0=g0):
                    ins = None
                    for q in range(8):
                        ins = e.transpose(out=pbb[:, q * 128:q * 128 + nk], in_=Xu.ap((g0 + q) * 128, (g0 + q + 1) * 128, 0, nk),
                                          identity=c.identb.ap(0, nk, 0, nk))
                    return ins
                S.op("pe", trU, reads=[Xu.reg(0, 32 * 128), c.identb.reg(0, 128)], writes=[pb.reg(0, 512)])
                cp("act", (u3[:, g0:g0 + 8, uoff:uoff + nk], Udst.reg(0, 32 * ulen)),
                   (pbb[:, 0:1024].rearrange("p (q k) -> p q k", q=8)[:, :, 0:nk], pb.reg(0, 512)))
        return [step_a, step_b]

    ep5 = EP.ap(0, 128 * 128).rearrange("p (d g r h x) -> p d g r h x", d=2, g=G2, r=2, h=2)
    tb4 = TB.ap(0, 2 * NDG * NJ).rearrange("p (c n j) -> p c n j", c=2, n=NDG)

    def e_unit(Usrc, uoff, ulen, nk, d, qq, j0, rev, Vdst, tmp):
        u3 = Usrc.ap(0, 32 * ulen).rearrange("p (g k) -> p g k", g=32)
        pbs = [c.ps_next("b"), c.ps_next("b")]

        def mm(e):
            ins = None
            for ri in range(2):
                for gq in range(4):
                    g2 = qq * 4 + gq
                    for gh in range(2):
                        ins = e.matmul(pbs[ri].ap(gq * 128, gq * 128 + nk), lhsT=ep5[:, d, g2, ri, gh, :],
                                       rhs=u3[:, 2 * g2 + gh, uoff:uoff + nk], start=(gh == 0), stop=(gh == 1))
            return ins
        S.op("pe", mm, reads=[EP.reg(0, 128 * 128), Usrc.reg(0, 32 * ulen)], writes=[pbs[0].reg(0, 512), pbs[1].reg(0, 512)])
        n0 = d * G2 + qq * 4
        if not rev:
            tr = tb4[:, 0, n0:n0 + 4, j0:j0 + nk]
            ti = tb4[:, 1, n0:n0 + 4, j0:j0 + nk]
        else:
            lo = j0 - nk + 1
            tr = tb4[:, 0, n0:n0 + 4, lo:lo + nk][:, :, ::-1]
            ti = tb4[:, 1, n0:n0 + 4, lo:lo + nk][:, :, ::-1]
        TR, TI = (tr, TB.reg(0, 2 * NDG * NJ)), (ti, TB.reg(0, 2 * NDG * NJ))
        sre = (pbs[0].ap(0, 512).rearrange("p (g k) -> p g k", g=4)[:, :, 0:nk], pbs[0].reg(0, 512))
        sim = (pbs[1].ap(0, 512).rearrange("p (g k) -> p g k", g=4)[:, :, 0:nk], pbs[1].reg(0, 512))
        vb, lo_ = Vdst
        vre = (vb.ap(lo_, lo_ + 512).rearrange("p (g k) -> p g k", g=4)[:, :, 0:nk], vb.reg(lo_, lo_ + 512))
        vim = (vb.ap(lo_ + 512, lo_ + 1024).rearrange("p (g k) -> p g k", g=4)[:, :, 0:nk], vb.reg(lo_ + 512, lo_ + 1024))
        ta = (tmp.ap(0, 512).rearrange("p (g k) -> p g k", g=4)[:, :, 0:nk], tmp.reg(0, 512))
        tb_ = (tmp.ap(512, 1024).rearrange("p (g k) -> p g k", g=4)[:, :, 0:nk], tmp.reg(512, 1024))
        tt("dve", vre, sre, TR, ALU.mult)
        tt("dve", ta, sim, TI, ALU.mult)
        tt("dve", vim, sim, TR, ALU.mult)
        tt("dve", tb_, sre, TI, ALU.mult)
        tt("dve", vre, vre, ta, ALU.subtract)
        tt("dve", vim, vim, tb_, ALU.add)

    def scan_unit(Vsrc, nk, d, qq, rev, inits, Wdst, lasts=None):
        vb, vlo = Vsrc
        wb, wlo = Wdst
        v3 = vb.ap(vlo, vlo + 1024).rearrange("p (n k) -> p n k", n=8)
        w3 = wb.ap(wlo, wlo + 1024).rearrange("p (n k) -> p n k", n=8)
        for ri in range(2):
            for gq in range(4):
                n_ = d * G2 + qq * 4 + gq
                coef = (rho8[0][:, n_:n_ + 1].to_broadcast([128, nk]), rho8[1])
                n = ri * 4 + gq
                dv = v3[:, n, 0:nk]
                ov = w3[:, n, 0:nk]
                if rev:
                    dv, ov = dv[:, ::-1], ov[:, ::-1]
                ini = inits[n]
                rd = [vb.reg(vlo + ri * 512, vlo + (ri + 1) * 512), coef[1]] + ([ini[1]] if isinstance(ini, tuple) else [])
                S.op("dve", lambda e, ov=ov, dv=dv, coef=coef, ini=ini: e.tensor_tensor_scan(
                    out=ov, data0=coef[0], data1=dv, initial=(ini[0] if isinstance(ini, tuple) else ini), op0=ALU.mult, op1=ALU.add),
                    reads=rd, writes=[wb.reg(wlo + n * 128, wlo + (n + 1) * 128)])

    def newt(n):
        return small(n)
    Eend = {}
    units = [("ctx", ctx_rows, 0, 256, mod["cgs1"], mod["csh1"])]
    for sl in range(3):
        units.append((sl, xb_rows, sl * 2048, 2048, mod["gs1"], mod["sh1"]))
    units.append(("own", own_x, 0, 2048, mod["gs1"], mod["sh1"]))

    def fu_steps(ui):
        (name, src, row0, ntok, gsc, shc) = units[ui]
        NK = ntok // 8
        nkt = max(1, NK // 128)
        nk = min(NK, 128)
        steps = []
        plain = (name != "ctx")
        for kt in range(nkt):
            steps += front.steps(src, row0 + kt * 1024, min(ntok, 1024), gsc, shc, hTu, 0, 1024, plain=plain)
            steps += ua_steps(hTu, 0, 1024, nk, Ubufs[ui % 2], kt * 128, 256, bias=plain)
        return steps
    onesb = sb("onesb", 128, BF16, 2560)
    brow = sb("brow", 512, BF16, 2816)
    shb = sb("shb", KC, BF16, 3840)
    ms("pool", (onesb.ap(0, 128, 0, 1), onesb.reg(0, 128)), 1.0)
    cp("act", V(shb, 0, KC), V(mod["sh1"], 0, KC))
    pbq = c.ps_next()

    def mm_b(e):
        ins = None
        for kc in range(KC):
            ins = e.matmul(pbq.ap(0, 512, 0, 1), lhsT=shb.ap(kc, kc + 1), rhs=wA3[:, kc, :], start=(kc == 0), stop=(kc == KC - 1))
        return ins
    S.op("pe", mm_b, reads=[shb.reg(0, KC), wA.reg(0, KC * 512)], writes=[pbq.reg(0, 512)])
    cp("act", (brow.ap(0, 512, 0, 1), brow.reg(0, 512)), (pbq.ap(0, 512, 0, 1), pbq.reg(0, 512)))
    for st_ in fu_steps(0):
        st_()
    for kc in range(KC):
        ts("dve", (wA3[:, kc, :], wA.reg(kc * 512, (kc + 1) * 512)), (wA3[:, kc, :], wA.reg(kc * 512, (kc + 1) * 512)), V(mod["gs1"], kc, kc + 1), ALU.mult)
    for ui in range(4):
        (name, src, row0, ntok, gsc, shc) = units[ui]
        Ucur = Ubufs[ui % 2]
        pending = fu_steps(ui + 1)
        NK = ntok // 8
        nkt = max(1, NK // 128)
        nk = min(NK, 128)
        wl = {d: newt(32) for d in range(2)}
        Eend[name] = wl
        for d in range(2):
            kts = list(range(nkt)) if d == 0 else list(range(nkt - 1, -1, -1))
            for qq in range(4):
                prevV = None
                for ki, kt in enumerate(kts):
                    j0 = kt * 128 if d == 0 else (NK - 1 - kt * 128)
                    vsl = (ucnt["i"] % 2) * 1024
                    ucnt["i"] += 1
                    e_unit(Ucur, kt * 128, 256, nk, d, qq, j0, d == 1, (Vq, vsl), tq)
                    pos = 0 if d == 1 else nk - 1
                    if prevV is None:
                        inits = [0.0] * 8
                    else:
                        pw3 = Vq.ap(prevV, prevV + 1024).rearrange("p (n k) -> p n k", n=8)
                        inits = [(pw3[:, n, pos:pos + 1], Vq.reg(prevV + n * 128, prevV + (n + 1) * 128)) for n in range(8)]
                    scan_unit((Vq, vsl), nk, d, qq, d == 1, inits, (Vq, vsl))
                    prevV = vsl
                    if pending:
                        pending.pop(0)()
                pw3 = Vq.ap(prevV, prevV + 1024).rearrange("p (r g k) -> p r g k", r=2, g=4)
                w3_ = wl[d].ap(qq * 8, (qq + 1) * 8).rearrange("p (g r) -> p g r", g=4)
                for ri in range(2):
                    cp("act", (w3_[:, :, ri:ri + 1], wl[d].reg(qq * 8, (qq + 1) * 8)),
                       (pw3[:, ri, :, pos:pos + 1], Vq.reg(prevV, prevV + 1024)))
        while pending:
            pending.pop(0)()
    Uown = Ubufs[4 % 2]
    t1, t2 = s5["t12"]
    upows = s5["upows"]

    def half(x, d):
        return (x[0][:, d * G2:(d + 1) * G2], x[1])

    def conj_mul(outr, outi, ar, ai, br, bi, ta, tb):
        tt("dve", ta, ar, br, ALU.mult)
        tt("dve", tb, ai, bi, ALU.mult)
        tt("dve", outr, ta, tb, ALU.add)
        tt("dve", ta, ar, bi, ALU.mult)
        tt("dve", tb, ai, br, ALU.mult)
        tt("dve", outi, ta, tb, ALU.subtract)
    u255r, u255i, u31r, u31i = sv(), sv(), sv(), sv()
    conj_mul(u255r, u255i, upows[0][0], upows[0][1], upows[8][0], upows[8][1], t1, t2)
    conj_mul(u31r, u31i, upows[0][0], upows[0][1], upows[5][0], upows[5][1], t1, t2)
    r256, Ar, Ai = sv(), sv(), sv()
    cp("dve", r256, rho8)
    for _ in range(8):
        tt("dve", r256, r256, r256, ALU.mult)
    tt("dve", Ar, upows[8][0], r256, ALU.mult)
    stt("dve", Ai, upows[8][1], -1.0, r256, ALU.mult, ALU.mult)
    oh = small(4)
    dma("sp", V(oh, 0, 4), din["onehot"][:, :])
    carry = {}
    h1, h2 = small(16), small(16)
    H1, H2 = V(h1, 0, 16), V(h2, 0, 16)
    sbuf_ = [(V(small(16), 0, 16), V(small(16), 0, 16)) for _ in range(2)]
    nbuf_ = [(V(small(16), 0, 16), V(small(16), 0, 16)) for _ in range(2)]
    cr_, ci_ = small(16), small(16)
    CR, CI = V(cr_, 0, 16), V(ci_, 0, 16)
    for d in range(2):
        def send(name, slot):
            w3 = Eend[name][d].ap(0, 32).rearrange("p (g r) -> p g r", g=G2)
            wr_, wi_ = (w3[:, :, 0], Eend[name][d].reg(0, 32)), (w3[:, :, 1], Eend[name][d].reg(0, 32))
            ur_, ui_ = (u31r, u31i) if name == "ctx" else (u255r, u255i)
            conj_mul(sbuf_[slot][0], sbuf_[slot][1], half(ur_, d), half(ui_, d), wr_, wi_, H1, H2)
            return sbuf_[slot]
        cur = send("ctx", 0)
        ms("dve", CR, 0.0)
        ms("dve", CI, 0.0)
        order = [0, 1, 2, 3] if d == 0 else [3, 2, 1, 0]
        for si_, pos_ in enumerate(order):
            stt("dve", CR, cur[0], V(oh, pos_, pos_ + 1), CR, ALU.mult, ALU.add)
            stt("dve", CI, cur[1], V(oh, pos_, pos_ + 1), CI, ALU.mult, ALU.add)
            if si_ == 3:
                break
            slot = pos_ if d == 0 else pos_ - 1
            e_ = send(slot, 1)
            NR, NI = nbuf_[si_ % 2]
            tt("dve", H1, half(Ar, d), cur[0], ALU.mult)
            tt("dve", H2, half(Ai, d), cur[1], ALU.mult)
            tt("dve", NR, H1, H2, ALU.subtract)
            tt("dve", NR, NR, e_[0], ALU.add)
            tt("dve", H1, half(Ar, d), cur[1], ALU.mult)
            tt("dve", H2, half(Ai, d), cur[0], ALU.mult)
            tt("dve", NI, H1, H2, ALU.add)
            tt("dve", NI, NI, e_[1], ALU.add)
            cur = (NR, NI)
        cpk, w0 = small(32), small(32)
        c3 = cpk.ap(0, 32).rearrange("p (g r) -> p g r", g=G2)
        w03 = w0.ap(0, 32).rearrange("p (g r) -> p g r", g=G2)
        cp("dve", (c3[:, :, 0], cpk.reg(0, 32)), CR)
        cp("dve", (c3[:, :, 1], cpk.reg(0, 32)), CI)
        conj_mul((w03[:, :, 0], w0.reg(0, 32)), (w03[:, :, 1], w0.reg(0, 32)), half(upows[0][0], d), half(upows[0][1], d), CR, CI, H1, H2)
        carry[d] = (cpk, w0)
        c.dump("carry_%d" % d, cpk, 0, 32)

    SIN = {0: sb("SINf", 8 * 1024, BF16, 53248), 1: sb("SINb", 8 * 1024, BF16, 94208)}
    Vo2 = sb("Vo2", 2 * 1024, F32, 69632)
    Uu = Uown
    c.dump("Uown", Uu, 0, 32 * 256)
    NK = 256
    for d in range(2):
        kts = [0, 1] if d == 0 else [1, 0]
        cpk, w0 = carry[d]
        for qq in range(4):
            for kt in kts:
                j0 = kt * 128 if d == 0 else (NK - 1 - kt * 128)
                e_unit(Uu, kt * 128, 256, 128, d, qq, j0, d == 1, (Vo2, kt * 1024), tq)
            lastw = small(8)
            for ki, kt in enumerate(kts):
                j0 = kt * 128 if d == 0 else (NK - 1 - kt * 128)
                if ki == 0:
                    w03 = w0.ap(0, 32).rearrange("p (g r) -> p g r", g=G2)
                    inits = [(w03[:, qq * 4 + n % 4, n // 4:n // 4 + 1], w0.reg(0, 32)) for n in range(8)]
                else:
                    inits = [V(lastw, n, n + 1) for n in range(8)]
                scan_unit((Vo2, kt * 1024), 128, d, qq, d == 1, inits, (Vo2, kt * 1024))
                if ki == 0:
                    pos = 0 if d == 1 else 127
                    cp("act", (lastw.ap(0, 8).rearrange("p (n o) -> p n o", o=1), lastw.reg(0, 8)),
                       (Vo2.ap(kt * 1024, (kt + 1) * 1024).rearrange("p (n k) -> p n k", n=8)[:, :, pos:pos + 1], Vo2.reg(kt * 1024, (kt + 1) * 1024)))
                n0 = d * G2 + qq * 4
                if d == 0:
                    tr = tb4[:, 0, n0:n0 + 4, j0:j0 + 128]
                    ti = tb4[:, 1, n0:n0 + 4, j0:j0 + 128]
                else:
                    lo = j0 - 127
                    tr = tb4[:, 0, n0:n0 + 4, lo:lo + 128][:, :, ::-1]
                    ti = tb4[:, 1, n0:n0 + 4, lo:lo + 128][:, :, ::-1]
                TR, TI = (tr, TB.reg(0, 2 * NDG * NJ)), (ti, TB.reg(0, 2 * NDG * NJ))
                wre = (Vo2.ap(kt * 1024, kt * 1024 + 512).rearrange("p (g k) -> p g k", g=4), Vo2.reg(kt * 1024, kt * 1024 + 512))
                wim = (Vo2.ap(kt * 1024 + 512, (kt + 1) * 1024).rearrange("p (g k) -> p g k", g=4), Vo2.reg(kt * 1024 + 512, (kt + 1) * 1024))
                ore = (Vq.ap(0, 512).rearrange("p (g k) -> p g k", g=4), Vq.reg(0, 512))
                oim = (Vq.ap(512, 1024).rearrange("p (g k) -> p g k", g=4), Vq.reg(512, 1024))
                ta = (tq.ap(0, 512).rearrange("p (g k) -> p g k", g=4), tq.reg(0, 512))
                tb_ = (tq.ap(512, 1024).rearrange("p (g k) -> p g k", g=4), tq.reg(512, 1024))
                tt("dve", ore, wre, TR, ALU.mult)
                tt("dve", ta, wim, TI, ALU.mult)
                tt("dve", oim, wim, TR, ALU.mult)
                tt("dve", tb_, wre, TI, ALU.mult)
                tt("dve", ore, ore, ta, ALU.add)
                tt("dve", oim, oim, tb_, ALU.subtract)
                sin = SIN[d]
                base = (kt * 4 + qq) * 1024
                s4 = sin.ap(base, base + 1024).rearrange("p (g r k) -> p g r k", g=4, r=2)
                sreg = sin.reg(base, base + 1024)
                c3 = cpk.ap(0, 32).rearrange("p (g r) -> p g r", g=G2)
                okt = 1 - kt
                nb_ = (okt * 4 + qq) * 1024
                nx4 = sin.ap(nb_, nb_ + 1024).rearrange("p (g r k) -> p g r k", g=4, r=2)
                for ri in range(2):
                    v3 = (Vq.ap(ri * 512, (ri + 1) * 512).rearrange("p (g k) -> p g k", g=4), Vq.reg(ri * 512, (ri + 1) * 512))
                    if d == 0:
                        cp("act", (s4[:, :, ri, 1:128], sreg), (v3[0][:, :, 0:127], v3[1]))
                        if kt == 0:
                            cp("act", (s4[:, :, ri, 0:1], sreg), (c3[:, qq * 4:(qq + 1) * 4, ri:ri + 1], cpk.reg(0, 32)))
                            cp("act", (nx4[:, :, ri, 0:1], sin.reg(nb_, nb_ + 1024)), (v3[0][:, :, 127:128], v3[1]))
                    else:
                        cp("act", (s4[:, :, ri, 0:127], sreg), (v3[0][:, :, 1:128], v3[1]))
                        if kt == 1:
                            cp("act", (s4[:, :, ri, 127:128], sreg), (c3[:, qq * 4:(qq + 1) * 4, ri:ri + 1], cpk.reg(0, 32)))
                            cp("act", (nx4[:, :, ri, 127:128], sin.reg(nb_, nb_ + 1024)), (v3[0][:, :, 0:1], v3[1]))
    c.dump("SINf", SIN[0], 0, 8 * 1024)
    c.dump("SINb", SIN[1], 0, 8 * 1024)

    FP = sb("FP", 128 * 128, BF16, 151552)
    fp4 = FP.ap(0, 128 * 128).rearrange("p (n h x) -> p n h x", h=2, x=128)
    ftb3 = FTb.ap(0, 8192).rearrange("p (n x) -> p n x", x=128)
    hm = s5["hm"]
    for gh in range(2):
        ts("dve", (fp4[:, :, gh, :], FP.reg(0, 128 * 128)), (ftb3, FTb.reg(0, 8192)), V(hm, gh, gh + 1), ALU.mult)
    Ytok = sb("Ytok", 2 * 8 * 512, BF16, 118784)
    yT = sb("yT", 4 * NTOK, BF16, 135168)
    u3 = Uu.ap(0, 32 * 256).rearrange("p (g k) -> p g k", g=32)
    fp3 = FP.ap(0, 128 * 128).rearrange("p (n x) -> p n x", x=128)
    for kt in range(2):
        for g0 in range(0, 32, 4):
            pb = c.ps_next()

            def mmY(e, pb=pb, g0=g0, kt=kt):
                ins = None
                for gq in range(4):
                    g = g0 + gq
                    g2, gh = g // 2, g % 2
                    qq, gl = g2 // 4, g2 % 4
                    o_ = pb.ap(gq * 128, (gq + 1) * 128)
                    ins = e.matmul(o_, lhsT=u3[:, g, kt * 128:(kt + 1) * 128], rhs=MT.ap(g * 128, (g + 1) * 128), start=True, stop=False)
                    for d in range(2):
                        for ri in range(2):
                            lo = (kt * 4 + qq) * 1024 + (gl * 2 + ri) * 128
                            n = ((d * G2 + g2) * 2 + ri) * 2 + gh
                            ins = e.matmul(o_, lhsT=SIN[d].ap(lo, lo + 128), rhs=fp3[:, n, :], start=False, stop=(d == 1 and ri == 1))
                return ins
            S.op("pe", mmY, reads=[Uu.reg(0, 32 * 256), MT.reg(0, 32 * 128), SIN[0].reg(0, 8192), SIN[1].reg(0, 8192), FP.reg(0, 128 * 128)],
                 writes=[pb.reg(0, 512)])
            y4 = Ytok.ap(kt * 4096, (kt + 1) * 4096).rearrange("p (i g x) -> p i g x", i=8, g=32)
            for gq in range(4):
                act((y4[:, :, g0 + gq, :], Ytok.reg(kt * 4096, (kt + 1) * 4096)),
                    (pb.ap(gq * 128, (gq + 1) * 128).rearrange("p (i x) -> p i x", i=8), pb.reg(0, 512)), AF.Gelu_apprx_tanh)
    yT3 = yT.ap(0, 4 * NTOK).rearrange("p (q t) -> p q t", q=4)
    for kt in range(2):
        for q4 in range(4):
            pb = c.ps_next()
            pbb = pb.t[:, :].bitcast(BF16)

            def trY(e, pbb=pbb, kt=kt, q4=q4):
                ins = None
                for i in range(8):
                    lo = kt * 4096 + i * 512 + q4 * 128
                    ins = e.transpose(out=pbb[:, i * 128:(i + 1) * 128], in_=Ytok.ap(lo, lo + 128), identity=c.identb.ap(0, 128))
                return ins
            S.op("pe", trY, reads=[Ytok.reg(0, 2 * 4096), c.identb.reg(0, 128)], writes=[pb.reg(0, 512)])
            cp("dve", (yT3[:, q4, kt * 1024:(kt + 1) * 1024].rearrange("p (k i) -> p k i", i=8), yT.reg(q4 * NTOK + kt * 1024, q4 * NTOK + (kt + 1) * 1024)),
               (pbb[:, 0:1024].rearrange("p (i k) -> p k i", i=8), pb.reg(0, 512)))
    c.dump("yT", yT, 0, 4 * NTOK)
    return yT


def build_nc():
    nc = bass.Bass("TRN2", target_bir_lowering=False)

    def din_(name, shape):
        return nc.dram_tensor(name, list(shape), F32, kind="ExternalInput").ap()
    din = {}
    for name, shape in INPUT_SHAPES.items():
        din[name] = din_(name, shape)
    out = nc.dram_tensor("out", [NTOK, D], F32, kind="ExternalOutput").ap()
    S = Sched()
    with ExitStack() as es:
        c = make_ctx(nc, es, S, DEBUG, DBG_WORDS)
        c.din = din
        sb = c.sb
        c.identf = sb("identf", 128, F32, 0)
        c.onesf = sb("onesf", 128, F32, 512)
        c.identb = sb("identb", 128, BF16, 1024)
        c.dma("sp", V(c.identf, 0, 128), din["ident"][:, :])
        c.dma("pool", V(c.identb, 0, 128), din["ident"][:, :])
        c.ms("dve", V(c.onesf, 0, 128), 1.0)
        c.st = sb("st", 128, F32, 2048)
        epsb = sb("epsb", 1, F32, 1792)
        c.ms("pool", V(epsb, 0, 1), EPS)
        c.epsc = V(epsb, 0, 1)
        if STAGE in ("full", "nos5", "none"):
            x1d = nc.dram_tensor("x1d", [NTOK, D], F32, kind="Internal").ap()
            yT = None
            mod = mod_phase(c)
            if STAGE == "full":
                s5 = s5_precompute(c, din)
            mod["finish"]()
            if STAGE == "full":
                yT = s5_run(c, s5, mod, din["xb"], din["ctx"], None, din["x"])
            if STAGE == "none":
                x1d = din["x"]
            else:
                backend(c, mod, yT, din["x"], x1d, use_s5=(STAGE == "full"))
            ffn_phase(c, mod, x1d, out)
        if STAGE in ("s5pre", "s5ana"):
            if STAGE == "s5ana":
                mod = mod_phase(c)
            s5 = s5_precompute(c, din)
            if STAGE == "s5ana":
                mod["finish"]()
            if STAGE == "s5ana":
                s5_run(c, s5, mod, din["xb"], din["ctx"], None, din["x"])
            yo = sb("yo", D, F32, 4096)
            c.ms("dve", V(yo, 0, D), 0.0)
            for tt_ in range(NT):
                S.op("sp", lambda e, tt_=tt_: e.dma_start(out=out[tt_ * 128:(tt_ + 1) * 128, :], in_=yo.ap(0, D)),
                     reads=[yo.reg(0, D)], writes=[R("dram_out", tt_ * 128, (tt_ + 1) * 128)], dma=True, key=("st", 0))
        S.op("sp", None, reads=[R("dram_out", 0, NTOK), R("dram_dbg", 0, DBG_WORDS)])
        S.analyse()
        sems = {e: es.enter_context(nc.semaphore("sem_" + e)) for e in ("pe", "act", "dve", "pool", "sp")}
        dsems = {k: es.enter_context(nc.semaphore("dsem%d" % i)) for i, k in enumerate(S.dma_keys)}
        print("ops", len(S.ops), "dma sems", len(dsems))
        with nc.Block() as block:
            @block.sync
            def _(e):
                S.emit("sp", e, sems, dsems)

            @block.scalar
            def _(e):
                S.emit("act", e, sems, dsems)

            @block.vector
            def _(e):
                S.emit("dve", e, sems, dsems)

            @block.gpsimd
            def _(e):
                S.emit("pool", e, sems, dsems)

            @block.tensor
            def _(e):
                S.emit("pe", e, sems, dsems)
    return nc


INPUT_SHAPES = {
    "x": [NTOK, D], "ident": [128, 128],
    "s5sm": [128, 96], "s5B": [128, 1024], "s5C": [128, 1024], "hm": [128, 2], "dcol": [128, 32],
    "maskf": [128, 128], "maskb": [128, 128],
    "xb": [3 * NTOK, D], "ctx": [256, D], "cc": [128, KC * 64], "w_mod": [D, 6 * D], "b_mod": [1, 6 * D],
    "onehot": [128, 4], "n1col": [128, KC], "n2col": [128, KC], "final_norm_g": [1, D], "w_in": [D, 3 * D],
    "w_branch_a": [512, D], "w_branch_b": [512, D], "w_glu": [512, 512], "pool_w": [4, 128, 128], "poolB": [4, 128, 128],
    "w_out": [D, D], "invc": [1, 512], "sc2": [128, 8], "w_ffn_in": [D, 2 * FH], "w_ffn_out": [FH, D],
}


def host_layout(inputs):
    f = lambda a: np.ascontiguousarray(np.asarray(a), dtype=np.float32)
    cm = {}
    cm["ident"] = np.eye(128, dtype=np.float32)

    def pm(a):
        a = f(a)
        sh = a.shape
        a = a.reshape(2, 16, 2, 64, *sh[3:])
        a = np.moveaxis(a, (2, 3), (0, 1))
        return np.ascontiguousarray(a.reshape(128, -1))
    are = pm(inputs["s5_a_re"][0])
    aim = pm(inputs["s5_a_im"][0])
    ldt = pm(np.broadcast_to(f(inputs["s5_log_dt"][0])[:, :, None], (2, 32, 64)))
    cm["s5sm"] = np.concatenate([are, aim, ldt], axis=1)
    cm["s5B"] = np.concatenate([pm(inputs["s5_b_re"][0]), pm(inputs["s5_b_im"][0])], axis=1)
    cre = np.swapaxes(f(inputs["s5_c_re"][0]), 2, 3)
    cim = np.swapaxes(f(inputs["s5_c_im"][0]), 2, 3)
    cm["s5C"] = np.concatenate([pm(cre), pm(cim)], axis=1)
    hm = np.zeros((128, 2), np.float32)
    hm[:64, 0] = 1.0
    hm[64:, 1] = 1.0
    cm["hm"] = hm
    dsk = f(inputs["s5_d"][0]).reshape(32, 16)
    cm["dcol"] = np.ascontiguousarray(np.tile(dsk.T[None, :, :], (8, 1, 1)).reshape(128, 32))
    ii = np.arange(128) // 16
    cm["maskf"] = (ii[:, None] <= ii[None, :]).astype(np.float32)
    cm["maskb"] = (ii[:, None] >= ii[None, :]).astype(np.float32)
    cm["w_mod"] = f(inputs["w_mod"][0])
    cm["b_mod"] = f(inputs["b_mod"][0]).reshape(1, 6 * D)
    cm["n1col"] = f(f(inputs["norm1_g"][0]).reshape(KC, 128).T)
    cm["n2col"] = f(f(inputs["norm2_g"][0]).reshape(KC, 128).T)
    cm["final_norm_g"] = f(inputs["final_norm_g"]).reshape(1, D)
    cm["w_in"] = f(inputs["w_in"][0])
    for k in ("w_branch_a", "w_branch_b", "w_glu", "pool_w", "w_out", "w_ffn_in", "w_ffn_out"):
        cm[k] = f(inputs[k][0])
    pos = np.arange(64)
    PB = np.zeros((4, 128, 128), np.float32)
    invc = np.zeros((1, 4, 128), np.float32)
    for jw, w in enumerate((2, 4, 8, 16)):
        lo = np.clip(pos - w // 2, 0, 63)
        hi = np.clip(pos + w - 1 - w // 2, 0, 63) + 1
        blk = ((pos[:, None] >= lo[None, :]) & (pos[:, None] < hi[None, :])).astype(np.float32)
        cnt = (hi - lo).astype(np.float32)
        blk = blk - np.diag(cnt)
        PB[jw, :64, :64] = blk
        PB[jw, 64:, 64:] = blk
        invc[0, jw, :64] = 1.0 / cnt
        invc[0, jw, 64:] = 1.0 / cnt
    cm["poolB"] = PB
    cm["invc"] = invc.reshape(1, 512)
    sc2 = np.zeros((128, 8), np.float32)
    sc2[:, 0:4] = f(inputs["pool_scale"][0]).reshape(4, 128).T
    sc2[:, 4:8] = f(inputs["b_glu"][0]).reshape(4, 128).T
    cm["sc2"] = sc2
    return cm


def kernel(**inputs):
    f = lambda a: np.ascontiguousarray(np.asarray(a), dtype=np.float32)
    x = f(inputs["x"])
    nc = build_nc()
    common = host_layout(inputs)
    in_maps = []
    for cid in range(8):
        b, j = cid // 4, cid % 4
        m = {k: v for k, v in common.items() if k in INPUT_SHAPES}
        m["x"] = np.ascontiguousarray(x[b, j * NTOK:(j + 1) * NTOK, :])
        m["xb"] = np.ascontiguousarray(np.concatenate([x[b, i * NTOK:(i + 1) * NTOK] for i in range(4) if i != j], axis=0))
        m["ctx"] = f(inputs["ctx"][b])
        cc = np.zeros((128, KC, 64), np.float32)
        cc[:, :, 0] = f(inputs["c"])[b].reshape(KC, 128).T
        cc[:, :, 32] = f(inputs["c_ctx"]).reshape(KC, 128).T
        m["cc"] = cc.reshape(128, KC * 64)
        oh = np.zeros((128, 4), np.float32)
        oh[:, j] = 1.0
        m["onehot"] = oh
        in_maps.append(m)
    res = run_bass_kernel_spmd(nc, in_maps, core_ids=list(range(8)))
    if DEBUG:
        global DBG_OUT
        DBG_OUT = [res.results[cid]["dbg"] for cid in range(8)]
    outp = np.empty((2, 8192, D), np.float32)
    for cid in range(8):
        b, j = cid // 4, cid % 4
        outp[b, j * NTOK:(j + 1) * NTOK, :] = res.results[cid]["out"]
    return outp
```
